# Optimizing a Trainium2 kernel written in Bass

```python
import math
import jax, jax.numpy as jnp
from jax import lax
import numpy as np

D_MODEL = 1024
BATCH = 16
SEQ = 2048
DEPTH = 4

GRID_W = 64
CTX_LEN = 256
HEAD_DIM = 64
Q_BLOCK = 128
WINDOW = 128
ROPE_THETA = 10000.0
EPS = 1e-6
NEG_INF = -1e30

A_Q_HEADS = 8
A_KV_HEADS = 2
A_GROUP = A_Q_HEADS // A_KV_HEADS
B_HEADS = 8
B_Q_RANK = 256
B_KV_RANK = 128
B_NOPE = 64
B_ROPE = 32
B_V = 64
C_Q_HEADS = 8
C_KV_HEADS = 2
C_GROUP = C_Q_HEADS // C_KV_HEADS
D_HEADS = 4
D_V = 2 * HEAD_DIM

EVEN_SIZES = (A_Q_HEADS * HEAD_DIM, A_KV_HEADS * HEAD_DIM, A_KV_HEADS * HEAD_DIM, B_Q_RANK, B_KV_RANK, B_ROPE)
ODD_SIZES = (C_Q_HEADS * HEAD_DIM, C_KV_HEADS * HEAD_DIM, C_KV_HEADS * HEAD_DIM,
             D_HEADS * 2 * HEAD_DIM, D_HEADS * 2 * HEAD_DIM, D_HEADS * D_V)
EVEN_IN = sum(EVEN_SIZES)
ODD_IN = sum(ODD_SIZES)
MIX_WIDTH = A_Q_HEADS * HEAD_DIM + B_HEADS * B_V
FFN_HIDDEN = -(-8 * D_MODEL // (3 * 256)) * 256
N_EVEN = (DEPTH + 1) // 2
N_ODD = DEPTH // 2

kernel_name = "hybrid_diffusion_prefix_trunk"


def rms_norm(x, g):
    xf = x.astype(jnp.float32)
    y = xf * lax.rsqrt(jnp.mean(xf * xf, axis=-1, keepdims=True) + EPS)
    return (y * g.astype(jnp.float32)).astype(x.dtype)


def split_cols(p, sizes):
    return jnp.split(p, [int(v) for v in np.cumsum(sizes)[:-1]], axis=-1)


def rope_1d(x, pos):
    half = x.shape[-1] // 2
    freqs = ROPE_THETA ** (-jnp.arange(half, dtype=jnp.float32) / half)
    ang = pos.astype(jnp.float32)[:, None] * freqs[None, :]
    ang = ang.reshape((ang.shape[0],) + (1,) * (x.ndim - 3) + (half,))
    cos, sin = jnp.cos(ang), jnp.sin(ang)
    x1 = x[..., :half].astype(jnp.float32)
    x2 = x[..., half:].astype(jnp.float32)
    return jnp.concatenate([x1 * cos - x2 * sin, x2 * cos + x1 * sin], axis=-1).astype(x.dtype)


def rope_2d(x, rows, cols):
    h = x.shape[-1] // 2
    return jnp.concatenate([rope_1d(x[..., :h], rows), rope_1d(x[..., h:], cols)], axis=-1)


def sweep_query_blocks(fn, q):
    B, S = q.shape[:2]
    nb = S // Q_BLOCK
    qb = jnp.moveaxis(q.reshape((B, nb, Q_BLOCK) + q.shape[2:]), 1, 0)
    ob = lax.map(fn, qb)
    return jnp.moveaxis(ob, 0, 1).reshape((B, S) + ob.shape[3:])


def gqa_attend(q, k, v, scale):
    s = jnp.einsum('bqhgd,bkhd->bhgqk', q, k, preferred_element_type=jnp.float32) * scale
    p = jax.nn.softmax(s, axis=-1)
    return jnp.einsum('bhgqk,bkhd->bqhgd', p.astype(v.dtype), v)


def sink_attend(q, k, v, sink):
    Hk, G, d = q.shape[2], q.shape[3], q.shape[4]
    s = jnp.einsum('bqhgd,bkhd->bhgqk', q, k, preferred_element_type=jnp.float32) * d ** -0.5
    sk = jnp.broadcast_to(sink.astype(jnp.float32).reshape(Hk, G, 1, 1), s.shape[:-1] + (1,))
    p = jax.nn.softmax(jnp.concatenate([s, sk], axis=-1), axis=-1)[..., :-1]
    return jnp.einsum('bhgqk,bkhd->bqhgd', p.astype(v.dtype), v)


def windowed_sink_attention(q, k, v, k_ctx, v_ctx, sink):
    B, S, Hk, G, d = q.shape
    nb = S // Q_BLOCK
    L = 3 * Q_BLOCK
    scale = d ** -0.5
    qb = q.reshape(B, nb, Q_BLOCK, Hk, G, d)

    def band(t):
        tb = t.reshape(B, nb, Q_BLOCK, Hk, t.shape[-1])
        tp = jnp.pad(tb, ((0, 0), (1, 1), (0, 0), (0, 0), (0, 0)))
        return jnp.concatenate([tp[:, :-2], tp[:, 1:-1], tp[:, 2:]], axis=2)

    kw, vw = band(k), band(v)
    blk = jnp.arange(nb)[:, None, None]
    q_pos = blk * Q_BLOCK + jnp.arange(Q_BLOCK)[None, :, None]
    k_pos = (blk - 1) * Q_BLOCK + jnp.arange(L)[None, None, :]
    valid = (jnp.abs(q_pos - k_pos) <= WINDOW) & (k_pos >= 0) & (k_pos < S)
    s_loc = jnp.einsum('bnqhgd,bnkhd->bnhgqk', qb, kw, preferred_element_type=jnp.float32) * scale
    s_loc = jnp.where(valid[None, :, None, None, :, :], s_loc, NEG_INF)
    s_ctx = jnp.einsum('bnqhgd,bchd->bnhgqc', qb, k_ctx, preferred_element_type=jnp.float32) * scale
    sk = jnp.broadcast_to(sink.astype(jnp.float32).reshape(Hk, G, 1, 1), s_loc.shape[:-1] + (1,))
    p = jax.nn.softmax(jnp.concatenate([s_loc, s_ctx, sk], axis=-1), axis=-1)
    C = k_ctx.shape[1]
    o = (jnp.einsum('bnhgqk,bnkhd->bnqhgd', p[..., :L].astype(v.dtype), vw)
         + jnp.einsum('bnhgqc,bchd->bnqhgd', p[..., L:L + C].astype(v.dtype), v_ctx))
    return o.reshape(B, S, Hk, G, v.shape[-1])


def diff_attend(q, k, v, lam):
    s = jnp.einsum('bqhid,bkhid->bhiqk', q, k, preferred_element_type=jnp.float32) * q.shape[-1] ** -0.5
    p = jax.nn.softmax(s, axis=-1)
    a = p[:, :, 0] - lam * p[:, :, 1]
    return jnp.einsum('bhqk,bkhd->bqhd', a.astype(v.dtype), v)


def even_project(t, rope, w_in, q_norm, w_uq, kv_norm, w_ukv):
    B, L, _ = t.shape
    qa, ka, va, cq, ckv, kpe = split_cols(t @ w_in, EVEN_SIZES)
    qa = rope(qa.reshape(B, L, A_KV_HEADS, A_GROUP, HEAD_DIM))
    ka = rope(ka.reshape(B, L, A_KV_HEADS, HEAD_DIM))
    va = va.reshape(B, L, A_KV_HEADS, HEAD_DIM)
    qb = (rms_norm(cq, q_norm) @ w_uq).reshape(B, L, B_HEADS, B_NOPE + B_ROPE)
    qb = jnp.concatenate([qb[..., :B_NOPE], rope(qb[..., B_NOPE:])], axis=-1)[:, :, :, None, :]
    kvb = (rms_norm(ckv, kv_norm) @ w_ukv).reshape(B, L, B_HEADS, B_NOPE + B_V)
    kpe = jnp.broadcast_to(rope(kpe[:, :, None, :]), (B, L, B_HEADS, B_ROPE))
    kb = jnp.concatenate([kvb[..., :B_NOPE], kpe], axis=-1)
    vb = kvb[..., B_NOPE:]
    return qa, ka, va, qb, kb, vb


def mixer_even(h, hc, rows, cols, w_in, sink, q_norm, w_uq, kv_norm, w_ukv, w_out, need_ctx):
    B, S = h.shape[:2]
    qa, ka, va, qb, kb, vb = even_project(h, lambda t: rope_2d(t, rows, cols), w_in, q_norm, w_uq, kv_norm, w_ukv)
    qa_c, ka_c, va_c, qb_c, kb_c, vb_c = even_project(hc, lambda t: t, w_in, q_norm, w_uq, kv_norm, w_ukv)
    scale_b = (B_NOPE + B_ROPE) ** -0.5
    oa = windowed_sink_attention(qa, ka, va, ka_c, va_c, sink)
    kb_all = jnp.concatenate([kb_c, kb], axis=1)
    vb_all = jnp.concatenate([vb_c, vb], axis=1)
    ob = sweep_query_blocks(lambda qq: gqa_attend(qq, kb_all, vb_all, scale_b), qb)
    y = jnp.concatenate([oa.reshape(B, S, -1), ob.reshape(B, S, -1)], axis=-1) @ w_out
    yc = None
    if need_ctx:
        C = hc.shape[1]
        oa_c = sink_attend(qa_c, ka_c, va_c, sink)
        ob_c = gqa_attend(qb_c, kb_c, vb_c, scale_b)
        yc = jnp.concatenate([oa_c.reshape(B, C, -1), ob_c.reshape(B, C, -1)], axis=-1) @ w_out
    return y, yc


def odd_project(t, rope, w_in, qk_norm):
    B, L, _ = t.shape
    qc, kc, vc, qd, kd, vd = split_cols(t @ w_in, ODD_SIZES)
    qc = rope(rms_norm(qc.reshape(B, L, C_KV_HEADS, C_GROUP, HEAD_DIM), qk_norm[0]))
    kc = rope(rms_norm(kc.reshape(B, L, C_KV_HEADS, HEAD_DIM), qk_norm[1]))
    vc = vc.reshape(B, L, C_KV_HEADS, HEAD_DIM)
    qd = rope(qd.reshape(B, L, D_HEADS, 2, HEAD_DIM))
    kd = rope(kd.reshape(B, L, D_HEADS, 2, HEAD_DIM))
    vd = vd.reshape(B, L, D_HEADS, D_V)
    return qc, kc, vc, qd, kd, vd


def mixer_odd(h, hc, rows, cols, w_in, qk_norm, lam_p, subln, w_out, lam_init, need_ctx):
    B, S = h.shape[:2]
    qc, kc, vc, qd, kd, vd = odd_project(h, lambda t: rope_2d(t, rows, cols), w_in, qk_norm)
    qc_c, kc_c, vc_c, qd_c, kd_c, vd_c = odd_project(hc, lambda t: t, w_in, qk_norm)
    lp = lam_p.astype(jnp.float32)
    lam = jnp.exp(jnp.sum(lp[0] * lp[1])) - jnp.exp(jnp.sum(lp[2] * lp[3])) + lam_init
    scale_c = HEAD_DIM ** -0.5
    kc_all = jnp.concatenate([kc_c, kc], axis=1)
    vc_all = jnp.concatenate([vc_c, vc], axis=1)
    kd_all = jnp.concatenate([kd_c, kd], axis=1)
    vd_all = jnp.concatenate([vd_c, vd], axis=1)
    oc = sweep_query_blocks(lambda qq: gqa_attend(qq, kc_all, vc_all, scale_c), qc)
    od = sweep_query_blocks(lambda qq: diff_attend(qq, kd_all, vd_all, lam), qd)
    od = rms_norm(od, subln) * (1.0 - lam_init)
    y = jnp.concatenate([oc.reshape(B, S, -1), od.reshape(B, S, -1)], axis=-1) @ w_out
    yc = None
    if need_ctx:
        C = hc.shape[1]
        oc_c = gqa_attend(qc_c, kc_c, vc_c, scale_c)
        od_c = rms_norm(diff_attend(qd_c, kd_c, vd_c, lam), subln) * (1.0 - lam_init)
        yc = jnp.concatenate([oc_c.reshape(B, C, -1), od_c.reshape(B, C, -1)], axis=-1) @ w_out
    return y, yc


def swiglu(h, w_in, w_out):
    g, u = jnp.split(h @ w_in, 2, axis=-1)
    return (jax.nn.silu(g) * u) @ w_out


def diff_lambda_init(layer):
    return 0.8 - 0.6 * math.exp(-0.3 * layer)


def setup_inputs(seed: int = 0) -> dict:
    key = jax.random.key(seed)
    ks = jax.random.split(key, 24)
    D = D_MODEL

    def nrm(k, shape, scale):
        return jax.random.normal(k, shape, jnp.float32) * scale

    return {
        "x": nrm(ks[0], (BATCH, SEQ, D), 1.0),
        "c": nrm(ks[1], (BATCH, D), 1.0),
        "ctx": nrm(ks[2], (BATCH, CTX_LEN, D), 1.0),
        "c_ctx": nrm(ks[3], (D,), 1.0),
        "ada_w": nrm(ks[4], (DEPTH, D, 6 * D), 0.5 * D ** -0.5),
        "ada_b": nrm(ks[5], (DEPTH, 6 * D), 0.02),
        "norm_g": 1.0 + nrm(ks[6], (DEPTH, 4, D), 0.1),
        "ffn_w_in": nrm(ks[7], (DEPTH, D, 2 * FFN_HIDDEN), D ** -0.5),
        "ffn_w_out": nrm(ks[8], (DEPTH, FFN_HIDDEN, D), FFN_HIDDEN ** -0.5),
        "ev_w_in": nrm(ks[9], (N_EVEN, D, EVEN_IN), D ** -0.5),
        "ev_sink": nrm(ks[10], (N_EVEN, A_Q_HEADS), 0.5),
        "ev_q_norm": 1.0 + nrm(ks[11], (N_EVEN, B_Q_RANK), 0.1),
        "ev_w_uq": nrm(ks[12], (N_EVEN, B_Q_RANK, B_HEADS * (B_NOPE + B_ROPE)), B_Q_RANK ** -0.5),
        "ev_kv_norm": 1.0 + nrm(ks[13], (N_EVEN, B_KV_RANK), 0.1),
        "ev_w_ukv": nrm(ks[14], (N_EVEN, B_KV_RANK, B_HEADS * (B_NOPE + B_V)), B_KV_RANK ** -0.5),
        "ev_w_out": nrm(ks[15], (N_EVEN, MIX_WIDTH, D), MIX_WIDTH ** -0.5),
        "od_w_in": nrm(ks[16], (N_ODD, D, ODD_IN), D ** -0.5),
        "od_qk_norm": 1.0 + nrm(ks[17], (N_ODD, 2, HEAD_DIM), 0.1),
        "od_lambda": nrm(ks[18], (N_ODD, 4, HEAD_DIM), 0.1),
        "od_subln": 1.0 + nrm(ks[19], (N_ODD, D_V), 0.1),
        "od_w_out": nrm(ks[20], (N_ODD, MIX_WIDTH, D), MIX_WIDTH ** -0.5),
    }


def reference(x, c, ctx, c_ctx, ada_w, ada_b, norm_g, ffn_w_in, ffn_w_out,
              ev_w_in, ev_sink, ev_q_norm, ev_w_uq, ev_kv_norm, ev_w_ukv, ev_w_out,
              od_w_in, od_qk_norm, od_lambda, od_subln, od_w_out):
    B, S, _ = x.shape
    ROWS = S // GRID_W
    rows = jnp.repeat(jnp.arange(ROWS), GRID_W)
    cols = jnp.tile(jnp.arange(GRID_W), ROWS)
    s_c = jax.nn.silu(c)
    s_cc = jax.nn.silu(c_ctx)
    xc = ctx
    for l in range(DEPTH):
        need_ctx = l < DEPTH - 1
        mod = (s_c @ ada_w[l] + ada_b[l])[:, None, :]
        mod_c = s_cc @ ada_w[l] + ada_b[l]
        sh1, sc1, g1, sh2, sc2, g2 = jnp.split(mod, 6, axis=-1)
        csh1, csc1, cg1, csh2, csc2, cg2 = jnp.split(mod_c, 6, axis=-1)
        h = rms_norm(x, norm_g[l, 0]) * (1.0 + sc1) + sh1
        hc = rms_norm(xc, norm_g[l, 0]) * (1.0 + csc1) + csh1
        if l % 2 == 0:
            e = l // 2
            y, yc = mixer_even(h, hc, rows, cols, ev_w_in[e], ev_sink[e], ev_q_norm[e], ev_w_uq[e],
                               ev_kv_norm[e], ev_w_ukv[e], ev_w_out[e], need_ctx)
        else:
            o = l // 2
            y, yc = mixer_odd(h, hc, rows, cols, od_w_in[o], od_qk_norm[o], od_lambda[o], od_subln[o],
                              od_w_out[o], diff_lambda_init(l), need_ctx)
        x = x + g1 * rms_norm(y, norm_g[l, 1])
        h = rms_norm(x, norm_g[l, 2]) * (1.0 + sc2) + sh2
        x = x + g2 * rms_norm(swiglu(h, ffn_w_in[l], ffn_w_out[l]), norm_g[l, 3])
        if need_ctx:
            xc = xc + cg1 * rms_norm(yc, norm_g[l, 1])
            hc = rms_norm(xc, norm_g[l, 2]) * (1.0 + csc2) + csh2
            xc = xc + cg2 * rms_norm(swiglu(hc, ffn_w_in[l], ffn_w_out[l]), norm_g[l, 3])
    return x
```

```python
import contextlib
import math
import numpy as np
import concourse.bass as bass
import concourse.mybir as mybir
from concourse.bass_utils import run_bass_kernel_spmd

F32 = mybir.dt.float32
BF16 = mybir.dt.bfloat16
AF = mybir.ActivationFunctionType
ALU = mybir.AluOpType
AX = mybir.AxisListType

ENGS = ("pe", "act", "dve", "pool", "sp")


class Buf:
    __slots__ = ("name", "w", "r")

    def __init__(self, name=""):
        self.name = name
        self.w = None
        self.r = []


class Op:
    __slots__ = ("eng", "fn", "deps", "sig", "idx", "dma", "dsem", "dval")

    def __init__(self, eng, fn, dma):
        self.eng = eng
        self.fn = fn
        self.deps = []
        self.sig = False
        self.idx = 0
        self.dma = dma
        self.dsem = None
        self.dval = 0


class Prog:
    NDSEM = 48
    NHW = 32

    def __init__(self, nc):
        self.nc = nc
        self.ops = []
        self.ndma = 0
        self.nsw = 0
        self.last = {}
        self.dmas_since = []

    def add(self, eng, fn, reads=(), writes=(), dma=False):
        op = Op(eng, fn, dma)
        deps = {}
        for b in reads:
            if b.w is not None:
                deps[id(b.w)] = b.w
        for b in writes:
            if b.w is not None:
                deps[id(b.w)] = b.w
            for r in b.r:
                deps[id(r)] = r
        for d in deps.values():
            if d is op:
                continue
            if d.eng == "pe" and eng == "pe" and not d.dma and not dma:
                continue
            op.deps.append(d)
            if not d.dma:
                d.sig = True
        for b in writes:
            b.w = op
            b.r = []
        for b in reads:
            if b.w is not op:
                b.r.append(op)
        if dma:
            if eng == "pool":
                op.dsem = self.NHW + self.nsw % (self.NDSEM - self.NHW)
                self.nsw += 1
            else:
                op.dsem = self.ndma % self.NHW
                self.ndma += 1
            self.dmas_since.append(op)
        else:
            self.last[eng] = op
        self.ops.append(op)
        return op

    def pe(self, fn, reads=(), writes=()):
        return self.add("pe", fn, reads, writes)

    def act(self, fn, reads=(), writes=()):
        return self.add("act", fn, reads, writes)

    def dve(self, fn, reads=(), writes=()):
        return self.add("dve", fn, reads, writes)

    def pool(self, fn, reads=(), writes=()):
        return self.add("pool", fn, reads, writes)

    def dma(self, fn, reads=(), writes=(), q="sp"):
        return self.add(q, fn, reads, writes, dma=True)

    def barrier(self):
        lasts = dict(self.last)
        dm = {}
        for d in self.dmas_since:
            dm[d.dsem] = d
        self.dmas_since = []
        for e in ENGS:
            op = Op(e, lambda eng: eng.nop(), False)
            for e2, lo in lasts.items():
                if e2 != e:
                    op.deps.append(lo)
                    lo.sig = True
            op.deps.extend(dm.values())
            self.ops.append(op)
            self.last[e] = op

    def emit(self):
        nc = self.nc
        cnt = {e: 0 for e in ENGS}
        dcount = [0] * self.NDSEM
        for op in self.ops:
            if op.dma:
                dcount[op.dsem] += 16
                op.dval = dcount[op.dsem]
            elif op.sig:
                cnt[op.eng] += 1
                op.idx = cnt[op.eng]
        per = {e: [op for op in self.ops if op.eng == e] for e in ENGS}
        with contextlib.ExitStack() as st:
            esem = {e: st.enter_context(nc.semaphore("s_" + e)) for e in ENGS}
            dsem = [st.enter_context(nc.semaphore("d%d" % i)) for i in range(self.NDSEM)]
            block = st.enter_context(nc.Block())

            def run(ename, eng):
                waited = {}
                for op in per[ename]:
                    for d in op.deps:
                        if d.dma:
                            key, val, sem = ("d", d.dsem), d.dval, dsem[d.dsem]
                        else:
                            key, val, sem = ("e", d.eng), d.idx, esem[d.eng]
                        if waited.get(key, 0) < val:
                            eng.wait_ge(sem, val)
                            waited[key] = val
                    if op.dma:
                        key = ("d", op.dsem)
                        if op.dval > 16 and waited.get(key, 0) < op.dval - 16:
                            eng.wait_ge(dsem[op.dsem], op.dval - 16)
                            waited[key] = op.dval - 16
                        op.fn(eng).then_inc(dsem[op.dsem], 16)
                    else:
                        ins = op.fn(eng)
                        if op.sig:
                            ins.then_inc(esem[ename], 1)
                if ename == "sp":
                    for i in range(self.NDSEM):
                        if dcount[i] > 0 and waited.get(("d", i), 0) < dcount[i]:
                            eng.wait_ge(dsem[i], dcount[i])

            @block.tensor
            def _(eng):
                run("pe", eng)

            @block.scalar
            def _(eng):
                run("act", eng)

            @block.vector
            def _(eng):
                run("dve", eng)

            @block.gpsimd
            def _(eng):
                run("pool", eng)

            @block.sync
            def _(eng):
                run("sp", eng)


class Arena:
    def __init__(self, nc, base, limit):
        self.nc, self.off, self.limit, self.n = nc, base, limit, 0

    def alloc(self, shape, dtype):
        per = 1
        for s in shape[1:]:
            per *= s
        nbytes = per * (4 if dtype == F32 else 2)
        off = (self.off + 31) // 32 * 32
        assert off + nbytes <= self.limit, ("SBUF arena overflow", off, nbytes, self.limit)
        self.off = off + nbytes
        self.n += 1
        return self.nc.alloc_sbuf_tensor_at("t%d" % self.n, list(shape), dtype, offset=off)

    def mark(self):
        return self.off

    def reset(self, m):
        self.off = m


class TB:
    __slots__ = ("t", "b")

    def __init__(self, t):
        self.t = t
        self.b = Buf()


class Ring:
    def __init__(self, arena, n, shape, dtype):
        self.items = [TB(arena.alloc(shape, dtype)) for _ in range(n)]
        self.i = 0

    def next(self):
        it = self.items[self.i % len(self.items)]
        self.i += 1
        return it


DEBUG_STOP = None
DEBUG_ROPE = None
DEBUG_NT = 1
D = 1024
FH = 2816
EPS = 1e-6
EV_IN = 1184
OD_IN = 2304


def lam_init_of(layer):
    return 0.8 - 0.6 * math.exp(-0.3 * layer)


def build_program(S, C, L, NB):
    NL, NC = S // 128, C // 128
    NT = NL + NC
    T = S + C
    NE, NO = (L + 1) // 2, L // 2
    nc = bass.Bass("TRN2", target_bir_lowering=False)

    def din(name, shape):
        return nc.dram_tensor(name, list(shape), F32, kind="ExternalInput").ap()

    x_in = din("x", [NB, S, D])
    ctx_in = din("ctx", [NB, C, D])
    cvec = din("cvec", [NB + 1, D])
    ada_w = din("ada_w", [L, D, 6 * D])
    ada_b = din("ada_b", [L, 6 * D])
    norm_g = din("norm_g", [L, 4, D])
    ffn_w_in = din("ffn_w_in", [L, D, 2 * FH])
    ffn_w_out = din("ffn_w_out", [L, FH, D])
    ev_w_in = din("ev_w_in", [NE, D, EV_IN])
    ev_sink = din("ev_sink", [NE, 8])
    ev_q_norm = din("ev_q_norm", [NE, 256])
    ev_w_uq = din("ev_w_uq", [NE, 256, 768])
    ev_kv_norm = din("ev_kv_norm", [NE, 128])
    ev_w_ukv = din("ev_w_ukv", [NE, 128, 1024])
    ev_w_out = din("ev_w_out", [NE, 1024, D])
    od_w_in = din("od_w_in", [max(NO, 1), D, OD_IN])
    od_qk_norm = din("od_qk_norm", [max(NO, 1), 2, 64])
    od_lambda = din("od_lambda", [max(NO, 1), 4, 64])
    od_subln = din("od_subln", [max(NO, 1), 128])
    od_w_out = din("od_w_out", [max(NO, 1), 1024, D])
    rope64 = din("rope64", [S, 128])
    rope32 = din("rope32", [S, 64])
    ident_in = din("ident", [128, 128])
    masks_in = din("masks", [2, 128, 128])
    out = nc.dram_tensor("out", [NB, S, D], F32, kind="ExternalOutput").ap()
    xres = nc.dram_tensor("xres", [NB, T, D], F32, kind="Internal").ap()
    x1s = nc.dram_tensor("x1s", [NB, T, D], F32, kind="Internal").ap()
    modv = nc.dram_tensor("modv", [L, 6, NB + 1, D], F32, kind="Internal").ap()

    P = Prog(nc)
    ar = Arena(nc, 16640, 229376 - 256)
    ps = nc.alloc_psum_tensor("ps", [128, 8, 512], F32)
    PB = [Buf("ps%d" % i) for i in range(8)]
    xresB = [[Buf() for _ in range(NT)] for _ in range(NB)]
    x1B = [[Buf() for _ in range(NT)] for _ in range(NB)]
    modvB = [Buf() for _ in range(L)]

    def psb(b):
        return ps[:, b, :]

    def psbf(b):
        return ps[:, b, :].bitcast(BF16)

    def ps2(b):
        return ps[:, b:b + 2, :].rearrange("p a b -> p (a b)")

    identb = TB(ar.alloc([128, 128], BF16))
    maskb = TB(ar.alloc([128, 2, 128], BF16))
    onesf = TB(ar.alloc([128, 64], F32))
    epsb = TB(ar.alloc([128, 1], F32))
    oneb = TB(ar.alloc([128, 1], F32))
    P.dma(lambda e: e.dma_start(out=identb.t[:], in_=ident_in[:, :]), writes=[identb.b], q="pool")
    P.dma(lambda e: e.dma_start(out=maskb.t[:], in_=masks_in.rearrange("m k q -> k m q")), writes=[maskb.b], q="pool")
    P.pool(lambda e: e.memset(onesf.t[:], 1.0), writes=[onesf.b])
    P.pool(lambda e: e.memset(epsb.t[:], EPS), writes=[epsb.b])
    P.pool(lambda e: e.memset(oneb.t[:], 1.0), writes=[oneb.b])
    base_mark = ar.mark()

    def rstd_from_ss(ss, n, col=slice(0, 1)):
        P.act(lambda e: e.activation(out=ss.t[:, col], in_=ss.t[:, col], func=AF.Ln, scale=1.0 / n, bias=epsb.t[:]),
              reads=[ss.b, epsb.b], writes=[ss.b])
        P.act(lambda e: e.activation(out=ss.t[:, col], in_=ss.t[:, col], func=AF.Exp, scale=-0.5),
              reads=[ss.b], writes=[ss.b])

    def phase0():
        cT = TB(ar.alloc([128, 8, NB + 1], F32))
        t1 = TB(ar.alloc([128, 8, NB + 1], F32))
        scT = TB(ar.alloc([128, 8, NB + 1], F32))
        R = NB + 1
        for r in range(R):
            P.dma(lambda e, r=r: e.dma_start(out=cT.t[:, :, r], in_=cvec[r, :].rearrange("(k p) -> p k", p=128), allow_slow_non_contiguous=True),
                  writes=[cT.b])
        P.act(lambda e: e.activation(out=t1.t[:], in_=cT.t[:], func=AF.Exp, scale=-1.0), reads=[cT.b], writes=[t1.b])
        P.act(lambda e: e.activation(out=t1.t[:], in_=t1.t[:], func=AF.Ln, bias=oneb.t[:]), reads=[t1.b, oneb.b], writes=[t1.b])
        P.act(lambda e: e.activation(out=t1.t[:], in_=t1.t[:], func=AF.Exp, scale=-1.0), reads=[t1.b], writes=[t1.b])
        P.dve(lambda e: e.tensor_tensor(out=scT.t[:], in0=cT.t[:], in1=t1.t[:], op=ALU.mult), reads=[cT.b, t1.b], writes=[scT.b])
        awr = Ring(ar, 2, [128, 8, 512], F32)
        modrow = TB(ar.alloc([R, 6 * D], F32))
        abt = TB(ar.alloc([R, 6 * D], F32))
        ngt = TB(ar.alloc([R, 4 * D], F32))
        mv = TB(ar.alloc([R, 6, D], F32))
        for l in range(L):
            for n in range(12):
                aw = awr.next()
                P.dma(lambda e, aw=aw, l=l, n=n: e.dma_start(out=aw.t[:], in_=ada_w[l, :, n * 512:(n + 1) * 512].rearrange("(k p) n -> p k n", p=128)),
                      writes=[aw.b])
                bk = n % 2
                for k in range(8):
                    P.pe(lambda e, aw=aw, k=k, bk=bk: e.matmul(ps[0:R, bk, :], lhsT=scT.t[:, k, :], rhs=aw.t[:, k, :], start=(k == 0), stop=(k == 7)),
                         reads=[scT.b, aw.b], writes=[PB[bk]])
                P.act(lambda e, n=n, bk=bk: e.activation(out=modrow.t[:, n * 512:(n + 1) * 512], in_=ps[0:R, bk, :], func=AF.Copy),
                      reads=[PB[bk]], writes=[modrow.b])
            P.dma(lambda e, l=l: e.dma_start(out=abt.t[:], in_=ada_b[l:l + 1, :].partition_broadcast(R)), writes=[abt.b])
            P.dma(lambda e, l=l: e.dma_start(out=ngt.t[:], in_=norm_g[l:l + 1, :, :].rearrange("o j d -> o (j d)").partition_broadcast(R)), writes=[ngt.b])
            P.dve(lambda e: e.tensor_tensor(out=modrow.t[:], in0=modrow.t[:], in1=abt.t[:], op=ALU.add), reads=[modrow.b, abt.b], writes=[modrow.b])

            def seg(i):
                return modrow.t[:, i * D:(i + 1) * D]

            def ng(i):
                return ngt.t[:, i * D:(i + 1) * D]
            P.dve(lambda e: e.scalar_tensor_tensor(out=mv.t[:, 0, :], in0=seg(1), scalar=1.0, in1=ng(0), op0=ALU.add, op1=ALU.mult), reads=[modrow.b, ngt.b], writes=[mv.b])
            P.dve(lambda e: e.tensor_copy(out=mv.t[:, 1, :], in_=seg(0)), reads=[modrow.b], writes=[mv.b])
            P.dve(lambda e: e.tensor_tensor(out=mv.t[:, 2, :], in0=seg(2), in1=ng(1), op=ALU.mult), reads=[modrow.b, ngt.b], writes=[mv.b])
            P.dve(lambda e: e.scalar_tensor_tensor(out=mv.t[:, 3, :], in0=seg(4), scalar=1.0, in1=ng(2), op0=ALU.add, op1=ALU.mult), reads=[modrow.b, ngt.b], writes=[mv.b])
            P.dve(lambda e: e.tensor_copy(out=mv.t[:, 4, :], in_=seg(3)), reads=[modrow.b], writes=[mv.b])
            P.dve(lambda e: e.tensor_tensor(out=mv.t[:, 5, :], in0=seg(5), in1=ng(3), op=ALU.mult), reads=[modrow.b, ngt.b], writes=[mv.b])
            P.dma(lambda e, l=l: e.dma_start(out=modv[l].rearrange("j r d -> r j d"), in_=mv.t[:]), reads=[mv.b], writes=[modvB[l]])

    phase0()
    P.barrier()
    ar.reset(base_mark)
    if DEBUG_STOP == "p0":
        P.emit()
        return nc

    def x_src(l, b, tt):
        if l == 0:
            if tt < NC:
                return ctx_in[b, tt * 128:(tt + 1) * 128, :], []
            return x_in[b, (tt - NC) * 128:(tt - NC + 1) * 128, :], []
        return xres[b, tt * 128:(tt + 1) * 128, :], [xresB[b][tt]]

    def load_bc(dst, src_row, reads=()):
        P.dma(lambda e: e.dma_start(out=dst.t[:], in_=src_row.partition_broadcast(128)), reads=list(reads), writes=[dst.b])

    def transposes(srcs, bank, src_bufs):
        for i, (ap, m) in enumerate(srcs):
            P.pe(lambda e, ap=ap, m=m, i=i: e.transpose(out=psbf(bank)[0:m, i * 128:(i + 1) * 128], in_=ap, identity=identb.t[:]),
                 reads=list(src_bufs) + [identb.b], writes=[PB[bank]])

    def norm_mod_T(xt, gt, sht, hT_dst_ap, hT_buf, rings):
        junk, ssr, tmpr, hbr = rings
        jk, ss, tmp, hb = junk.next(), ssr.next(), tmpr.next(), hbr.next()
        P.act(lambda e: e.activation(out=jk.t[:], in_=xt.t[:], func=AF.Square, accum_out=ss.t[:, 0:1]), reads=[xt.b], writes=[jk.b, ss.b])
        rstd_from_ss(ss, D)
        P.pool(lambda e: e.tensor_tensor(out=tmp.t[:], in0=xt.t[:], in1=gt.t[:], op=ALU.mult), reads=[xt.b, gt.b], writes=[tmp.b])
        P.dve(lambda e: e.scalar_tensor_tensor(out=hb.t[:], in0=tmp.t[:], scalar=ss.t[:, 0:1], in1=sht.t[:], op0=ALU.mult, op1=ALU.add),
              reads=[tmp.b, ss.b, sht.b], writes=[hb.b])
        transposes([(hb.t[:, k * 128:(k + 1) * 128], 128) for k in range(8)], 7, [hb.b])
        P.act(lambda e: e.activation(out=hT_dst_ap, in_=psbf(7).rearrange("p (k t) -> p k t", k=8), func=AF.Copy), reads=[PB[7]], writes=[hT_buf])

    def rope_apply(src_ap, nh, cs, dst_ap, tmps, src_bufs, dst_bufs, dh=64, split=None):
        t1, t2 = tmps
        q = dh // 4
        if DEBUG_ROPE == "copy":
            P.act(lambda e: e.activation(out=dst_ap, in_=(src_ap if split is None else src_ap.rearrange("p (j s) d -> p j s d", j=split)), func=AF.Copy), reads=list(src_bufs), writes=list(dst_bufs))
            return
        cos = cs.t[:, 0:dh].unsqueeze(1).to_broadcast([128, nh, dh])
        x3 = t1.t[:, 0:nh * dh].rearrange("p (h d) -> p h d", h=nh)
        P.act(lambda e: e.activation(out=x3, in_=src_ap, func=AF.Copy), reads=list(src_bufs), writes=[t1.b])
        s5 = t1.t[:, 0:nh * dh].rearrange("p (h b x d) -> p h b x d", h=nh, b=2, x=2)
        o5 = t2.t[:, 0:nh * dh].rearrange("p (h b x d) -> p h b x d", h=nh, b=2, x=2)
        sn = cs.t[:, dh:2 * dh].rearrange("p (b x d) -> p b x d", b=2, x=2)
        for xx in range(2):
            P.dve(lambda e, xx=xx: e.tensor_tensor(out=o5[:, :, :, xx, :], in0=s5[:, :, :, 1 - xx, :],
                                                 in1=sn[:, :, xx, :].unsqueeze(1).to_broadcast([128, nh, 2, q]), op=ALU.mult),
                  reads=[t1.b, cs.b], writes=[t2.b])
        P.dve(lambda e: e.tensor_tensor(out=x3, in0=x3, in1=cos, op=ALU.mult), reads=[t1.b, cs.b], writes=[t1.b])
        if split is None:
            a1 = t1.t[:, 0:nh * dh].rearrange("p (h d) -> p h d", h=nh)
            a2 = t2.t[:, 0:nh * dh].rearrange("p (h d) -> p h d", h=nh)
        else:
            a1 = t1.t[:, 0:nh * dh].rearrange("p (j s d) -> p j s d", j=split, d=dh)
            a2 = t2.t[:, 0:nh * dh].rearrange("p (j s d) -> p j s d", j=split, d=dh)
        if DEBUG_ROPE == "noswap":
            P.act(lambda e: e.activation(out=dst_ap, in_=a1, func=AF.Copy), reads=[t1.b, t2.b], writes=list(dst_bufs))
            return
        fin = P.dve if DEBUG_ROPE == "dvefin" else P.pool
        fin(lambda e: e.tensor_tensor(out=dst_ap, in0=a1, in1=a2, op=ALU.add), reads=[t1.b, t2.b], writes=list(dst_bufs))

    def do_layer(l):
        even = (l % 2 == 0)
        li = l // 2
        need_ctx = l < L - 1
        lam0 = lam_init_of(l)
        layer_mark = ar.mark()
        if even:
            qn_bc = TB(ar.alloc([128, 256], F32))
            kvn_bc = TB(ar.alloc([128, 128], F32))
            sinkx = TB(ar.alloc([128, 8], F32))
            load_bc(qn_bc, ev_q_norm[li:li + 1, :])
            load_bc(kvn_bc, ev_kv_norm[li:li + 1, :])
            load_bc(sinkx, ev_sink[li:li + 1, :])
            P.act(lambda e: e.activation(out=sinkx.t[:], in_=sinkx.t[:], func=AF.Exp), reads=[sinkx.b], writes=[sinkx.b])
        else:
            qkg = TB(ar.alloc([128, 10, 64], F32))
            lamt = TB(ar.alloc([128, 4, 64], F32))
            lamw = TB(ar.alloc([128, 2, 64], F32))
            lams = TB(ar.alloc([128, 4], F32))
            subs = TB(ar.alloc([64, 2], F32))
            for h in range(10):
                P.dma(lambda e, h=h: e.dma_start(out=qkg.t[:, h, :], in_=od_qk_norm[li, (0 if h < 8 else 1):(1 if h < 8 else 2), :].partition_broadcast(128)),
                      writes=[qkg.b])
            P.dma(lambda e: e.dma_start(out=lamt.t[:].rearrange("p a d -> p (a d)"), in_=od_lambda[li:li + 1, :, :].rearrange("o a d -> o (a d)").partition_broadcast(128)),
                  writes=[lamt.b])
            P.dma(lambda e: e.dma_start(out=subs.t[:], in_=od_subln[li, :].rearrange("(h p) -> p h", p=64), allow_slow_non_contiguous=True), writes=[subs.b])
            P.dve(lambda e: e.tensor_tensor(out=lamw.t[:, 0, :], in0=lamt.t[:, 0, :], in1=lamt.t[:, 1, :], op=ALU.mult), reads=[lamt.b], writes=[lamw.b])
            P.dve(lambda e: e.tensor_tensor(out=lamw.t[:, 1, :], in0=lamt.t[:, 2, :], in1=lamt.t[:, 3, :], op=ALU.mult), reads=[lamt.b, lamw.b], writes=[lamw.b])
            P.dve(lambda e: e.tensor_reduce(out=lams.t[:, 0:2], in_=lamw.t[:], axis=AX.X, op=ALU.add), reads=[lamw.b], writes=[lams.b])
            P.act(lambda e: e.activation(out=lams.t[:, 0:2], in_=lams.t[:, 0:2], func=AF.Exp), reads=[lams.b], writes=[lams.b])
            P.dve(lambda e: e.tensor_tensor(out=lams.t[:, 2:3], in0=lams.t[:, 1:2], in1=lams.t[:, 0:1], op=ALU.subtract), reads=[lams.b], writes=[lams.b])
            P.dve(lambda e: e.tensor_scalar(out=lams.t[:, 3:4], in0=lams.t[:, 2:3], scalar1=-lam0, scalar2=None, op0=ALU.add), reads=[lams.b], writes=[lams.b])
            P.dve(lambda e: e.tensor_scalar(out=subs.t[:], in0=subs.t[:], scalar1=1.0 - lam0, scalar2=None, op0=ALU.mult), reads=[subs.b], writes=[subs.b])
        const_mark = ar.mark()

        def do_seq(b):
            P.barrier()
            ar.reset(const_mark)
            if even:
                qaT = ar.alloc([128, 4, T], BF16)
                kaT = ar.alloc([128, T], BF16)
                va = ar.alloc([128, NT, 1, 3, 64], BF16)
                qbT = ar.alloc([128, 8, T], BF16)
                kbT = ar.alloc([128, 8, T], BF16)
                vb = ar.alloc([128, NT, 4, 3, 64], BF16)
                vones = [(va, 2), (vb, 8)]
            else:
                qcT = ar.alloc([128, 4, T], BF16)
                kcT = ar.alloc([128, T], BF16)
                vc = ar.alloc([128, NT, 1, 3, 64], BF16)
                qdT = ar.alloc([128, 4, T], BF16)
                kdT = ar.alloc([128, 4, T], BF16)
                vd = ar.alloc([128, NT, 4, 3, 64], BF16)
                vones = [(vc, 2), (vd, 8)]
            KB = [Buf() for _ in range(NT)]
            VB1 = Buf()
            for (vt, nh) in vones:
                P.pool(lambda e, vt=vt: e.memset(vt[:, :, :, 1, :], 1.0), writes=[VB1])
            qkv_mark = ar.mark()

            g1t = [TB(ar.alloc([128, D], F32)) for _ in range(2)]
            sh1t = [TB(ar.alloc([128, D], F32)) for _ in range(2)]
            for j, r in enumerate((b, NB)):
                load_bc(g1t[j], modv[l, 0, r:r + 1, :], [modvB[l]])
                load_bc(sh1t[j], modv[l, 1, r:r + 1, :], [modvB[l]])
            NIN = EV_IN if even else OD_IN
            win = TB(ar.alloc([128, 8, NIN], BF16))
            wsrc = ev_w_in[li] if even else od_w_in[li]
            for k2 in range(4):
                P.dma(lambda e, k2=k2: e.dma_start(out=win.t[:, 2 * k2:2 * k2 + 2, :], in_=wsrc[k2 * 256:(k2 + 1) * 256, :].rearrange("(k p) n -> p k n", p=128)),
                      writes=[win.b], q="pool")
            if even:
                wuq = TB(ar.alloc([128, 2, 768], BF16))
                wukv = TB(ar.alloc([128, 1024], BF16))
                P.dma(lambda e: e.dma_start(out=wuq.t[:], in_=ev_w_uq[li].rearrange("(k p) n -> p k n", p=128)), writes=[wuq.b], q="pool")
                P.dma(lambda e: e.dma_start(out=wukv.t[:], in_=ev_w_ukv[li]), writes=[wukv.b], q="pool")
            xr = Ring(ar, 2, [128, D], F32)
            rings = (Ring(ar, 1, [128, D], BF16), Ring(ar, 2, [128, 4], F32), Ring(ar, 1, [128, D], F32), Ring(ar, 2, [128, D], BF16))
            hTr = Ring(ar, 2, [128, 8, 128], BF16)
            csr64 = Ring(ar, 2, [128, 128], F32)
            csr32 = Ring(ar, 2, [128, 64], F32)
            rt1 = TB(ar.alloc([128, 512 if even else 1024], F32))
            rt2 = TB(ar.alloc([128, 512 if even else 1024], F32))
            stg = Ring(ar, 2, [128, 1024], BF16)
            ss2 = Ring(ar, 2, [128, 16], F32)
            sqt = TB(ar.alloc([128, 384 if even else 640], F32))
            nrm = Ring(ar, 2, [128, 256], BF16)
            smT = Ring(ar, 2, [128, 2, 128], BF16)
            kper = Ring(ar, 2, [128, 32], BF16)

            def p1_tile(tt):
                isctx = tt < NC
                pt = tt - NC
                tok = slice(tt * 128, (tt + 1) * 128)
                xt = xr.next()
                src, srcb = x_src(l, b, tt)
                P.dma(lambda e, xt=xt, src=src: e.dma_start(out=xt.t[:], in_=src), reads=srcb, writes=[xt.b])
                hT = hTr.next()
                norm_mod_T(xt, g1t[1 if isctx else 0], sh1t[1 if isctx else 0], hT.t[:], hT.b, rings)
                if not isctx:
                    cs64, cs32 = csr64.next(), csr32.next()
                    P.dma(lambda e, cs64=cs64, pt=pt: e.dma_start(out=cs64.t[:], in_=rope64[pt * 128:(pt + 1) * 128, :]), writes=[cs64.b])
                    if even:
                        P.dma(lambda e, cs32=cs32, pt=pt: e.dma_start(out=cs32.t[:], in_=rope32[pt * 128:(pt + 1) * 128, :]), writes=[cs32.b])
                nchunks = (NIN + 511) // 512
                for n in range(nchunks):
                    w = min(512, NIN - n * 512)
                    for k in range(8):
                        P.pe(lambda e, hT=hT, n=n, k=k, w=w: e.matmul(ps[:, n, 0:w], lhsT=hT.t[:, k, :], rhs=win.t[:, k, n * 512:n * 512 + w], start=(k == 0), stop=(k == 7)),
                             reads=[hT.b, win.b], writes=[PB[n]])
                flat = ps[:, 0:5, :].rearrange("p a b -> p (a b)")
                st = stg.next()
                if even:
                    qk_src = flat[:, 0:640].rearrange("p (h d) -> p h d", h=10)
                    dq = st.t[:, 0:512].rearrange("p (s j d) -> p j s d", s=4, j=2)
                    dk = st.t[:, 512:640].rearrange("p (h d) -> p h d", h=2)
                    if isctx:
                        P.act(lambda e, dq=dq, qk_src=qk_src: e.activation(out=dq, in_=qk_src[:, 0:8, :].rearrange("p (j s) d -> p j s d", j=2), func=AF.Copy), reads=[PB[0]], writes=[st.b])
                        P.act(lambda e, dk=dk, qk_src=qk_src: e.activation(out=dk, in_=qk_src[:, 8:10, :], func=AF.Copy), reads=[PB[1]], writes=[st.b])
                    else:
                        rope_apply(qk_src[:, 0:8, :], 8, cs64, dq, (rt1, rt2), [PB[0]], [st.b], split=2)
                        rope_apply(qk_src[:, 8:10, :], 2, cs64, dk, (rt1, rt2), [PB[1]], [st.b])
                    transposes([(st.t[:, s * 128:(s + 1) * 128], 128) for s in range(5)], 7, [st.b])
                    P.act(lambda e, tok=tok: e.activation(out=qaT[:, :, tok], in_=psbf(7)[:, 0:512].rearrange("p (s t) -> p s t", s=4), func=AF.Copy), reads=[PB[7]], writes=[KB[tt]])
                    P.act(lambda e, tok=tok: e.activation(out=kaT[:, tok], in_=psbf(7)[:, 512:640], func=AF.Copy), reads=[PB[7]], writes=[KB[tt]])
                    P.act(lambda e, tt=tt: e.activation(out=va[:, tt, :, 0:3:2, :], in_=flat[:, 640:768].rearrange("p (a e d) -> p a e d", a=1, e=2), func=AF.Copy), reads=[PB[1]], writes=[KB[tt]])
                    ss = ss2.next()
                    P.act(lambda e, ss=ss: e.activation(out=sqt.t[:, 0:256], in_=flat[:, 768:1024], func=AF.Square, accum_out=ss.t[:, 0:1]), reads=[PB[1]], writes=[sqt.b, ss.b])
                    P.act(lambda e, ss=ss: e.activation(out=sqt.t[:, 256:384], in_=flat[:, 1024:1152], func=AF.Square, accum_out=ss.t[:, 1:2]), reads=[PB[2]], writes=[sqt.b, ss.b])
                    rstd_from_ss(ss, 256, slice(0, 1))
                    rstd_from_ss(ss, 128, slice(1, 2))
                    nq = nrm.next()
                    P.dve(lambda e, nq=nq, ss=ss: e.scalar_tensor_tensor(out=nq.t[:, 0:256], in0=flat[:, 768:1024], scalar=ss.t[:, 0:1], in1=qn_bc.t[:], op0=ALU.mult, op1=ALU.mult),
                          reads=[PB[1], ss.b, qn_bc.b], writes=[nq.b])
                    transposes([(nq.t[:, k * 128:(k + 1) * 128], 128) for k in range(2)], 7, [nq.b])
                    cqT = smT.next()
                    P.act(lambda e, cqT=cqT: e.activation(out=cqT.t[:], in_=psbf(7)[:, 0:256].rearrange("p (k t) -> p k t", k=2), func=AF.Copy), reads=[PB[7]], writes=[cqT.b])
                    for n, (c0, w) in enumerate(((0, 512), (512, 256))):
                        for k in range(2):
                            P.pe(lambda e, cqT=cqT, n=n, k=k, c0=c0, w=w: e.matmul(ps[:, 3 + n, 0:w], lhsT=cqT.t[:, k, :], rhs=wuq.t[:, k, c0:c0 + w], start=(k == 0), stop=(k == 1)),
                                 reads=[cqT.b, wuq.b], writes=[PB[3 + n]])
                    qb_src = ps[:, 3:5, :].rearrange("p a b -> p (a b)")[:, 0:768].rearrange("p (h d) -> p h d", h=8)
                    st2 = stg.next()
                    qb_dst = st2.t[:, 0:768].rearrange("p (h d) -> p h d", h=8)
                    if isctx:
                        P.act(lambda e, qb_dst=qb_dst, qb_src=qb_src: e.activation(out=qb_dst, in_=qb_src, func=AF.Copy), reads=[PB[3], PB[4]], writes=[st2.b])
                    else:
                        P.act(lambda e, qb_dst=qb_dst, qb_src=qb_src: e.activation(out=qb_dst[:, :, 0:64], in_=qb_src[:, :, 0:64], func=AF.Copy), reads=[PB[3], PB[4]], writes=[st2.b])
                        rope_apply(qb_src[:, :, 64:96], 8, cs32, qb_dst[:, :, 64:96], (rt1, rt2), [PB[3], PB[4]], [st2.b], dh=32)
                    transposes([(st2.t[:, h * 96:(h + 1) * 96], 96) for h in range(8)], 7, [st2.b])
                    P.act(lambda e, tok=tok: e.activation(out=qbT[0:96, :, tok], in_=psbf(7)[0:96, :].rearrange("p (h t) -> p h t", h=8), func=AF.Copy), reads=[PB[7]], writes=[KB[tt]])
                    nk = nrm.next()
                    P.dve(lambda e, nk=nk, ss=ss: e.scalar_tensor_tensor(out=nk.t[:, 0:128], in0=flat[:, 1024:1152], scalar=ss.t[:, 1:2], in1=kvn_bc.t[:], op0=ALU.mult, op1=ALU.mult),
                          reads=[PB[2], ss.b, kvn_bc.b], writes=[nk.b])
                    transposes([(nk.t[:, 0:128], 128)], 7, [nk.b])
                    ckT = smT.next()
                    P.act(lambda e, ckT=ckT: e.activation(out=ckT.t[:, 0, :], in_=psbf(7)[:, 0:128], func=AF.Copy), reads=[PB[7]], writes=[ckT.b])
                    for n in range(2):
                        P.pe(lambda e, ckT=ckT, n=n: e.matmul(ps[:, 5 + n, :], lhsT=ckT.t[:, 0, :], rhs=wukv.t[:, n * 512:(n + 1) * 512], start=True, stop=True),
                             reads=[ckT.b, wukv.b], writes=[PB[5 + n]])
                    kv_src = ps[:, 5:7, :].rearrange("p a b -> p (a b)").rearrange("p (h d) -> p h d", h=8)
                    st3 = stg.next()
                    kb_dst = st3.t[:, 0:768].rearrange("p (h d) -> p h d", h=8)
                    P.act(lambda e, kb_dst=kb_dst, kv_src=kv_src: e.activation(out=kb_dst[:, :, 0:64], in_=kv_src[:, :, 0:64], func=AF.Copy), reads=[PB[5], PB[6]], writes=[st3.b])
                    P.act(lambda e, tt=tt, kv_src=kv_src: e.activation(out=vb[:, tt, :, 0:3:2, :], in_=kv_src[:, :, 64:128].rearrange("p (a e) d -> p a e d", e=2), func=AF.Copy), reads=[PB[5], PB[6]], writes=[KB[tt]])
                    kp = kper.next()
                    kpe_src = flat[:, 1152:1184].rearrange("p (h d) -> p h d", h=1)
                    if isctx:
                        P.act(lambda e, kp=kp, kpe_src=kpe_src: e.activation(out=kp.t[:].rearrange("p (h d) -> p h d", h=1), in_=kpe_src, func=AF.Copy), reads=[PB[2]], writes=[kp.b])
                    else:
                        rope_apply(kpe_src, 1, cs32, kp.t[:].rearrange("p (h d) -> p h d", h=1), (rt1, rt2), [PB[2]], [kp.b], dh=32)
                    P.pool(lambda e, kp=kp, kb_dst=kb_dst: e.tensor_copy(out=kb_dst[:, :, 64:96], in_=kp.t[:].unsqueeze(1).to_broadcast([128, 8, 32])), reads=[kp.b], writes=[st3.b])
                    transposes([(st3.t[:, h * 96:(h + 1) * 96], 96) for h in range(8)], 7, [st3.b])
                    P.act(lambda e, tok=tok: e.activation(out=kbT[0:96, :, tok], in_=psbf(7)[0:96, :].rearrange("p (h t) -> p h t", h=8), func=AF.Copy), reads=[PB[7]], writes=[KB[tt]])
                else:
                    qk_src = flat[:, 0:640].rearrange("p (h d) -> p h d", h=10)
                    ss = ss2.next()
                    P.act(lambda e: e.activation(out=sqt.t[:], in_=flat[:, 0:640], func=AF.Square), reads=[PB[0], PB[1]], writes=[sqt.b])
                    P.dve(lambda e, ss=ss: e.tensor_reduce(out=ss.t[:, 0:10], in_=sqt.t[:].rearrange("p (h d) -> p h d", h=10), axis=AX.X, op=ALU.add), reads=[sqt.b], writes=[ss.b])
                    rstd_from_ss(ss, 64, slice(0, 10))
                    xn = rt1.t[:, 0:640].rearrange("p (h d) -> p h d", h=10)
                    xs = rt2.t[:, 0:640].rearrange("p (h d) -> p h d", h=10)
                    P.act(lambda e, qk_src=qk_src, xs=xs: e.activation(out=xs, in_=qk_src, func=AF.Copy), reads=[PB[0], PB[1]], writes=[rt2.b])
                    P.dve(lambda e, ss=ss, xs=xs, xn=xn: e.tensor_tensor(out=xn, in0=xs, in1=ss.t[:, 0:10].unsqueeze(2).to_broadcast([128, 10, 64]), op=ALU.mult),
                          reads=[rt2.b, ss.b], writes=[rt1.b])
                    xg = sqt.t[:].rearrange("p (h d) -> p h d", h=10)
                    P.pool(lambda e, xn=xn, xg=xg: e.tensor_tensor(out=xg, in0=xn, in1=qkg.t[:], op=ALU.mult), reads=[rt1.b, qkg.b], writes=[sqt.b])
                    dq = st.t[:, 0:512].rearrange("p (s j d) -> p j s d", s=4, j=2)
                    dk = st.t[:, 512:640].rearrange("p (h d) -> p h d", h=2)
                    if isctx:
                        P.pool(lambda e, dq=dq, xg=xg: e.tensor_copy(out=dq, in_=xg[:, 0:8, :].rearrange("p (j s) d -> p j s d", j=2)), reads=[sqt.b], writes=[st.b])
                        P.pool(lambda e, dk=dk, xg=xg: e.tensor_copy(out=dk, in_=xg[:, 8:10, :]), reads=[sqt.b], writes=[st.b])
                    else:
                        rope_apply(xg[:, 0:8, :], 8, cs64, dq, (rt1, rt2), [sqt.b], [st.b], split=2)
                        rope_apply(xg[:, 8:10, :], 2, cs64, dk, (rt1, rt2), [sqt.b], [st.b])
                    transposes([(st.t[:, s * 128:(s + 1) * 128], 128) for s in range(5)], 7, [st.b])
                    P.act(lambda e, tok=tok: e.activation(out=qcT[:, :, tok], in_=psbf(7)[:, 0:512].rearrange("p (s t) -> p s t", s=4), func=AF.Copy), reads=[PB[7]], writes=[KB[tt]])
                    P.act(lambda e, tok=tok: e.activation(out=kcT[:, tok], in_=psbf(7)[:, 512:640], func=AF.Copy), reads=[PB[7]], writes=[KB[tt]])
                    P.act(lambda e, tt=tt: e.activation(out=vc[:, tt, :, 0:3:2, :], in_=flat[:, 640:768].rearrange("p (a e d) -> p a e d", a=1, e=2), func=AF.Copy), reads=[PB[1]], writes=[KB[tt]])
                    st2 = stg.next()
                    qd_src = flat[:, 768:1792].rearrange("p (h d) -> p h d", h=16)
                    qd_dst = st2.t[:].rearrange("p (h d) -> p h d", h=16)
                    if isctx:
                        P.act(lambda e, qd_dst=qd_dst, qd_src=qd_src: e.activation(out=qd_dst, in_=qd_src, func=AF.Copy), reads=[PB[1], PB[2], PB[3]], writes=[st2.b])
                    else:
                        rope_apply(qd_src, 16, cs64, qd_dst, (rt1, rt2), [PB[1], PB[2], PB[3]], [st2.b])
                    transposes([(st2.t[:, s * 128:(s + 1) * 128], 128) for s in range(8)], 7, [st2.b])
                    P.act(lambda e, tok=tok: e.activation(out=qdT[:, :, tok], in_=psbf(7)[:, 0:512].rearrange("p (s t) -> p s t", s=4), func=AF.Copy), reads=[PB[7]], writes=[KB[tt]])
                    P.act(lambda e, tok=tok: e.activation(out=kdT[:, :, tok], in_=psbf(7)[:, 512:1024].rearrange("p (s t) -> p s t", s=4), func=AF.Copy), reads=[PB[7]], writes=[KB[tt]])
                    P.act(lambda e, tt=tt: e.activation(out=vd[:, tt, :, 0:3:2, :], in_=flat[:, 1792:2304].rearrange("p (a e d) -> p a e d", a=4, e=2), func=AF.Copy), reads=[PB[3], PB[4]], writes=[KB[tt]])

            for tt in range(NT if DEBUG_STOP != "p1a" else DEBUG_NT):
                p1_tile(tt)
            if DEBUG_STOP in ("p1", "p1a"):
                return
            P.barrier()
            ar.reset(qkv_mark)
            wout = TB(ar.alloc([128, 8, D], BF16))
            wosrc = ev_w_out[li] if even else od_w_out[li]
            for c4 in range(4):
                P.dma(lambda e, c4=c4: e.dma_start(out=wout.t[:, 2 * c4:2 * c4 + 2, :], in_=wosrc[c4 * 256:(c4 + 1) * 256, :].rearrange("(c p) n -> p c n", p=128)),
                      writes=[wout.b], q="pool")
            gg1 = TB(ar.alloc([128, D], F32))
            ptr = Ring(ar, 3, [128, 2, 512], BF16)
            osbr = Ring(ar, 4, [128, 512], F32)
            oT = TB(ar.alloc([128, 8, 512], BF16))

            def oTa(hc, N):
                return oT.t[(hc % 2) * 64:(hc % 2) * 64 + 64, hc // 2, 0:N]
            xr2 = Ring(ar, 2, [128, D], F32)
            ytr = Ring(ar, 2, [128, D], F32)
            x1r = Ring(ar, 2, [128, D], F32)
            jk2 = Ring(ar, 1, [128, D], BF16)
            ss3 = Ring(ar, 2, [128, 4], F32)
            if not even:
                odr = Ring(ar, 8, [64, 512], F32)
                sqr = Ring(ar, 2, [64, 512], BF16)
                onesb = TB(ar.alloc([64, 64], BF16))
                P.pool(lambda e: e.memset(onesb.t[:], 1.0), writes=[onesb.b])
                rsd = Ring(ar, 3, [64, 512], F32)
            obank = [4]
            spair = [0]

            def next_obank():
                bk = obank[0]
                obank[0] = 4 + (bk - 4 + 1) % 4
                return bk

            def next_spair():
                s = spair[0]
                spair[0] = 2 - s
                return s

            def vop(vt, t, slot):
                p_, e_ = slot // 2, slot % 2
                return vt[:, t, p_, e_:e_ + 2, :].rearrange("p a d -> p (a d)")

            def normalise(bk, N, dst_ap, dst_bufs, sink_h=None, keep=None, e_=0):
                vs = slice(e_ * 64, e_ * 64 + 64)
                ds = slice((1 - e_) * 64, (1 - e_) * 64 + 64)
                r = osbr.next()
                if sink_h is not None:
                    P.dve(lambda e: e.tensor_scalar(out=r.t[ds, 0:N], in0=ps[ds, bk, 0:N], scalar1=sinkx.t[ds, sink_h:sink_h + 1], scalar2=None, op0=ALU.add),
                          reads=[PB[bk], sinkx.b], writes=[r.b])
                    P.dve(lambda e: e.reciprocal(out=r.t[ds, 0:N], in_=r.t[ds, 0:N]), reads=[r.b], writes=[r.b])
                else:
                    P.dve(lambda e: e.reciprocal(out=r.t[ds, 0:N], in_=ps[ds, bk, 0:N]), reads=[PB[bk]], writes=[r.b])
                if keep is None:
                    P.dve(lambda e: e.tensor_tensor(out=dst_ap, in0=ps[vs, bk, 0:N], in1=r.t[ds, 0:N], op=ALU.mult), reads=[r.b, PB[bk]], writes=list(dst_bufs))
                else:
                    P.dve(lambda e: e.tensor_tensor(out=keep.t[:, 0:N], in0=ps[vs, bk, 0:N], in1=r.t[ds, 0:N], op=ALU.mult), reads=[r.b, PB[bk]], writes=[keep.b])

            def attend_jobs(jobs, units, ktiles, N, scale, post, qB):
                if len(units) == 1:
                    u = units[0]
                    jl = [[(u, t)] for t in ktiles]
                    steps = [jl[i] + (jl[i + 1] if i + 1 < len(jl) else []) for i in range(0, len(jl), 2)]
                else:
                    steps = [[(units[0], t), (units[1], t)] for t in ktiles]
                banks = {}
                for u in units:
                    for vi in range(len(u["V"])):
                        banks[(id(u), vi)] = next_obank()
                first = {k: True for k in banks}
                last_step = len(steps) - 1
                single = len(units) == 1
                for si, step in enumerate(steps):
                    sp = next_spair()
                    pt = ptr.next()
                    nj = len(step)

                    def S(step=step, sp=sp):
                        for j, (u, t) in enumerate(step):
                            P.pe(lambda e, u=u, t=t, j=j: e.matmul(ps[:, sp + j, 0:N], lhsT=u["kT"](t), rhs=u["qT"], start=True, stop=True),
                                 reads=[KB[t]] + qB, writes=[PB[sp + j]])

                    def E(sp=sp, pt=pt, nj=nj):
                        P.act(lambda e: e.activation(out=pt.t[:, 0:nj, 0:N], in_=ps[:, sp:sp + nj, 0:N], func=AF.Exp, scale=scale),
                              reads=[PB[sp + j] for j in range(nj)], writes=[pt.b])

                    def V(step=step, pt=pt, si=si, nj=nj):
                        for j, (u, t) in enumerate(step):
                            for vi, vfn in enumerate(u["V"]):
                                key = (id(u), vi)
                                bk = banks[key]
                                lastc = (si == last_step and j == nj - 1) if single else (si == last_step)
                                P.pe(lambda e, j=j, t=t, vfn=vfn, bk=bk, st_=first[key], lastc=lastc: e.matmul(ps[:, bk, 0:N], lhsT=vfn(t), rhs=pt.t[:, j, 0:N], start=st_, stop=lastc),
                                     reads=[KB[t], VB1, pt.b], writes=[PB[bk]])
                                first[key] = False
                    jobs.append(dict(S=S, E=E, V=V, post=None, obanks=set(banks.values())))
                jobs[-1]["post"] = lambda: post(banks, sp)

            def run_jobs(jobs):
                n = len(jobs)
                pend = None
                later = []

                def do_post(p, i):
                    c = p()
                    if c is not None:
                        later.append((i + 3, c))
                if n:
                    jobs[0]["S"]()
                for i in range(n):
                    if i + 1 < n:
                        jobs[i + 1]["S"]()
                    jobs[i]["E"]()
                    jobs[i]["V"]()
                    for (due, c) in [x for x in later if x[0] <= i]:
                        c()
                    later[:] = [x for x in later if x[0] > i]
                    if pend is not None:
                        do_post(pend, i)
                        pend = None
                    p = jobs[i]["post"]
                    if p is not None:
                        if i + 1 < n and not (jobs[i]["obanks"] & jobs[i + 1]["obanks"]):
                            pend = p
                        else:
                            do_post(p, i)
                if pend is not None:
                    do_post(pend, n)
                for (due, c) in later:
                    c()

            groups = []
            if need_ctx:
                groups.append(("ctx", list(range(NC))))
            for g in range(NL // 4):
                groups.append(("lat", [NC + 4 * g + j for j in range(4)]))

            gg_loaded = [None]
            def p2_group(gkind, qtiles):
                if gg_loaded[0] != gkind:
                    r = NB if gkind == "ctx" else b
                    load_bc(gg1, modv[l, 2, r:r + 1, :], [modvB[l]])
                    gg_loaded[0] = gkind
                N = 128 * len(qtiles)
                q0 = qtiles[0] * 128
                qs = slice(q0, q0 + N)
                qB = [KB[t] for t in qtiles]
                ktiles = list(range(NC)) if gkind == "ctx" else list(range(NT))
                jobs = []
                if even:
                    def a_head(h):
                        s_, j_ = h % 4, h // 4
                        pr = slice(j_ * 64, (j_ + 1) * 64)
                        bk = next_obank()
                        jobs = []

                        def a_qtile(qi, qt):
                            lt = qt - NC
                            kl = [(t, None) for t in range(NC)]
                            if lt > 0:
                                kl.append((qt - 1, 0))
                            kl.append((qt, None))
                            if lt < NL - 1:
                                kl.append((qt + 1, 1))
                            qcols = slice(qt * 128, (qt + 1) * 128)
                            nk = len(kl)
                            st = {}

                            def S():
                                sp = st["sp"] = next_spair()
                                pt = st["pt"] = ptr.next()
                                st["ptv"] = pt.t[:].rearrange("p a (c n) -> p (a c) n", n=128)
                                st["psv"] = ps[:, sp:sp + 2, :].rearrange("p a (c n) -> p (a c) n", n=128)
                                for ci, (t, mk) in enumerate(kl):
                                    bnk, col = sp + ci // 4, (ci % 4) * 128
                                    P.pe(lambda e, t=t, bnk=bnk, col=col, mk=mk: e.matmul(
                                        ps[:, bnk, col:col + 128], lhsT=kaT[pr, t * 128:(t + 1) * 128], rhs=qaT[pr, s_, qcols], start=True, stop=(mk is None)),
                                        reads=[KB[t], KB[qt]], writes=[PB[bnk]])
                                    if mk is not None:
                                        P.pe(lambda e, bnk=bnk, col=col, mk=mk: e.matmul(ps[:, bnk, col:col + 128], lhsT=identb.t[:], rhs=maskb.t[:, mk, :], start=False, stop=True),
                                             reads=[identb.b, maskb.b], writes=[PB[bnk]])

                            def E():
                                sp, pt, ptv, psv = st["sp"], st["pt"], st["ptv"], st["psv"]
                                P.act(lambda e: e.activation(out=ptv[:, 0:nk, :], in_=psv[:, 0:nk, :], func=AF.Exp, scale=0.125),
                                      reads=[PB[sp], PB[sp + 1]], writes=[pt.b])

                            def V():
                                pt, ptv = st["pt"], st["ptv"]
                                for ci, (t, mk) in enumerate(kl):
                                    P.pe(lambda e, ci=ci, t=t: e.matmul(ps[:, bk, qi * 128:(qi + 1) * 128], lhsT=vop(va, t, j_), rhs=ptv[:, ci, :],
                                                                       start=(ci == 0), stop=(ci == nk - 1)),
                                         reads=[KB[t], VB1, pt.b], writes=[PB[bk]])
                            jobs.append(dict(S=S, E=E, V=V, post=None, obanks={bk}))
                        for qi, qt in enumerate(qtiles):
                            a_qtile(qi, qt)
                        jobs[-1]["post"] = lambda: normalise(bk, N, oTa(h, N), [oT.b], sink_h=h, e_=j_)
                        return jobs

                    def a_ctx_pair(s_):
                        us = []
                        for j_ in range(2):
                            pr = slice(j_ * 64, (j_ + 1) * 64)
                            us.append(dict(kT=lambda t, pr=pr: kaT[pr, t * 128:(t + 1) * 128], qT=qaT[pr, s_, qs], V=[lambda t, j_=j_: vop(va, t, j_)]))

                        def post(banks, sp_):
                            for j_ in range(2):
                                normalise(banks[(id(us[j_]), 0)], N, oTa(s_ + 4 * j_, N), [oT.b], sink_h=s_ + 4 * j_, e_=j_)
                        attend_jobs(jobs, us, ktiles, N, 0.125, post, qB)
                    for s_ in range(4):
                        if gkind == "ctx":
                            a_ctx_pair(s_)
                        else:
                            ja, jb = a_head(s_), a_head(s_ + 4)
                            for x_, y_ in zip(ja, jb):
                                jobs.append(x_)
                                jobs.append(y_)

                    def b_head(h):
                        u = dict(kT=lambda t: kbT[0:96, h, t * 128:(t + 1) * 128], qT=qbT[0:96, h, qs], V=[lambda t: vop(vb, t, h)])
                        attend_jobs(jobs, [u], ktiles, N, 96 ** -0.5, lambda banks, sp_: normalise(banks[(id(u), 0)], N, oTa(8 + h, N), [oT.b], e_=h % 2), qB)
                    for h in range(8):
                        b_head(h)
                else:
                    def c_pair(s_):
                        us = []
                        for j_ in range(2):
                            pr = slice(j_ * 64, (j_ + 1) * 64)
                            us.append(dict(kT=lambda t, pr=pr: kcT[pr, t * 128:(t + 1) * 128], qT=qcT[pr, s_, qs], V=[lambda t, j_=j_: vop(vc, t, j_)]))

                        def post(banks, sp_):
                            for j_ in range(2):
                                normalise(banks[(id(us[j_]), 0)], N, oTa(s_ + 4 * j_, N), [oT.b], e_=j_)
                        attend_jobs(jobs, us, ktiles, N, 0.125, post, qB)
                    for s_ in range(4):
                        c_pair(s_)

                    def d_head(h):
                        us = []
                        for i in range(2):
                            pr = slice(i * 64, (i + 1) * 64)
                            us.append(dict(kT=lambda t, pr=pr: kdT[pr, h, t * 128:(t + 1) * 128], qT=qdT[pr, h, qs],
                                           V=[lambda t, hf=hf: vop(vd, t, 2 * h + hf) for hf in range(2)]))

                        def d_post(banks, sp):
                            ods = []
                            rr = []
                            for i in range(2):
                                r = osbr.next()
                                bk0 = banks[(id(us[i]), 0)]
                                if i == 0:
                                    P.dve(lambda e, r=r, bk0=bk0: e.reciprocal(out=r.t[64:128, 0:N], in_=ps[64:128, bk0, 0:N]), reads=[PB[bk0]], writes=[r.b])
                                else:
                                    P.act(lambda e, r=r, bk0=bk0: e.activation(out=r.t[64:128, 0:N], in_=ps[64:128, bk0, 0:N], func=AF.Ln), reads=[PB[bk0]], writes=[r.b])
                                    P.act(lambda e, r=r: e.activation(out=r.t[64:128, 0:N], in_=r.t[64:128, 0:N], func=AF.Exp, scale=-1.0), reads=[r.b], writes=[r.b])
                                rr.append(r)
                            oo = [[odr.next(), odr.next()] for _ in range(2)]
                            for hf in range(2):
                                vs = slice(hf * 64, hf * 64 + 64)
                                for i in range(2):
                                    bk_ = banks[(id(us[i]), hf)]
                                    P.dve(lambda e, o=oo[hf][i], r=rr[i], bk_=bk_, vs=vs: e.tensor_tensor(out=o.t[:, 0:N], in0=ps[vs, bk_, 0:N], in1=r.t[64:128, 0:N], op=ALU.mult),
                                          reads=[rr[i].b, PB[bk_]], writes=[oo[hf][i].b])
                            for hf in range(2):
                                o1, o2 = oo[hf]
                                P.dve(lambda e, o1=o1, o2=o2: e.scalar_tensor_tensor(out=o1.t[:, 0:N], in0=o2.t[:, 0:N], scalar=lams.t[0:64, 3:4], in1=o1.t[:, 0:N], op0=ALU.mult, op1=ALU.add),
                                      reads=[o1.b, o2.b, lams.b], writes=[o1.b])
                                ods.append(o1)
                            for hf in range(2):
                                sq = sqr.next()
                                P.pool(lambda e, sq=sq, o=ods[hf]: e.tensor_tensor(out=sq.t[:, 0:N], in0=o.t[:, 0:N], in1=o.t[:, 0:N], op=ALU.mult), reads=[ods[hf].b], writes=[sq.b])
                                P.pe(lambda e, sq=sq, hf=hf: e.matmul(ps[0:64, sp, 0:N], lhsT=onesb.t[:], rhs=sq.t[:, 0:N], start=(hf == 0), stop=(hf == 1)),
                                     reads=[onesb.b, sq.b], writes=[PB[sp]])
                            rs = rsd.next()
                            P.dve(lambda e: e.tensor_copy(out=rs.t[:, 0:N], in_=ps[0:64, sp, 0:N]), reads=[PB[sp]], writes=[rs.b])

                            def stage2():
                                P.act(lambda e: e.activation(out=rs.t[:, 0:N], in_=rs.t[:, 0:N], func=AF.Ln, scale=1.0 / 128, bias=epsb.t[0:64, :]), reads=[rs.b, epsb.b], writes=[rs.b])
                                P.act(lambda e: e.activation(out=rs.t[:, 0:N], in_=rs.t[:, 0:N], func=AF.Exp, scale=-0.5), reads=[rs.b], writes=[rs.b])
                                for hf in range(2):
                                    P.dve(lambda e, hf=hf, o=ods[hf]: e.scalar_tensor_tensor(out=oTa(8 + 2 * h + hf, N), in0=o.t[:, 0:N], scalar=subs.t[:, hf:hf + 1], in1=rs.t[:, 0:N],
                                                                                          op0=ALU.mult, op1=ALU.mult),
                                          reads=[ods[hf].b, subs.b, rs.b], writes=[oT.b])
                            return stage2
                        attend_jobs(jobs, us, ktiles, N, 0.125, d_post, qB)
                    for h in range(4):
                        d_head(h)
                run_jobs(jobs)

                def outproj(qi, qt):
                    sp = next_spair()
                    for hfc in range(2):
                        for hc in range(8):
                            P.pe(lambda e, hc=hc, hfc=hfc, sp=sp, qi=qi: e.matmul(ps[:, sp + hfc, :], lhsT=oT.t[:, hc, qi * 128:(qi + 1) * 128], rhs=wout.t[:, hc, hfc * 512:(hfc + 1) * 512],
                                                                            start=(hc == 0), stop=(hc == 7)),
                                 reads=[oT.b, wout.b], writes=[PB[sp + hfc]])
                    jk, ss = jk2.next(), ss3.next()
                    P.act(lambda e, jk=jk, ss=ss, sp=sp: e.activation(out=jk.t[:], in_=ps2(sp), func=AF.Square, accum_out=ss.t[:, 0:1]), reads=[PB[sp], PB[sp + 1]], writes=[jk.b, ss.b])
                    rstd_from_ss(ss, D)
                    xt = xr2.next()
                    src, srcb = x_src(l, b, qt)
                    P.dma(lambda e, xt=xt, src=src: e.dma_start(out=xt.t[:], in_=src), reads=srcb, writes=[xt.b])
                    yt = ytr.next()
                    ggs = gg1
                    P.dve(lambda e, yt=yt, ss=ss, sp=sp, ggs=ggs: e.scalar_tensor_tensor(out=yt.t[:], in0=ps2(sp), scalar=ss.t[:, 0:1], in1=ggs.t[:], op0=ALU.mult, op1=ALU.mult),
                          reads=[PB[sp], PB[sp + 1], ss.b, ggs.b], writes=[yt.b])
                    x1 = x1r.next()
                    P.pool(lambda e, x1=x1, xt=xt, yt=yt: e.tensor_tensor(out=x1.t[:], in0=xt.t[:], in1=yt.t[:], op=ALU.add), reads=[xt.b, yt.b], writes=[x1.b])
                    P.dma(lambda e, x1=x1, qt=qt: e.dma_start(out=x1s[b, qt * 128:(qt + 1) * 128, :], in_=x1.t[:]), reads=[x1.b], writes=[x1B[b][qt]])

                for qi, qt in enumerate(qtiles):
                    outproj(qi, qt)
            for gkind, qtiles in groups:
                p2_group(gkind, qtiles)
        for b in range(NB if DEBUG_STOP != "p1a" else 1):
            do_seq(b)
        if DEBUG_STOP in ("p1", "p1a", "p2"):
            return
        P.barrier()
        ar.reset(const_mark)
        fwin = TB(ar.alloc([128, 8, 2 * FH], BF16))
        fwout = TB(ar.alloc([128, 22, D], BF16))
        fwinB = [Buf() for _ in range(11)]
        fwoutB = [Buf() for _ in range(11)]
        for c2 in range(11):
            for gu in range(2):
                col = gu * FH + c2 * 256
                P.dma(lambda e, col=col: e.dma_start(out=fwin.t[:, :, col:col + 256], in_=ffn_w_in[l, :, col:col + 256].rearrange("(k p) n -> p k n", p=128)),
                      writes=[fwinB[c2]], q="pool")
        for c2 in range(11):
            P.dma(lambda e, c2=c2: e.dma_start(out=fwout.t[:, 2 * c2:2 * c2 + 2, :], in_=ffn_w_out[l, c2 * 256:(c2 + 1) * 256, :].rearrange("(c p) n -> p c n", p=128)),
                  writes=[fwoutB[c2]], q="pool")
        mod3 = [TB(ar.alloc([128, D], F32)) for _ in range(3)]
        xr3 = Ring(ar, 2, [128, D], F32)
        rings3 = (Ring(ar, 1, [128, D], BF16), Ring(ar, 2, [128, 4], F32), Ring(ar, 1, [128, D], F32))
        hbr4 = Ring(ar, 4, [128, D], BF16)
        h2T = TB(ar.alloc([128, 8, 512], BF16))
        actT = TB(ar.alloc([128, 22, 512], BF16))
        sgr = Ring(ar, 2, [128, 512], F32)
        ytr3 = rings3[2]
        spair3 = [0]
        opair3 = [4]
        def p3_seq(b):
            groups = []
            if need_ctx:
                groups.append(list(range(NC)))
            for g in range(NL // 4):
                groups.append([NC + 4 * g + j for j in range(4)])
            m3_loaded = [None]
            pre_done = [None]

            def preA(tiles):
                hbs = []
                for tt in tiles:
                    xt = xr3.next()
                    P.dma(lambda e, xt=xt, tt=tt: e.dma_start(out=xt.t[:], in_=x1s[b, tt * 128:(tt + 1) * 128, :]), reads=[x1B[b][tt]], writes=[xt.b])
                    jk, ss, tmp, hb = rings3[0].next(), rings3[1].next(), rings3[2].next(), hbr4.next()
                    P.act(lambda e, jk=jk, ss=ss, xt=xt: e.activation(out=jk.t[:], in_=xt.t[:], func=AF.Square, accum_out=ss.t[:, 0:1]), reads=[xt.b], writes=[jk.b, ss.b])
                    rstd_from_ss(ss, D)
                    P.pool(lambda e, tmp=tmp, xt=xt: e.tensor_tensor(out=tmp.t[:], in0=xt.t[:], in1=mod3[0].t[:], op=ALU.mult), reads=[xt.b, mod3[0].b], writes=[tmp.b])
                    P.dve(lambda e, hb=hb, tmp=tmp, ss=ss: e.scalar_tensor_tensor(out=hb.t[:], in0=tmp.t[:], scalar=ss.t[:, 0:1], in1=mod3[1].t[:], op0=ALU.mult, op1=ALU.add),
                          reads=[tmp.b, ss.b, mod3[1].b], writes=[hb.b])
                    hbs.append(hb)
                return hbs

            def preB(hbs):
                for ti, hb in enumerate(hbs):
                    transposes([(hb.t[:, k * 128:(k + 1) * 128], 128) for k in range(8)], 7, [hb.b])
                    P.act(lambda e, ti=ti: e.activation(out=h2T.t[:, :, ti * 128:(ti + 1) * 128], in_=psbf(7).rearrange("p (k t) -> p k t", k=8), func=AF.Copy), reads=[PB[7]], writes=[h2T.b])

            def kind_of(tiles):
                return "ctx" if tiles[0] < NC else "lat"

            def p3_group(gi):
                tiles = groups[gi]
                N = 128 * len(tiles)
                kind = kind_of(tiles)
                if m3_loaded[0] != kind:
                    r = NB if kind == "ctx" else b
                    for j in range(3):
                        load_bc(mod3[j], modv[l, 3 + j, r:r + 1, :], [modvB[l]])
                    m3_loaded[0] = kind
                if pre_done[0] != gi:
                    preB(preA(tiles))
                nxt = groups[gi + 1] if gi + 1 < len(groups) and kind_of(groups[gi + 1]) == kind else None
                def p3_chunk(c):
                    sp = spair3[0]
                    spair3[0] = 2 - sp
                    for gu in range(2):
                        col = gu * FH + c * 128
                        for k in range(8):
                            P.pe(lambda e, gu=gu, col=col, k=k, sp=sp: e.matmul(ps[:, sp + gu, 0:N], lhsT=fwin.t[:, k, col:col + 128], rhs=h2T.t[:, k, 0:N], start=(k == 0), stop=(k == 7)),
                                 reads=[fwinB[c // 2], h2T.b], writes=[PB[sp + gu]])
                    sg = sgr.next()
                    P.act(lambda e, sg=sg, sp=sp: e.activation(out=sg.t[:, 0:N], in_=ps[:, sp, 0:N], func=AF.Silu), reads=[PB[sp]], writes=[sg.b])
                    P.dve(lambda e, sg=sg, sp=sp, c=c: e.tensor_tensor(out=actT.t[:, c, 0:N], in0=sg.t[:, 0:N], in1=ps[:, sp + 1, 0:N], op=ALU.mult), reads=[sg.b, PB[sp + 1]], writes=[actT.b])
                for c in range(22):
                    p3_chunk(c)
                def p3_epi(ti, tt):
                    op_ = opair3[0]
                    opair3[0] = 10 - op_
                    for hfc in range(2):
                        for c in range(22):
                            P.pe(lambda e, c=c, hfc=hfc, op_=op_, ti=ti: e.matmul(ps[:, op_ + hfc, :], lhsT=actT.t[:, c, ti * 128:(ti + 1) * 128], rhs=fwout.t[:, c, hfc * 512:(hfc + 1) * 512],
                                                                            start=(c == 0), stop=(c == 21)),
                                 reads=[actT.b, fwoutB[c // 2]], writes=[PB[op_ + hfc]])
                    jk, ss = rings3[0].next(), rings3[1].next()
                    P.act(lambda e, jk=jk, ss=ss, op_=op_: e.activation(out=jk.t[:], in_=ps2(op_), func=AF.Square, accum_out=ss.t[:, 0:1]), reads=[PB[op_], PB[op_ + 1]], writes=[jk.b, ss.b])
                    rstd_from_ss(ss, D)
                    xt = xr3.next()
                    P.dma(lambda e, xt=xt, tt=tt: e.dma_start(out=xt.t[:], in_=x1s[b, tt * 128:(tt + 1) * 128, :]), reads=[x1B[b][tt]], writes=[xt.b])
                    yt = ytr3.next()
                    ggs = mod3[2]
                    P.dve(lambda e, yt=yt, ss=ss, op_=op_, ggs=ggs: e.scalar_tensor_tensor(out=yt.t[:], in0=ps2(op_), scalar=ss.t[:, 0:1], in1=ggs.t[:], op0=ALU.mult, op1=ALU.mult),
                          reads=[PB[op_], PB[op_ + 1], ss.b, ggs.b], writes=[yt.b])
                    x2 = xt
                    P.pool(lambda e, xt=xt, yt=yt: e.tensor_tensor(out=xt.t[:], in0=xt.t[:], in1=yt.t[:], op=ALU.add), reads=[xt.b, yt.b], writes=[xt.b])
                    if l == L - 1:
                        P.dma(lambda e, x2=x2, tt=tt: e.dma_start(out=out[b, (tt - NC) * 128:(tt - NC + 1) * 128, :], in_=x2.t[:]), reads=[x2.b])
                    else:
                        P.dma(lambda e, x2=x2, tt=tt: e.dma_start(out=xres[b, tt * 128:(tt + 1) * 128, :], in_=x2.t[:]), reads=[x2.b], writes=[xresB[b][tt]])
                nhb = preA(nxt) if nxt is not None else None
                for ti, tt in enumerate(tiles):
                    p3_epi(ti, tt)
                if nxt is not None:
                    preB(nhb)
                    pre_done[0] = gi + 1
            for gi in range(len(groups)):
                p3_group(gi)
        for b in range(NB):
            p3_seq(b)
        P.barrier()
        ar.reset(layer_mark)

    for l in range(L):
        do_layer(l)
    P.emit()
    return nc


def rope_tables(S, grid_w=64, theta=10000.0):
    t = np.arange(S)
    rows, cols = (t // grid_w).astype(np.float32), (t % grid_w).astype(np.float32)

    def tab(dh):
        half = dh // 4
        freqs = (theta ** (-np.arange(half, dtype=np.float32) / half)).astype(np.float32)
        ar_, ac_ = rows[:, None] * freqs[None, :], cols[:, None] * freqs[None, :]
        cr, sr, cc, sc = np.cos(ar_), np.sin(ar_), np.cos(ac_), np.sin(ac_)
        cos = np.concatenate([cr, cr, cc, cc], 1)
        sin = np.concatenate([-sr, sr, -sc, sc], 1)
        return np.concatenate([cos, sin], 1).astype(np.float32)
    return tab(64), tab(32)


def const_inputs(S):
    r64, r32 = rope_tables(S)
    k = np.arange(128)[:, None]
    q = np.arange(128)[None, :]
    neg = np.float32(-30000.0)
    mprev = np.where(k >= q, np.float32(0), neg)
    mnext = np.where(k <= q, np.float32(0), neg)
    return {"rope64": r64, "rope32": r32, "ident": np.eye(128, dtype=np.float32),
            "masks": np.stack([mprev, mnext]).astype(np.float32)}


_W_KEYS = ("ada_w", "ada_b", "norm_g", "ffn_w_in", "ffn_w_out", "ev_w_in", "ev_sink", "ev_q_norm", "ev_w_uq",
           "ev_kv_norm", "ev_w_ukv", "ev_w_out", "od_w_in", "od_qk_norm", "od_lambda", "od_subln", "od_w_out")


def run(inputs, n_cores, nb):
    x = np.asarray(inputs["x"], np.float32)
    B, S, _ = x.shape
    ctx = np.asarray(inputs["ctx"], np.float32)
    C = ctx.shape[1]
    L = inputs["ada_w"].shape[0]
    c = np.asarray(inputs["c"], np.float32)
    c_ctx = np.asarray(inputs["c_ctx"], np.float32)
    nc = build_program(S, C, L, nb)
    consts = const_inputs(S)
    shared = {k: np.ascontiguousarray(np.asarray(inputs[k], np.float32)) for k in _W_KEYS}
    in_maps = []
    for i in range(n_cores):
        sl = slice(i * nb, (i + 1) * nb)
        m = {"x": np.ascontiguousarray(x[sl]), "ctx": np.ascontiguousarray(ctx[sl]),
             "cvec": np.ascontiguousarray(np.concatenate([c[sl], c_ctx[None, :]], 0))}
        m.update(shared)
        m.update(consts)
        in_maps.append(m)
    res = run_bass_kernel_spmd(nc, in_maps, core_ids=list(range(n_cores)))
    return np.concatenate([r["out"] for r in res.results], axis=0)


def kernel(**inputs):
    return run(inputs, 8, 2)
```

```python
import contextlib
import math
import numpy as np
import concourse.bass as bass
import concourse.mybir as mybir
from concourse.bass_utils import run_bass_kernel_spmd

F32 = mybir.dt.float32
BF16 = mybir.dt.bfloat16
AF = mybir.ActivationFunctionType
ALU = mybir.AluOpType
AX = mybir.AxisListType

ENGS = ("pe", "act", "dve", "pool", "sp")


class Buf:
    __slots__ = ("name", "w", "r")

    def __init__(self, name=""):
        self.name = name
        self.w = None
        self.r = []


class Op:
    __slots__ = ("eng", "fn", "deps", "sig", "idx", "dma", "dsem", "dval")

    def __init__(self, eng, fn, dma):
        self.eng = eng
        self.fn = fn
        self.deps = []
        self.sig = False
        self.idx = 0
        self.dma = dma
        self.dsem = None
        self.dval = 0


class Prog:
    NDSEM = 48
    NHW = 32

    def __init__(self, nc):
        self.nc = nc
        self.ops = []
        self.ndma = 0
        self.nsw = 0
        self.last = {}
        self.dmas_since = []

    def add(self, eng, fn, reads=(), writes=(), dma=False):
        op = Op(eng, fn, dma)
        deps = {}
        for b in reads:
            if b.w is not None:
                deps[id(b.w)] = b.w
        for b in writes:
            if b.w is not None:
                deps[id(b.w)] = b.w
            for r in b.r:
                deps[id(r)] = r
        for d in deps.values():
            if d is op:
                continue
            if d.eng == "pe" and eng == "pe" and not d.dma and not dma:
                continue
            op.deps.append(d)
            if not d.dma:
                d.sig = True
        for b in writes:
            b.w = op
            b.r = []
        for b in reads:
            if b.w is not op:
                b.r.append(op)
        if dma:
            if eng == "pool":
                op.dsem = self.NHW + self.nsw % (self.NDSEM - self.NHW)
                self.nsw += 1
            else:
                op.dsem = self.ndma % self.NHW
                self.ndma += 1
            self.dmas_since.append(op)
        else:
            self.last[eng] = op
        self.ops.append(op)
        return op

    def pe(self, fn, reads=(), writes=()):
        return self.add("pe", fn, reads, writes)

    def act(self, fn, reads=(), writes=()):
        return self.add("act", fn, reads, writes)

    def dve(self, fn, reads=(), writes=()):
        return self.add("dve", fn, reads, writes)

    def pool(self, fn, reads=(), writes=()):
        return self.add("pool", fn, reads, writes)

    def dma(self, fn, reads=(), writes=(), q="sp"):
        return self.add(q, fn, reads, writes, dma=True)

    def barrier(self):
        lasts = dict(self.last)
        dm = {}
        for d in self.dmas_since:
            dm[d.dsem] = d
        self.dmas_since = []
        for e in ENGS:
            op = Op(e, lambda eng: eng.nop(), False)
            for e2, lo in lasts.items():
                if e2 != e:
                    op.deps.append(lo)
                    lo.sig = True
            op.deps.extend(dm.values())
            self.ops.append(op)
            self.last[e] = op

    def emit(self):
        nc = self.nc
        cnt = {e: 0 for e in ENGS}
        dcount = [0] * self.NDSEM
        for op in self.ops:
            if op.dma:
                dcount[op.dsem] += 16
                op.dval = dcount[op.dsem]
            elif op.sig:
                cnt[op.eng] += 1
                op.idx = cnt[op.eng]
        per = {e: [op for op in self.ops if op.eng == e] for e in ENGS}
        with contextlib.ExitStack() as st:
            esem = {e: st.enter_context(nc.semaphore("s_" + e)) for e in ENGS}
            dsem = [st.enter_context(nc.semaphore("d%d" % i)) for i in range(self.NDSEM)]
            block = st.enter_context(nc.Block())

            def run(ename, eng):
                waited = {}
                for op in per[ename]:
                    for d in op.deps:
                        if d.dma:
                            key, val, sem = ("d", d.dsem), d.dval, dsem[d.dsem]
                        else:
                            key, val, sem = ("e", d.eng), d.idx, esem[d.eng]
                        if waited.get(key, 0) < val:
                            eng.wait_ge(sem, val)
                            waited[key] = val
                    if op.dma:
                        key = ("d", op.dsem)
                        if op.dval > 16 and waited.get(key, 0) < op.dval - 16:
                            eng.wait_ge(dsem[op.dsem], op.dval - 16)
                            waited[key] = op.dval - 16
                        op.fn(eng).then_inc(dsem[op.dsem], 16)
                    else:
                        ins = op.fn(eng)
                        if op.sig:
                            ins.then_inc(esem[ename], 1)
                if ename == "sp":
                    for i in range(self.NDSEM):
                        if dcount[i] > 0 and waited.get(("d", i), 0) < dcount[i]:
                            eng.wait_ge(dsem[i], dcount[i])

            @block.tensor
            def _(eng):
                run("pe", eng)

            @block.scalar
            def _(eng):
                run("act", eng)

            @block.vector
            def _(eng):
                run("dve", eng)

            @block.gpsimd
            def _(eng):
                run("pool", eng)

            @block.sync
            def _(eng):
                run("sp", eng)


class Arena:
    def __init__(self, nc, base, limit):
        self.nc, self.off, self.limit, self.n = nc, base, limit, 0

    def alloc(self, shape, dtype):
        per = 1
        for s in shape[1:]:
            per *= s
        nbytes = per * (4 if dtype == F32 else 2)
        off = (self.off + 31) // 32 * 32
        assert off + nbytes <= self.limit, ("SBUF arena overflow", off, nbytes, self.limit)
        self.off = off + nbytes
        self.n += 1
        return self.nc.alloc_sbuf_tensor_at("t%d" % self.n, list(shape), dtype, offset=off)

    def mark(self):
        return self.off

    def reset(self, m):
        self.off = m


class TB:
    __slots__ = ("t", "b")

    def __init__(self, t):
        self.t = t
        self.b = Buf()


class Ring:
    def __init__(self, arena, n, shape, dtype):
        self.items = [TB(arena.alloc(shape, dtype)) for _ in range(n)]
        self.i = 0

    def next(self):
        it = self.items[self.i % len(self.items)]
        self.i += 1
        return it


DEBUG_STOP = None
DEBUG_ROPE = None
DEBUG_NT = 1
D = 1024
FH = 2816
EPS = 1e-6
EV_IN = 1184
OD_IN = 2304


def lam_init_of(layer):
    return 0.8 - 0.6 * math.exp(-0.3 * layer)


def build_program(S, C, L, NB):
    NL, NC = S // 128, C // 128
    NT = NL + NC
    T = S + C
    NE, NO = (L + 1) // 2, L // 2
    nc = bass.Bass("TRN2", target_bir_lowering=False)

    def din(name, shape):
        return nc.dram_tensor(name, list(shape), F32, kind="ExternalInput").ap()

    x_in = din("x", [NB, S, D])
    ctx_in = din("ctx", [NB, C, D])
    cvec = din("cvec", [NB + 1, D])
    ada_w = din("ada_w", [L, D, 6 * D])
    ada_b = din("ada_b", [L, 6 * D])
    norm_g = din("norm_g", [L, 4, D])
    ffn_w_in = din("ffn_w_in", [L, D, 2 * FH])
    ffn_w_out = din("ffn_w_out", [L, FH, D])
    ev_w_in = din("ev_w_in", [NE, D, EV_IN])
    ev_sink = din("ev_sink", [NE, 8])
    ev_q_norm = din("ev_q_norm", [NE, 256])
    ev_w_uq = din("ev_w_uq", [NE, 256, 768])
    ev_kv_norm = din("ev_kv_norm", [NE, 128])
    ev_w_ukv = din("ev_w_ukv", [NE, 128, 1024])
    ev_w_out = din("ev_w_out", [NE, 1024, D])
    od_w_in = din("od_w_in", [max(NO, 1), D, OD_IN])
    od_qk_norm = din("od_qk_norm", [max(NO, 1), 2, 64])
    od_lambda = din("od_lambda", [max(NO, 1), 4, 64])
    od_subln = din("od_subln", [max(NO, 1), 128])
    od_w_out = din("od_w_out", [max(NO, 1), 1024, D])
    rope64 = din("rope64", [S, 128])
    rope32 = din("rope32", [S, 64])
    ident_in = din("ident", [128, 128])
    masks_in = din("masks", [2, 128, 128])
    out = nc.dram_tensor("out", [NB, S, D], F32, kind="ExternalOutput").ap()
    xres = nc.dram_tensor("xres", [NB, T, D], F32, kind="Internal").ap()
    x1s = nc.dram_tensor("x1s", [NB, T, D], F32, kind="Internal").ap()
    modv = nc.dram_tensor("modv", [L, 6, NB + 1, D], F32, kind="Internal").ap()

    P = Prog(nc)
    ar = Arena(nc, 16640, 229376 - 256)
    ps = nc.alloc_psum_tensor("ps", [128, 8, 512], F32)
    PB = [Buf("ps%d" % i) for i in range(8)]
    xresB = [[Buf() for _ in range(NT)] for _ in range(NB)]
    x1B = [[Buf() for _ in range(NT)] for _ in range(NB)]
    modvB = [Buf() for _ in range(L)]

    def psb(b):
        return ps[:, b, :]

    def psbf(b):
        return ps[:, b, :].bitcast(BF16)

    def ps2(b):
        return ps[:, b:b + 2, :].rearrange("p a b -> p (a b)")

    identb = TB(ar.alloc([128, 128], BF16))
    maskb = TB(ar.alloc([128, 2, 128], BF16))
    onesf = TB(ar.alloc([128, 64], F32))
    epsb = TB(ar.alloc([128, 1], F32))
    oneb = TB(ar.alloc([128, 1], F32))
    P.dma(lambda e: e.dma_start(out=identb.t[:], in_=ident_in[:, :]), writes=[identb.b], q="pool")
    P.dma(lambda e: e.dma_start(out=maskb.t[:], in_=masks_in.rearrange("m k q -> k m q")), writes=[maskb.b], q="pool")
    P.pool(lambda e: e.memset(onesf.t[:], 1.0), writes=[onesf.b])
    P.pool(lambda e: e.memset(epsb.t[:], EPS), writes=[epsb.b])
    P.pool(lambda e: e.memset(oneb.t[:], 1.0), writes=[oneb.b])
    base_mark = ar.mark()

    def rstd_from_ss(ss, n, col=slice(0, 1)):
        P.act(lambda e: e.activation(out=ss.t[:, col], in_=ss.t[:, col], func=AF.Ln, scale=1.0 / n, bias=epsb.t[:]),
              reads=[ss.b, epsb.b], writes=[ss.b])
        P.act(lambda e: e.activation(out=ss.t[:, col], in_=ss.t[:, col], func=AF.Exp, scale=-0.5),
              reads=[ss.b], writes=[ss.b])

    def phase0():
        cT = TB(ar.alloc([128, 8, NB + 1], F32))
        t1 = TB(ar.alloc([128, 8, NB + 1], F32))
        scT = TB(ar.alloc([128, 8, NB + 1], F32))
        R = NB + 1
        for r in range(R):
            P.dma(lambda e, r=r: e.dma_start(out=cT.t[:, :, r], in_=cvec[r, :].rearrange("(k p) -> p k", p=128), allow_slow_non_contiguous=True),
                  writes=[cT.b])
        P.act(lambda e: e.activation(out=t1.t[:], in_=cT.t[:], func=AF.Exp, scale=-1.0), reads=[cT.b], writes=[t1.b])
        P.act(lambda e: e.activation(out=t1.t[:], in_=t1.t[:], func=AF.Ln, bias=oneb.t[:]), reads=[t1.b, oneb.b], writes=[t1.b])
        P.act(lambda e: e.activation(out=t1.t[:], in_=t1.t[:], func=AF.Exp, scale=-1.0), reads=[t1.b], writes=[t1.b])
        P.dve(lambda e: e.tensor_tensor(out=scT.t[:], in0=cT.t[:], in1=t1.t[:], op=ALU.mult), reads=[cT.b, t1.b], writes=[scT.b])
        awr = Ring(ar, 2, [128, 8, 512], F32)
        modrow = TB(ar.alloc([R, 6 * D], F32))
        abt = TB(ar.alloc([R, 6 * D], F32))
        ngt = TB(ar.alloc([R, 4 * D], F32))
        mv = TB(ar.alloc([R, 6, D], F32))
        for l in range(L):
            for n in range(12):
                aw = awr.next()
                P.dma(lambda e, aw=aw, l=l, n=n: e.dma_start(out=aw.t[:], in_=ada_w[l, :, n * 512:(n + 1) * 512].rearrange("(k p) n -> p k n", p=128)),
                      writes=[aw.b])
                bk = n % 2
                for k in range(8):
                    P.pe(lambda e, aw=aw, k=k, bk=bk: e.matmul(ps[0:R, bk, :], lhsT=scT.t[:, k, :], rhs=aw.t[:, k, :], start=(k == 0), stop=(k == 7)),
                         reads=[scT.b, aw.b], writes=[PB[bk]])
                P.act(lambda e, n=n, bk=bk: e.activation(out=modrow.t[:, n * 512:(n + 1) * 512], in_=ps[0:R, bk, :], func=AF.Copy),
                      reads=[PB[bk]], writes=[modrow.b])
            P.dma(lambda e, l=l: e.dma_start(out=abt.t[:], in_=ada_b[l:l + 1, :].partition_broadcast(R)), writes=[abt.b])
            P.dma(lambda e, l=l: e.dma_start(out=ngt.t[:], in_=norm_g[l:l + 1, :, :].rearrange("o j d -> o (j d)").partition_broadcast(R)), writes=[ngt.b])
            P.dve(lambda e: e.tensor_tensor(out=modrow.t[:], in0=modrow.t[:], in1=abt.t[:], op=ALU.add), reads=[modrow.b, abt.b], writes=[modrow.b])

            def seg(i):
                return modrow.t[:, i * D:(i + 1) * D]

            def ng(i):
                return ngt.t[:, i * D:(i + 1) * D]
            P.dve(lambda e: e.scalar_tensor_tensor(out=mv.t[:, 0, :], in0=seg(1), scalar=1.0, in1=ng(0), op0=ALU.add, op1=ALU.mult), reads=[modrow.b, ngt.b], writes=[mv.b])
            P.dve(lambda e: e.tensor_copy(out=mv.t[:, 1, :], in_=seg(0)), reads=[modrow.b], writes=[mv.b])
            P.dve(lambda e: e.tensor_tensor(out=mv.t[:, 2, :], in0=seg(2), in1=ng(1), op=ALU.mult), reads=[modrow.b, ngt.b], writes=[mv.b])
            P.dve(lambda e: e.scalar_tensor_tensor(out=mv.t[:, 3, :], in0=seg(4), scalar=1.0, in1=ng(2), op0=ALU.add, op1=ALU.mult), reads=[modrow.b, ngt.b], writes=[mv.b])
            P.dve(lambda e: e.tensor_copy(out=mv.t[:, 4, :], in_=seg(3)), reads=[modrow.b], writes=[mv.b])
            P.dve(lambda e: e.tensor_tensor(out=mv.t[:, 5, :], in0=seg(5), in1=ng(3), op=ALU.mult), reads=[modrow.b, ngt.b], writes=[mv.b])
            P.dma(lambda e, l=l: e.dma_start(out=modv[l].rearrange("j r d -> r j d"), in_=mv.t[:]), reads=[mv.b], writes=[modvB[l]])

    phase0()
    P.barrier()
    ar.reset(base_mark)
    if DEBUG_STOP == "p0":
        P.emit()
        return nc

    def x_src(l, b, tt):
        if l == 0:
            if tt < NC:
                return ctx_in[b, tt * 128:(tt + 1) * 128, :], []
            return x_in[b, (tt - NC) * 128:(tt - NC + 1) * 128, :], []
        return xres[b, tt * 128:(tt + 1) * 128, :], [xresB[b][tt]]

    def load_bc(dst, src_row, reads=()):
        P.dma(lambda e: e.dma_start(out=dst.t[:], in_=src_row.partition_broadcast(128)), reads=list(reads), writes=[dst.b])

    def transposes(srcs, bank, src_bufs):
        for i, (ap, m) in enumerate(srcs):
            P.pe(lambda e, ap=ap, m=m, i=i: e.transpose(out=psbf(bank)[0:m, i * 128:(i + 1) * 128], in_=ap, identity=identb.t[:]),
                 reads=list(src_bufs) + [identb.b], writes=[PB[bank]])

    def norm_mod_T(xt, gt, sht, hT_dst_ap, hT_buf, rings):
        junk, ssr, tmpr, hbr = rings
        jk, ss, tmp, hb = junk.next(), ssr.next(), tmpr.next(), hbr.next()
        P.act(lambda e: e.activation(out=jk.t[:], in_=xt.t[:], func=AF.Square, accum_out=ss.t[:, 0:1]), reads=[xt.b], writes=[jk.b, ss.b])
        rstd_from_ss(ss, D)
        P.pool(lambda e: e.tensor_tensor(out=tmp.t[:], in0=xt.t[:], in1=gt.t[:], op=ALU.mult), reads=[xt.b, gt.b], writes=[tmp.b])
        P.dve(lambda e: e.scalar_tensor_tensor(out=hb.t[:], in0=tmp.t[:], scalar=ss.t[:, 0:1], in1=sht.t[:], op0=ALU.mult, op1=ALU.add),
              reads=[tmp.b, ss.b, sht.b], writes=[hb.b])
        transposes([(hb.t[:, k * 128:(k + 1) * 128], 128) for k in range(8)], 7, [hb.b])
        P.act(lambda e: e.activation(out=hT_dst_ap, in_=psbf(7).rearrange("p (k t) -> p k t", k=8), func=AF.Copy), reads=[PB[7]], writes=[hT_buf])

    def rope_apply(src_ap, nh, cs, dst_ap, tmps, src_bufs, dst_bufs, dh=64, split=None):
        t1, t2 = tmps
        q = dh // 4
        if DEBUG_ROPE == "copy":
            P.act(lambda e: e.activation(out=dst_ap, in_=(src_ap if split is None else src_ap.rearrange("p (j s) d -> p j s d", j=split)), func=AF.Copy), reads=list(src_bufs), writes=list(dst_bufs))
            return
        cos = cs.t[:, 0:dh].unsqueeze(1).to_broadcast([128, nh, dh])
        x3 = t1.t[:, 0:nh * dh].rearrange("p (h d) -> p h d", h=nh)
        P.act(lambda e: e.activation(out=x3, in_=src_ap, func=AF.Copy), reads=list(src_bufs), writes=[t1.b])
        s5 = t1.t[:, 0:nh * dh].rearrange("p (h b x d) -> p h b x d", h=nh, b=2, x=2)
        o5 = t2.t[:, 0:nh * dh].rearrange("p (h b x d) -> p h b x d", h=nh, b=2, x=2)
        sn = cs.t[:, dh:2 * dh].rearrange("p (b x d) -> p b x d", b=2, x=2)
        for xx in range(2):
            P.dve(lambda e, xx=xx: e.tensor_tensor(out=o5[:, :, :, xx, :], in0=s5[:, :, :, 1 - xx, :],
                                                 in1=sn[:, :, xx, :].unsqueeze(1).to_broadcast([128, nh, 2, q]), op=ALU.mult),
                  reads=[t1.b, cs.b], writes=[t2.b])
        P.dve(lambda e: e.tensor_tensor(out=x3, in0=x3, in1=cos, op=ALU.mult), reads=[t1.b, cs.b], writes=[t1.b])
        if split is None:
            a1 = t1.t[:, 0:nh * dh].rearrange("p (h d) -> p h d", h=nh)
            a2 = t2.t[:, 0:nh * dh].rearrange("p (h d) -> p h d", h=nh)
        else:
            a1 = t1.t[:, 0:nh * dh].rearrange("p (j s d) -> p j s d", j=split, d=dh)
            a2 = t2.t[:, 0:nh * dh].rearrange("p (j s d) -> p j s d", j=split, d=dh)
        if DEBUG_ROPE == "noswap":
            P.act(lambda e: e.activation(out=dst_ap, in_=a1, func=AF.Copy), reads=[t1.b, t2.b], writes=list(dst_bufs))
            return
        fin = P.dve if DEBUG_ROPE == "dvefin" else P.pool
        fin(lambda e: e.tensor_tensor(out=dst_ap, in0=a1, in1=a2, op=ALU.add), reads=[t1.b, t2.b], writes=list(dst_bufs))

    def do_layer(l):
        even = (l % 2 == 0)
        li = l // 2
        need_ctx = l < L - 1
        lam0 = lam_init_of(l)
        layer_mark = ar.mark()
        if even:
            qn_bc = TB(ar.alloc([128, 256], F32))
            kvn_bc = TB(ar.alloc([128, 128], F32))
            sinkx = TB(ar.alloc([128, 8], F32))
            load_bc(qn_bc, ev_q_norm[li:li + 1, :])
            load_bc(kvn_bc, ev_kv_norm[li:li + 1, :])
            load_bc(sinkx, ev_sink[li:li + 1, :])
            P.act(lambda e: e.activation(out=sinkx.t[:], in_=sinkx.t[:], func=AF.Exp), reads=[sinkx.b], writes=[sinkx.b])
        else:
            qkg = TB(ar.alloc([128, 10, 64], F32))
            lamt = TB(ar.alloc([128, 4, 64], F32))
            lamw = TB(ar.alloc([128, 2, 64], F32))
            lams = TB(ar.alloc([128, 4], F32))
            subs = TB(ar.alloc([64, 2], F32))
            for h in range(10):
                P.dma(lambda e, h=h: e.dma_start(out=qkg.t[:, h, :], in_=od_qk_norm[li, (0 if h < 8 else 1):(1 if h < 8 else 2), :].partition_broadcast(128)),
                      writes=[qkg.b])
            P.dma(lambda e: e.dma_start(out=lamt.t[:].rearrange("p a d -> p (a d)"), in_=od_lambda[li:li + 1, :, :].rearrange("o a d -> o (a d)").partition_broadcast(128)),
                  writes=[lamt.b])
            P.dma(lambda e: e.dma_start(out=subs.t[:], in_=od_subln[li, :].rearrange("(h p) -> p h", p=64), allow_slow_non_contiguous=True), writes=[subs.b])
            P.dve(lambda e: e.tensor_tensor(out=lamw.t[:, 0, :], in0=lamt.t[:, 0, :], in1=lamt.t[:, 1, :], op=ALU.mult), reads=[lamt.b], writes=[lamw.b])
            P.dve(lambda e: e.tensor_tensor(out=lamw.t[:, 1, :], in0=lamt.t[:, 2, :], in1=lamt.t[:, 3, :], op=ALU.mult), reads=[lamt.b, lamw.b], writes=[lamw.b])
            P.dve(lambda e: e.tensor_reduce(out=lams.t[:, 0:2], in_=lamw.t[:], axis=AX.X, op=ALU.add), reads=[lamw.b], writes=[lams.b])
            P.act(lambda e: e.activation(out=lams.t[:, 0:2], in_=lams.t[:, 0:2], func=AF.Exp), reads=[lams.b], writes=[lams.b])
            P.dve(lambda e: e.tensor_tensor(out=lams.t[:, 2:3], in0=lams.t[:, 1:2], in1=lams.t[:, 0:1], op=ALU.subtract), reads=[lams.b], writes=[lams.b])
            P.dve(lambda e: e.tensor_scalar(out=lams.t[:, 3:4], in0=lams.t[:, 2:3], scalar1=-lam0, scalar2=None, op0=ALU.add), reads=[lams.b], writes=[lams.b])
            P.dve(lambda e: e.tensor_scalar(out=subs.t[:], in0=subs.t[:], scalar1=1.0 - lam0, scalar2=None, op0=ALU.mult), reads=[subs.b], writes=[subs.b])
        const_mark = ar.mark()

        def do_seq(b):
            P.barrier()
            ar.reset(const_mark)
            if even:
                qaT = ar.alloc([128, 4, T], BF16)
                kaT = ar.alloc([128, T], BF16)
                va = ar.alloc([128, NT, 1, 3, 64], BF16)
                qbT = ar.alloc([128, 8, T], BF16)
                kbT = ar.alloc([128, 8, T], BF16)
                vb = ar.alloc([128, NT, 4, 3, 64], BF16)
                vones = [(va, 2), (vb, 8)]
            else:
                qcT = ar.alloc([128, 4, T], BF16)
                kcT = ar.alloc([128, T], BF16)
                vc = ar.alloc([128, NT, 1, 3, 64], BF16)
                qdT = ar.alloc([128, 4, T], BF16)
                kdT = ar.alloc([128, 4, T], BF16)
                vd = ar.alloc([128, NT, 4, 3, 64], BF16)
                vones = [(vc, 2), (vd, 8)]
            KB = [Buf() for _ in range(NT)]
            VB1 = Buf()
            for (vt, nh) in vones:
                P.pool(lambda e, vt=vt: e.memset(vt[:, :, :, 1, :], 1.0), writes=[VB1])
            qkv_mark = ar.mark()

            g1t = [TB(ar.alloc([128, D], F32)) for _ in range(2)]
            sh1t = [TB(ar.alloc([128, D], F32)) for _ in range(2)]
            for j, r in enumerate((b, NB)):
                load_bc(g1t[j], modv[l, 0, r:r + 1, :], [modvB[l]])
                load_bc(sh1t[j], modv[l, 1, r:r + 1, :], [modvB[l]])
            NIN = EV_IN if even else OD_IN
            win = TB(ar.alloc([128, 8, NIN], BF16))
            wsrc = ev_w_in[li] if even else od_w_in[li]
            for k2 in range(4):
                P.dma(lambda e, k2=k2: e.dma_start(out=win.t[:, 2 * k2:2 * k2 + 2, :], in_=wsrc[k2 * 256:(k2 + 1) * 256, :].rearrange("(k p) n -> p k n", p=128)),
                      writes=[win.b], q="pool")
            if even:
                wuq = TB(ar.alloc([128, 2, 768], BF16))
                wukv = TB(ar.alloc([128, 1024], BF16))
                P.dma(lambda e: e.dma_start(out=wuq.t[:], in_=ev_w_uq[li].rearrange("(k p) n -> p k n", p=128)), writes=[wuq.b], q="pool")
                P.dma(lambda e: e.dma_start(out=wukv.t[:], in_=ev_w_ukv[li]), writes=[wukv.b], q="pool")
            xr = Ring(ar, 2, [128, D], F32)
            rings = (Ring(ar, 1, [128, D], BF16), Ring(ar, 2, [128, 4], F32), Ring(ar, 1, [128, D], F32), Ring(ar, 2, [128, D], BF16))
            hTr = Ring(ar, 2, [128, 8, 128], BF16)
            csr64 = Ring(ar, 2, [128, 128], F32)
            csr32 = Ring(ar, 2, [128, 64], F32)
            rt1 = TB(ar.alloc([128, 512 if even else 1024], F32))
            rt2 = TB(ar.alloc([128, 512 if even else 1024], F32))
            stg = Ring(ar, 2, [128, 1024], BF16)
            ss2 = Ring(ar, 2, [128, 16], F32)
            sqt = TB(ar.alloc([128, 384 if even else 640], F32))
            nrm = Ring(ar, 2, [128, 256], BF16)
            smT = Ring(ar, 2, [128, 2, 128], BF16)
            kper = Ring(ar, 2, [128, 32], BF16)

            def p1_tile(tt):
                isctx = tt < NC
                pt = tt - NC
                tok = slice(tt * 128, (tt + 1) * 128)
                xt = xr.next()
                src, srcb = x_src(l, b, tt)
                P.dma(lambda e, xt=xt, src=src: e.dma_start(out=xt.t[:], in_=src), reads=srcb, writes=[xt.b])
                hT = hTr.next()
                norm_mod_T(xt, g1t[1 if isctx else 0], sh1t[1 if isctx else 0], hT.t[:], hT.b, rings)
                if not isctx:
                    cs64, cs32 = csr64.next(), csr32.next()
                    P.dma(lambda e, cs64=cs64, pt=pt: e.dma_start(out=cs64.t[:], in_=rope64[pt * 128:(pt + 1) * 128, :]), writes=[cs64.b])
                    if even:
                        P.dma(lambda e, cs32=cs32, pt=pt: e.dma_start(out=cs32.t[:], in_=rope32[pt * 128:(pt + 1) * 128, :]), writes=[cs32.b])
                yield
                nchunks = (NIN + 511) // 512
                for n in range(nchunks):
                    w = min(512, NIN - n * 512)
                    for k in range(8):
                        P.pe(lambda e, hT=hT, n=n, k=k, w=w: e.matmul(ps[:, n, 0:w], lhsT=hT.t[:, k, :], rhs=win.t[:, k, n * 512:n * 512 + w], start=(k == 0), stop=(k == 7)),
                             reads=[hT.b, win.b], writes=[PB[n]])
                flat = ps[:, 0:5, :].rearrange("p a b -> p (a b)")
                st = stg.next()
                if even:
                    qk_src = flat[:, 0:640].rearrange("p (h d) -> p h d", h=10)
                    dq = st.t[:, 0:512].rearrange("p (s j d) -> p j s d", s=4, j=2)
                    dk = st.t[:, 512:640].rearrange("p (h d) -> p h d", h=2)
                    if isctx:
                        P.act(lambda e, dq=dq, qk_src=qk_src: e.activation(out=dq, in_=qk_src[:, 0:8, :].rearrange("p (j s) d -> p j s d", j=2), func=AF.Copy), reads=[PB[0]], writes=[st.b])
                        P.act(lambda e, dk=dk, qk_src=qk_src: e.activation(out=dk, in_=qk_src[:, 8:10, :], func=AF.Copy), reads=[PB[1]], writes=[st.b])
                    else:
                        rope_apply(qk_src[:, 0:8, :], 8, cs64, dq, (rt1, rt2), [PB[0]], [st.b], split=2)
                        rope_apply(qk_src[:, 8:10, :], 2, cs64, dk, (rt1, rt2), [PB[1]], [st.b])
                    transposes([(st.t[:, s * 128:(s + 1) * 128], 128) for s in range(5)], 7, [st.b])
                    P.act(lambda e, tok=tok: e.activation(out=qaT[:, :, tok], in_=psbf(7)[:, 0:512].rearrange("p (s t) -> p s t", s=4), func=AF.Copy), reads=[PB[7]], writes=[KB[tt]])
                    P.act(lambda e, tok=tok: e.activation(out=kaT[:, tok], in_=psbf(7)[:, 512:640], func=AF.Copy), reads=[PB[7]], writes=[KB[tt]])
                    P.act(lambda e, tt=tt: e.activation(out=va[:, tt, :, 0:3:2, :], in_=flat[:, 640:768].rearrange("p (a e d) -> p a e d", a=1, e=2), func=AF.Copy), reads=[PB[1]], writes=[KB[tt]])
                    yield
                    ss = ss2.next()
                    P.act(lambda e, ss=ss: e.activation(out=sqt.t[:, 0:256], in_=flat[:, 768:1024], func=AF.Square, accum_out=ss.t[:, 0:1]), reads=[PB[1]], writes=[sqt.b, ss.b])
                    P.act(lambda e, ss=ss: e.activation(out=sqt.t[:, 256:384], in_=flat[:, 1024:1152], func=AF.Square, accum_out=ss.t[:, 1:2]), reads=[PB[2]], writes=[sqt.b, ss.b])
                    rstd_from_ss(ss, 256, slice(0, 1))
                    rstd_from_ss(ss, 128, slice(1, 2))
                    nq = nrm.next()
                    P.dve(lambda e, nq=nq, ss=ss: e.scalar_tensor_tensor(out=nq.t[:, 0:256], in0=flat[:, 768:1024], scalar=ss.t[:, 0:1], in1=qn_bc.t[:], op0=ALU.mult, op1=ALU.mult),
                          reads=[PB[1], ss.b, qn_bc.b], writes=[nq.b])
                    transposes([(nq.t[:, k * 128:(k + 1) * 128], 128) for k in range(2)], 7, [nq.b])
                    cqT = smT.next()
                    P.act(lambda e, cqT=cqT: e.activation(out=cqT.t[:], in_=psbf(7)[:, 0:256].rearrange("p (k t) -> p k t", k=2), func=AF.Copy), reads=[PB[7]], writes=[cqT.b])
                    for n, (c0, w) in enumerate(((0, 512), (512, 256))):
                        for k in range(2):
                            P.pe(lambda e, cqT=cqT, n=n, k=k, c0=c0, w=w: e.matmul(ps[:, 3 + n, 0:w], lhsT=cqT.t[:, k, :], rhs=wuq.t[:, k, c0:c0 + w], start=(k == 0), stop=(k == 1)),
                                 reads=[cqT.b, wuq.b], writes=[PB[3 + n]])
                    qb_src = ps[:, 3:5, :].rearrange("p a b -> p (a b)")[:, 0:768].rearrange("p (h d) -> p h d", h=8)
                    st2 = stg.next()
                    qb_dst = st2.t[:, 0:768].rearrange("p (h d) -> p h d", h=8)
                    if isctx:
                        P.act(lambda e, qb_dst=qb_dst, qb_src=qb_src: e.activation(out=qb_dst, in_=qb_src, func=AF.Copy), reads=[PB[3], PB[4]], writes=[st2.b])
                    else:
                        P.act(lambda e, qb_dst=qb_dst, qb_src=qb_src: e.activation(out=qb_dst[:, :, 0:64], in_=qb_src[:, :, 0:64], func=AF.Copy), reads=[PB[3], PB[4]], writes=[st2.b])
                        rope_apply(qb_src[:, :, 64:96], 8, cs32, qb_dst[:, :, 64:96], (rt1, rt2), [PB[3], PB[4]], [st2.b], dh=32)
                    transposes([(st2.t[:, h * 96:(h + 1) * 96], 96) for h in range(8)], 7, [st2.b])
                    P.act(lambda e, tok=tok: e.activation(out=qbT[0:96, :, tok], in_=psbf(7)[0:96, :].rearrange("p (h t) -> p h t", h=8), func=AF.Copy), reads=[PB[7]], writes=[KB[tt]])
                    nk = nrm.next()
                    P.dve(lambda e, nk=nk, ss=ss: e.scalar_tensor_tensor(out=nk.t[:, 0:128], in0=flat[:, 1024:1152], scalar=ss.t[:, 1:2], in1=kvn_bc.t[:], op0=ALU.mult, op1=ALU.mult),
                          reads=[PB[2], ss.b, kvn_bc.b], writes=[nk.b])
                    transposes([(nk.t[:, 0:128], 128)], 7, [nk.b])
                    ckT = smT.next()
                    P.act(lambda e, ckT=ckT: e.activation(out=ckT.t[:, 0, :], in_=psbf(7)[:, 0:128], func=AF.Copy), reads=[PB[7]], writes=[ckT.b])
                    for n in range(2):
                        P.pe(lambda e, ckT=ckT, n=n: e.matmul(ps[:, 5 + n, :], lhsT=ckT.t[:, 0, :], rhs=wukv.t[:, n * 512:(n + 1) * 512], start=True, stop=True),
                             reads=[ckT.b, wukv.b], writes=[PB[5 + n]])
                    kv_src = ps[:, 5:7, :].rearrange("p a b -> p (a b)").rearrange("p (h d) -> p h d", h=8)
                    st3 = stg.next()
                    kb_dst = st3.t[:, 0:768].rearrange("p (h d) -> p h d", h=8)
                    P.act(lambda e, kb_dst=kb_dst, kv_src=kv_src: e.activation(out=kb_dst[:, :, 0:64], in_=kv_src[:, :, 0:64], func=AF.Copy), reads=[PB[5], PB[6]], writes=[st3.b])
                    P.act(lambda e, tt=tt, kv_src=kv_src: e.activation(out=vb[:, tt, :, 0:3:2, :], in_=kv_src[:, :, 64:128].rearrange("p (a e) d -> p a e d", e=2), func=AF.Copy), reads=[PB[5], PB[6]], writes=[KB[tt]])
                    kp = kper.next()
                    kpe_src = flat[:, 1152:1184].rearrange("p (h d) -> p h d", h=1)
                    if isctx:
                        P.act(lambda e, kp=kp, kpe_src=kpe_src: e.activation(out=kp.t[:].rearrange("p (h d) -> p h d", h=1), in_=kpe_src, func=AF.Copy), reads=[PB[2]], writes=[kp.b])
                    else:
                        rope_apply(kpe_src, 1, cs32, kp.t[:].rearrange("p (h d) -> p h d", h=1), (rt1, rt2), [PB[2]], [kp.b], dh=32)
                    P.pool(lambda e, kp=kp, kb_dst=kb_dst: e.tensor_copy(out=kb_dst[:, :, 64:96], in_=kp.t[:].unsqueeze(1).to_broadcast([128, 8, 32])), reads=[kp.b], writes=[st3.b])
                    transposes([(st3.t[:, h * 96:(h + 1) * 96], 96) for h in range(8)], 7, [st3.b])
                    P.act(lambda e, tok=tok: e.activation(out=kbT[0:96, :, tok], in_=psbf(7)[0:96, :].rearrange("p (h t) -> p h t", h=8), func=AF.Copy), reads=[PB[7]], writes=[KB[tt]])
                else:
                    qk_src = flat[:, 0:640].rearrange("p (h d) -> p h d", h=10)
                    ss = ss2.next()
                    P.act(lambda e: e.activation(out=sqt.t[:], in_=flat[:, 0:640], func=AF.Square), reads=[PB[0], PB[1]], writes=[sqt.b])
                    P.dve(lambda e, ss=ss: e.tensor_reduce(out=ss.t[:, 0:10], in_=sqt.t[:].rearrange("p (h d) -> p h d", h=10), axis=AX.X, op=ALU.add), reads=[sqt.b], writes=[ss.b])
                    rstd_from_ss(ss, 64, slice(0, 10))
                    xn = rt1.t[:, 0:640].rearrange("p (h d) -> p h d", h=10)
                    xs = rt2.t[:, 0:640].rearrange("p (h d) -> p h d", h=10)
                    P.act(lambda e, qk_src=qk_src, xs=xs: e.activation(out=xs, in_=qk_src, func=AF.Copy), reads=[PB[0], PB[1]], writes=[rt2.b])
                    P.dve(lambda e, ss=ss, xs=xs, xn=xn: e.tensor_tensor(out=xn, in0=xs, in1=ss.t[:, 0:10].unsqueeze(2).to_broadcast([128, 10, 64]), op=ALU.mult),
                          reads=[rt2.b, ss.b], writes=[rt1.b])
                    xg = sqt.t[:].rearrange("p (h d) -> p h d", h=10)
                    P.pool(lambda e, xn=xn, xg=xg: e.tensor_tensor(out=xg, in0=xn, in1=qkg.t[:], op=ALU.mult), reads=[rt1.b, qkg.b], writes=[sqt.b])
                    dq = st.t[:, 0:512].rearrange("p (s j d) -> p j s d", s=4, j=2)
                    dk = st.t[:, 512:640].rearrange("p (h d) -> p h d", h=2)
                    if isctx:
                        P.pool(lambda e, dq=dq, xg=xg: e.tensor_copy(out=dq, in_=xg[:, 0:8, :].rearrange("p (j s) d -> p j s d", j=2)), reads=[sqt.b], writes=[st.b])
                        P.pool(lambda e, dk=dk, xg=xg: e.tensor_copy(out=dk, in_=xg[:, 8:10, :]), reads=[sqt.b], writes=[st.b])
                    else:
                        rope_apply(xg[:, 0:8, :], 8, cs64, dq, (rt1, rt2), [sqt.b], [st.b], split=2)
                        rope_apply(xg[:, 8:10, :], 2, cs64, dk, (rt1, rt2), [sqt.b], [st.b])
                    transposes([(st.t[:, s * 128:(s + 1) * 128], 128) for s in range(5)], 7, [st.b])
                    P.act(lambda e, tok=tok: e.activation(out=qcT[:, :, tok], in_=psbf(7)[:, 0:512].rearrange("p (s t) -> p s t", s=4), func=AF.Copy), reads=[PB[7]], writes=[KB[tt]])
                    P.act(lambda e, tok=tok: e.activation(out=kcT[:, tok], in_=psbf(7)[:, 512:640], func=AF.Copy), reads=[PB[7]], writes=[KB[tt]])
                    P.act(lambda e, tt=tt: e.activation(out=vc[:, tt, :, 0:3:2, :], in_=flat[:, 640:768].rearrange("p (a e d) -> p a e d", a=1, e=2), func=AF.Copy), reads=[PB[1]], writes=[KB[tt]])
                    yield
                    st2 = stg.next()
                    qd_src = flat[:, 768:1792].rearrange("p (h d) -> p h d", h=16)
                    qd_dst = st2.t[:].rearrange("p (h d) -> p h d", h=16)
                    if isctx:
                        P.act(lambda e, qd_dst=qd_dst, qd_src=qd_src: e.activation(out=qd_dst, in_=qd_src, func=AF.Copy), reads=[PB[1], PB[2], PB[3]], writes=[st2.b])
                    else:
                        rope_apply(qd_src, 16, cs64, qd_dst, (rt1, rt2), [PB[1], PB[2], PB[3]], [st2.b])
                    transposes([(st2.t[:, s * 128:(s + 1) * 128], 128) for s in range(8)], 7, [st2.b])
                    P.act(lambda e, tok=tok: e.activation(out=qdT[:, :, tok], in_=psbf(7)[:, 0:512].rearrange("p (s t) -> p s t", s=4), func=AF.Copy), reads=[PB[7]], writes=[KB[tt]])
                    P.act(lambda e, tok=tok: e.activation(out=kdT[:, :, tok], in_=psbf(7)[:, 512:1024].rearrange("p (s t) -> p s t", s=4), func=AF.Copy), reads=[PB[7]], writes=[KB[tt]])
                    P.act(lambda e, tt=tt: e.activation(out=vd[:, tt, :, 0:3:2, :], in_=flat[:, 1792:2304].rearrange("p (a e d) -> p a e d", a=4, e=2), func=AF.Copy), reads=[PB[3], PB[4]], writes=[KB[tt]])

            ntl = NT if DEBUG_STOP != "p1a" else DEBUG_NT
            gens = [p1_tile(tt) for tt in range(ntl)]
            next(gens[0])
            for tt in range(ntl):
                next(gens[tt])
                if tt + 1 < ntl:
                    next(gens[tt + 1])
                for _ in gens[tt]:
                    pass
            if DEBUG_STOP in ("p1", "p1a"):
                return
            P.barrier()
            ar.reset(qkv_mark)
            wout = TB(ar.alloc([128, 8, D], BF16))
            wosrc = ev_w_out[li] if even else od_w_out[li]
            for c4 in range(4):
                P.dma(lambda e, c4=c4: e.dma_start(out=wout.t[:, 2 * c4:2 * c4 + 2, :], in_=wosrc[c4 * 256:(c4 + 1) * 256, :].rearrange("(c p) n -> p c n", p=128)),
                      writes=[wout.b], q="pool")
            gg1 = TB(ar.alloc([128, D], F32))
            ptr = Ring(ar, 3, [128, 2, 512], BF16)
            osbr = Ring(ar, 4, [128, 512], F32)
            oT = TB(ar.alloc([128, 8, 512], BF16))

            def oTa(hc, N):
                return oT.t[(hc % 2) * 64:(hc % 2) * 64 + 64, hc // 2, 0:N]
            xr2 = Ring(ar, 2, [128, D], F32)
            ytr = Ring(ar, 2, [128, D], F32)
            x1r = Ring(ar, 2, [128, D], F32)
            jk2 = Ring(ar, 1, [128, D], BF16)
            ss3 = Ring(ar, 2, [128, 4], F32)
            if not even:
                odr = Ring(ar, 8, [64, 512], F32)
                sqr = Ring(ar, 2, [64, 512], BF16)
                onesb = TB(ar.alloc([64, 64], BF16))
                P.pool(lambda e: e.memset(onesb.t[:], 1.0), writes=[onesb.b])
                rsd = Ring(ar, 3, [64, 512], F32)
            obank = [4]
            spair = [0]

            def next_obank():
                bk = obank[0]
                obank[0] = 4 + (bk - 4 + 1) % 4
                return bk

            def next_spair():
                s = spair[0]
                spair[0] = 2 - s
                return s

            def vop(vt, t, slot):
                p_, e_ = slot // 2, slot % 2
                return vt[:, t, p_, e_:e_ + 2, :].rearrange("p a d -> p (a d)")

            def normalise(bk, N, dst_ap, dst_bufs, sink_h=None, keep=None, e_=0):
                vs = slice(e_ * 64, e_ * 64 + 64)
                ds = slice((1 - e_) * 64, (1 - e_) * 64 + 64)
                r = osbr.next()
                if sink_h is not None:
                    P.dve(lambda e: e.tensor_scalar(out=r.t[ds, 0:N], in0=ps[ds, bk, 0:N], scalar1=sinkx.t[ds, sink_h:sink_h + 1], scalar2=None, op0=ALU.add),
                          reads=[PB[bk], sinkx.b], writes=[r.b])
                    P.dve(lambda e: e.reciprocal(out=r.t[ds, 0:N], in_=r.t[ds, 0:N]), reads=[r.b], writes=[r.b])
                else:
                    P.dve(lambda e: e.reciprocal(out=r.t[ds, 0:N], in_=ps[ds, bk, 0:N]), reads=[PB[bk]], writes=[r.b])
                if keep is None:
                    P.dve(lambda e: e.tensor_tensor(out=dst_ap, in0=ps[vs, bk, 0:N], in1=r.t[ds, 0:N], op=ALU.mult), reads=[r.b, PB[bk]], writes=list(dst_bufs))
                else:
                    P.dve(lambda e: e.tensor_tensor(out=keep.t[:, 0:N], in0=ps[vs, bk, 0:N], in1=r.t[ds, 0:N], op=ALU.mult), reads=[r.b, PB[bk]], writes=[keep.b])

            def attend_jobs(jobs, units, ktiles, N, scale, post, qB):
                if len(units) == 1:
                    u = units[0]
                    jl = [[(u, t)] for t in ktiles]
                    steps = [jl[i] + (jl[i + 1] if i + 1 < len(jl) else []) for i in range(0, len(jl), 2)]
                else:
                    steps = [[(units[0], t), (units[1], t)] for t in ktiles]
                banks = {}
                for u in units:
                    for vi in range(len(u["V"])):
                        banks[(id(u), vi)] = next_obank()
                first = {k: True for k in banks}
                last_step = len(steps) - 1
                single = len(units) == 1
                for si, step in enumerate(steps):
                    sp = next_spair()
                    pt = ptr.next()
                    nj = len(step)

                    def S(step=step, sp=sp):
                        for j, (u, t) in enumerate(step):
                            P.pe(lambda e, u=u, t=t, j=j: e.matmul(ps[:, sp + j, 0:N], lhsT=u["kT"](t), rhs=u["qT"], start=True, stop=True),
                                 reads=[KB[t]] + qB, writes=[PB[sp + j]])

                    def E(sp=sp, pt=pt, nj=nj):
                        P.act(lambda e: e.activation(out=pt.t[:, 0:nj, 0:N], in_=ps[:, sp:sp + nj, 0:N], func=AF.Exp, scale=scale),
                              reads=[PB[sp + j] for j in range(nj)], writes=[pt.b])

                    def V(step=step, pt=pt, si=si, nj=nj):
                        for j, (u, t) in enumerate(step):
                            for vi, vfn in enumerate(u["V"]):
                                key = (id(u), vi)
                                bk = banks[key]
                                lastc = (si == last_step and j == nj - 1) if single else (si == last_step)
                                P.pe(lambda e, j=j, t=t, vfn=vfn, bk=bk, st_=first[key], lastc=lastc: e.matmul(ps[:, bk, 0:N], lhsT=vfn(t), rhs=pt.t[:, j, 0:N], start=st_, stop=lastc),
                                     reads=[KB[t], VB1, pt.b], writes=[PB[bk]])
                                first[key] = False
                    jobs.append(dict(S=S, E=E, V=V, post=None, obanks=set(banks.values())))
                jobs[-1]["post"] = lambda: post(banks, sp)

            def run_jobs(jobs):
                n = len(jobs)
                pend = None
                later = []

                def do_post(p, i):
                    c = p()
                    if c is not None:
                        later.append((i + 3, c))
                if n:
                    jobs[0]["S"]()
                for i in range(n):
                    if i + 1 < n:
                        jobs[i + 1]["S"]()
                    jobs[i]["E"]()
                    jobs[i]["V"]()
                    for (due, c) in [x for x in later if x[0] <= i]:
                        c()
                    later[:] = [x for x in later if x[0] > i]
                    if pend is not None:
                        do_post(pend, i)
                        pend = None
                    p = jobs[i]["post"]
                    if p is not None:
                        if i + 1 < n and not (jobs[i]["obanks"] & jobs[i + 1]["obanks"]):
                            pend = p
                        else:
                            do_post(p, i)
                if pend is not None:
                    do_post(pend, n)
                for (due, c) in later:
                    c()

            groups = []
            if need_ctx:
                groups.append(("ctx", list(range(NC))))
            for g in range(NL // 4):
                groups.append(("lat", [NC + 4 * g + j for j in range(4)]))

            gg_loaded = [None]
            def p2_group(gkind, qtiles):
                if gg_loaded[0] != gkind:
                    r = NB if gkind == "ctx" else b
                    load_bc(gg1, modv[l, 2, r:r + 1, :], [modvB[l]])
                    gg_loaded[0] = gkind
                N = 128 * len(qtiles)
                q0 = qtiles[0] * 128
                qs = slice(q0, q0 + N)
                qB = [KB[t] for t in qtiles]
                ktiles = list(range(NC)) if gkind == "ctx" else list(range(NT))
                jobs = []
                if even:
                    def a_head(h):
                        s_, j_ = h % 4, h // 4
                        pr = slice(j_ * 64, (j_ + 1) * 64)
                        bk = next_obank()
                        jobs = []

                        def a_qtile(qi, qt):
                            lt = qt - NC
                            kl = [(t, None) for t in range(NC)]
                            if lt > 0:
                                kl.append((qt - 1, 0))
                            kl.append((qt, None))
                            if lt < NL - 1:
                                kl.append((qt + 1, 1))
                            qcols = slice(qt * 128, (qt + 1) * 128)
                            nk = len(kl)
                            st = {}

                            def S():
                                sp = st["sp"] = next_spair()
                                pt = st["pt"] = ptr.next()
                                st["ptv"] = pt.t[:].rearrange("p a (c n) -> p (a c) n", n=128)
                                st["psv"] = ps[:, sp:sp + 2, :].rearrange("p a (c n) -> p (a c) n", n=128)
                                for ci, (t, mk) in enumerate(kl):
                                    bnk, col = sp + ci // 4, (ci % 4) * 128
                                    P.pe(lambda e, t=t, bnk=bnk, col=col, mk=mk: e.matmul(
                                        ps[:, bnk, col:col + 128], lhsT=kaT[pr, t * 128:(t + 1) * 128], rhs=qaT[pr, s_, qcols], start=True, stop=(mk is None)),
                                        reads=[KB[t], KB[qt]], writes=[PB[bnk]])
                                    if mk is not None:
                                        P.pe(lambda e, bnk=bnk, col=col, mk=mk: e.matmul(ps[:, bnk, col:col + 128], lhsT=identb.t[:], rhs=maskb.t[:, mk, :], start=False, stop=True),
                                             reads=[identb.b, maskb.b], writes=[PB[bnk]])

                            def E():
                                sp, pt, ptv, psv = st["sp"], st["pt"], st["ptv"], st["psv"]
                                P.act(lambda e: e.activation(out=ptv[:, 0:nk, :], in_=psv[:, 0:nk, :], func=AF.Exp, scale=0.125),
                                      reads=[PB[sp], PB[sp + 1]], writes=[pt.b])

                            def V():
                                pt, ptv = st["pt"], st["ptv"]
                                for ci, (t, mk) in enumerate(kl):
                                    P.pe(lambda e, ci=ci, t=t: e.matmul(ps[:, bk, qi * 128:(qi + 1) * 128], lhsT=vop(va, t, j_), rhs=ptv[:, ci, :],
                                                                       start=(ci == 0), stop=(ci == nk - 1)),
                                         reads=[KB[t], VB1, pt.b], writes=[PB[bk]])
                            jobs.append(dict(S=S, E=E, V=V, post=None, obanks={bk}))
                        for qi, qt in enumerate(qtiles):
                            a_qtile(qi, qt)
                        jobs[-1]["post"] = lambda: normalise(bk, N, oTa(h, N), [oT.b], sink_h=h, e_=j_)
                        return jobs

                    def a_ctx_pair(s_):
                        us = []
                        for j_ in range(2):
                            pr = slice(j_ * 64, (j_ + 1) * 64)
                            us.append(dict(kT=lambda t, pr=pr: kaT[pr, t * 128:(t + 1) * 128], qT=qaT[pr, s_, qs], V=[lambda t, j_=j_: vop(va, t, j_)]))

                        def post(banks, sp_):
                            for j_ in range(2):
                                normalise(banks[(id(us[j_]), 0)], N, oTa(s_ + 4 * j_, N), [oT.b], sink_h=s_ + 4 * j_, e_=j_)
                        attend_jobs(jobs, us, ktiles, N, 0.125, post, qB)
                    for s_ in range(4):
                        if gkind == "ctx":
                            a_ctx_pair(s_)
                        else:
                            ja, jb = a_head(s_), a_head(s_ + 4)
                            for x_, y_ in zip(ja, jb):
                                jobs.append(x_)
                                jobs.append(y_)

                    def b_head(h):
                        u = dict(kT=lambda t: kbT[0:96, h, t * 128:(t + 1) * 128], qT=qbT[0:96, h, qs], V=[lambda t: vop(vb, t, h)])
                        attend_jobs(jobs, [u], ktiles, N, 96 ** -0.5, lambda banks, sp_: normalise(banks[(id(u), 0)], N, oTa(8 + h, N), [oT.b], e_=h % 2), qB)
                    for h in range(8):
                        b_head(h)
                else:
                    def c_pair(s_):
                        us = []
                        for j_ in range(2):
                            pr = slice(j_ * 64, (j_ + 1) * 64)
                            us.append(dict(kT=lambda t, pr=pr: kcT[pr, t * 128:(t + 1) * 128], qT=qcT[pr, s_, qs], V=[lambda t, j_=j_: vop(vc, t, j_)]))

                        def post(banks, sp_):
                            for j_ in range(2):
                                normalise(banks[(id(us[j_]), 0)], N, oTa(s_ + 4 * j_, N), [oT.b], e_=j_)
                        attend_jobs(jobs, us, ktiles, N, 0.125, post, qB)
                    for s_ in range(4):
                        c_pair(s_)

                    def d_head(h):
                        us = []
                        for i in range(2):
                            pr = slice(i * 64, (i + 1) * 64)
                            us.append(dict(kT=lambda t, pr=pr: kdT[pr, h, t * 128:(t + 1) * 128], qT=qdT[pr, h, qs],
                                           V=[lambda t, hf=hf: vop(vd, t, 2 * h + hf) for hf in range(2)]))

                        def d_post(banks, sp):
                            ods = []
                            rr = []
                            for i in range(2):
                                r = osbr.next()
                                bk0 = banks[(id(us[i]), 0)]
                                if i == 0:
                                    P.dve(lambda e, r=r, bk0=bk0: e.reciprocal(out=r.t[64:128, 0:N], in_=ps[64:128, bk0, 0:N]), reads=[PB[bk0]], writes=[r.b])
                                else:
                                    P.act(lambda e, r=r, bk0=bk0: e.activation(out=r.t[64:128, 0:N], in_=ps[64:128, bk0, 0:N], func=AF.Ln), reads=[PB[bk0]], writes=[r.b])
                                    P.act(lambda e, r=r: e.activation(out=r.t[64:128, 0:N], in_=r.t[64:128, 0:N], func=AF.Exp, scale=-1.0), reads=[r.b], writes=[r.b])
                                rr.append(r)
                            oo = [[odr.next(), odr.next()] for _ in range(2)]
                            for hf in range(2):
                                vs = slice(hf * 64, hf * 64 + 64)
                                for i in range(2):
                                    bk_ = banks[(id(us[i]), hf)]
                                    P.dve(lambda e, o=oo[hf][i], r=rr[i], bk_=bk_, vs=vs: e.tensor_tensor(out=o.t[:, 0:N], in0=ps[vs, bk_, 0:N], in1=r.t[64:128, 0:N], op=ALU.mult),
                                          reads=[rr[i].b, PB[bk_]], writes=[oo[hf][i].b])
                            for hf in range(2):
                                o1, o2 = oo[hf]
                                P.dve(lambda e, o1=o1, o2=o2: e.scalar_tensor_tensor(out=o1.t[:, 0:N], in0=o2.t[:, 0:N], scalar=lams.t[0:64, 3:4], in1=o1.t[:, 0:N], op0=ALU.mult, op1=ALU.add),
                                      reads=[o1.b, o2.b, lams.b], writes=[o1.b])
                                ods.append(o1)
                            for hf in range(2):
                                sq = sqr.next()
                                P.pool(lambda e, sq=sq, o=ods[hf]: e.tensor_tensor(out=sq.t[:, 0:N], in0=o.t[:, 0:N], in1=o.t[:, 0:N], op=ALU.mult), reads=[ods[hf].b], writes=[sq.b])
                                P.pe(lambda e, sq=sq, hf=hf: e.matmul(ps[0:64, sp, 0:N], lhsT=onesb.t[:], rhs=sq.t[:, 0:N], start=(hf == 0), stop=(hf == 1)),
                                     reads=[onesb.b, sq.b], writes=[PB[sp]])
                            rs = rsd.next()
                            P.dve(lambda e: e.tensor_copy(out=rs.t[:, 0:N], in_=ps[0:64, sp, 0:N]), reads=[PB[sp]], writes=[rs.b])

                            def stage2():
                                P.act(lambda e: e.activation(out=rs.t[:, 0:N], in_=rs.t[:, 0:N], func=AF.Ln, scale=1.0 / 128, bias=epsb.t[0:64, :]), reads=[rs.b, epsb.b], writes=[rs.b])
                                P.act(lambda e: e.activation(out=rs.t[:, 0:N], in_=rs.t[:, 0:N], func=AF.Exp, scale=-0.5), reads=[rs.b], writes=[rs.b])
                                for hf in range(2):
                                    P.dve(lambda e, hf=hf, o=ods[hf]: e.scalar_tensor_tensor(out=oTa(8 + 2 * h + hf, N), in0=o.t[:, 0:N], scalar=subs.t[:, hf:hf + 1], in1=rs.t[:, 0:N],
                                                                                          op0=ALU.mult, op1=ALU.mult),
                                          reads=[ods[hf].b, subs.b, rs.b], writes=[oT.b])
                            return stage2
                        attend_jobs(jobs, us, ktiles, N, 0.125, d_post, qB)
                    for h in range(4):
                        d_head(h)
                run_jobs(jobs)

                def outproj(qi, qt):
                    sp = next_spair()
                    for hfc in range(2):
                        for hc in range(8):
                            P.pe(lambda e, hc=hc, hfc=hfc, sp=sp, qi=qi: e.matmul(ps[:, sp + hfc, :], lhsT=oT.t[:, hc, qi * 128:(qi + 1) * 128], rhs=wout.t[:, hc, hfc * 512:(hfc + 1) * 512],
                                                                            start=(hc == 0), stop=(hc == 7)),
                                 reads=[oT.b, wout.b], writes=[PB[sp + hfc]])
                    jk, ss = jk2.next(), ss3.next()
                    P.act(lambda e, jk=jk, ss=ss, sp=sp: e.activation(out=jk.t[:], in_=ps2(sp), func=AF.Square, accum_out=ss.t[:, 0:1]), reads=[PB[sp], PB[sp + 1]], writes=[jk.b, ss.b])
                    rstd_from_ss(ss, D)
                    xt = xr2.next()
                    src, srcb = x_src(l, b, qt)
                    P.dma(lambda e, xt=xt, src=src: e.dma_start(out=xt.t[:], in_=src), reads=srcb, writes=[xt.b])
                    yt = ytr.next()
                    ggs = gg1
                    P.dve(lambda e, yt=yt, ss=ss, sp=sp, ggs=ggs: e.scalar_tensor_tensor(out=yt.t[:], in0=ps2(sp), scalar=ss.t[:, 0:1], in1=ggs.t[:], op0=ALU.mult, op1=ALU.mult),
                          reads=[PB[sp], PB[sp + 1], ss.b, ggs.b], writes=[yt.b])
                    x1 = x1r.next()
                    P.pool(lambda e, x1=x1, xt=xt, yt=yt: e.tensor_tensor(out=x1.t[:], in0=xt.t[:], in1=yt.t[:], op=ALU.add), reads=[xt.b, yt.b], writes=[x1.b])
                    P.dma(lambda e, x1=x1, qt=qt: e.dma_start(out=x1s[b, qt * 128:(qt + 1) * 128, :], in_=x1.t[:]), reads=[x1.b], writes=[x1B[b][qt]])

                for qi, qt in enumerate(qtiles):
                    outproj(qi, qt)
            for gkind, qtiles in groups:
                p2_group(gkind, qtiles)
        for b in range(NB if DEBUG_STOP != "p1a" else 1):
            do_seq(b)
        if DEBUG_STOP in ("p1", "p1a", "p2"):
            return
        P.barrier()
        ar.reset(const_mark)
        fwin = TB(ar.alloc([128, 8, 2 * FH], BF16))
        fwout = TB(ar.alloc([128, 22, D], BF16))
        fwinB = [Buf() for _ in range(11)]
        fwoutB = [Buf() for _ in range(11)]
        for c2 in range(11):
            for gu in range(2):
                col = gu * FH + c2 * 256
                P.dma(lambda e, col=col: e.dma_start(out=fwin.t[:, :, col:col + 256], in_=ffn_w_in[l, :, col:col + 256].rearrange("(k p) n -> p k n", p=128)),
                      writes=[fwinB[c2]], q="pool")
        for c2 in range(11):
            P.dma(lambda e, c2=c2: e.dma_start(out=fwout.t[:, 2 * c2:2 * c2 + 2, :], in_=ffn_w_out[l, c2 * 256:(c2 + 1) * 256, :].rearrange("(c p) n -> p c n", p=128)),
                  writes=[fwoutB[c2]], q="pool")
        mod3 = [TB(ar.alloc([128, D], F32)) for _ in range(3)]
        xr3 = Ring(ar, 2, [128, D], F32)
        rings3 = (Ring(ar, 1, [128, D], BF16), Ring(ar, 2, [128, 4], F32), Ring(ar, 1, [128, D], F32))
        hbr4 = Ring(ar, 4, [128, D], BF16)
        h2T = TB(ar.alloc([128, 8, 512], BF16))
        actT = TB(ar.alloc([128, 22, 512], BF16))
        sgr = Ring(ar, 2, [128, 512], F32)
        ytr3 = rings3[2]
        spair3 = [0]
        opair3 = [4]
        def p3_seq(b):
            groups = []
            if need_ctx:
                groups.append(list(range(NC)))
            for g in range(NL // 4):
                groups.append([NC + 4 * g + j for j in range(4)])
            m3_loaded = [None]
            pre_done = [None]

            def preA(tiles):
                hbs = []
                for tt in tiles:
                    xt = xr3.next()
                    P.dma(lambda e, xt=xt, tt=tt: e.dma_start(out=xt.t[:], in_=x1s[b, tt * 128:(tt + 1) * 128, :]), reads=[x1B[b][tt]], writes=[xt.b])
                    jk, ss, tmp, hb = rings3[0].next(), rings3[1].next(), rings3[2].next(), hbr4.next()
                    P.act(lambda e, jk=jk, ss=ss, xt=xt: e.activation(out=jk.t[:], in_=xt.t[:], func=AF.Square, accum_out=ss.t[:, 0:1]), reads=[xt.b], writes=[jk.b, ss.b])
                    rstd_from_ss(ss, D)
                    P.pool(lambda e, tmp=tmp, xt=xt: e.tensor_tensor(out=tmp.t[:], in0=xt.t[:], in1=mod3[0].t[:], op=ALU.mult), reads=[xt.b, mod3[0].b], writes=[tmp.b])
                    P.dve(lambda e, hb=hb, tmp=tmp, ss=ss: e.scalar_tensor_tensor(out=hb.t[:], in0=tmp.t[:], scalar=ss.t[:, 0:1], in1=mod3[1].t[:], op0=ALU.mult, op1=ALU.add),
                          reads=[tmp.b, ss.b, mod3[1].b], writes=[hb.b])
                    hbs.append(hb)
                return hbs

            def preB(hbs):
                for ti, hb in enumerate(hbs):
                    transposes([(hb.t[:, k * 128:(k + 1) * 128], 128) for k in range(8)], 7, [hb.b])
                    P.act(lambda e, ti=ti: e.activation(out=h2T.t[:, :, ti * 128:(ti + 1) * 128], in_=psbf(7).rearrange("p (k t) -> p k t", k=8), func=AF.Copy), reads=[PB[7]], writes=[h2T.b])

            def kind_of(tiles):
                return "ctx" if tiles[0] < NC else "lat"

            def p3_group(gi):
                tiles = groups[gi]
                N = 128 * len(tiles)
                kind = kind_of(tiles)
                if m3_loaded[0] != kind:
                    r = NB if kind == "ctx" else b
                    for j in range(3):
                        load_bc(mod3[j], modv[l, 3 + j, r:r + 1, :], [modvB[l]])
                    m3_loaded[0] = kind
                if pre_done[0] != gi:
                    preB(preA(tiles))
                nxt = groups[gi + 1] if gi + 1 < len(groups) and kind_of(groups[gi + 1]) == kind else None
                def p3_chunk(c):
                    sp = spair3[0]
                    spair3[0] = 2 - sp
                    for gu in range(2):
                        col = gu * FH + c * 128
                        for k in range(8):
                            P.pe(lambda e, gu=gu, col=col, k=k, sp=sp: e.matmul(ps[:, sp + gu, 0:N], lhsT=fwin.t[:, k, col:col + 128], rhs=h2T.t[:, k, 0:N], start=(k == 0), stop=(k == 7)),
                                 reads=[fwinB[c // 2], h2T.b], writes=[PB[sp + gu]])
                    sg = sgr.next()
                    P.act(lambda e, sg=sg, sp=sp: e.activation(out=sg.t[:, 0:N], in_=ps[:, sp, 0:N], func=AF.Silu), reads=[PB[sp]], writes=[sg.b])
                    P.dve(lambda e, sg=sg, sp=sp, c=c: e.tensor_tensor(out=actT.t[:, c, 0:N], in0=sg.t[:, 0:N], in1=ps[:, sp + 1, 0:N], op=ALU.mult), reads=[sg.b, PB[sp + 1]], writes=[actT.b])
                for c in range(22):
                    p3_chunk(c)
                def p3_epi(ti, tt):
                    op_ = opair3[0]
                    opair3[0] = 10 - op_
                    for hfc in range(2):
                        for c in range(22):
                            P.pe(lambda e, c=c, hfc=hfc, op_=op_, ti=ti: e.matmul(ps[:, op_ + hfc, :], lhsT=actT.t[:, c, ti * 128:(ti + 1) * 128], rhs=fwout.t[:, c, hfc * 512:(hfc + 1) * 512],
                                                                            start=(c == 0), stop=(c == 21)),
                                 reads=[actT.b, fwoutB[c // 2]], writes=[PB[op_ + hfc]])
                    jk, ss = rings3[0].next(), rings3[1].next()
                    P.act(lambda e, jk=jk, ss=ss, op_=op_: e.activation(out=jk.t[:], in_=ps2(op_), func=AF.Square, accum_out=ss.t[:, 0:1]), reads=[PB[op_], PB[op_ + 1]], writes=[jk.b, ss.b])
                    rstd_from_ss(ss, D)
                    xt = xr3.next()
                    P.dma(lambda e, xt=xt, tt=tt: e.dma_start(out=xt.t[:], in_=x1s[b, tt * 128:(tt + 1) * 128, :]), reads=[x1B[b][tt]], writes=[xt.b])
                    yt = ytr3.next()
                    ggs = mod3[2]
                    P.dve(lambda e, yt=yt, ss=ss, op_=op_, ggs=ggs: e.scalar_tensor_tensor(out=yt.t[:], in0=ps2(op_), scalar=ss.t[:, 0:1], in1=ggs.t[:], op0=ALU.mult, op1=ALU.mult),
                          reads=[PB[op_], PB[op_ + 1], ss.b, ggs.b], writes=[yt.b])
                    x2 = xt
                    P.pool(lambda e, xt=xt, yt=yt: e.tensor_tensor(out=xt.t[:], in0=xt.t[:], in1=yt.t[:], op=ALU.add), reads=[xt.b, yt.b], writes=[xt.b])
                    if l == L - 1:
                        P.dma(lambda e, x2=x2, tt=tt: e.dma_start(out=out[b, (tt - NC) * 128:(tt - NC + 1) * 128, :], in_=x2.t[:]), reads=[x2.b])
                    else:
                        P.dma(lambda e, x2=x2, tt=tt: e.dma_start(out=xres[b, tt * 128:(tt + 1) * 128, :], in_=x2.t[:]), reads=[x2.b], writes=[xresB[b][tt]])
                nhb = preA(nxt) if nxt is not None else None
                for ti, tt in enumerate(tiles):
                    p3_epi(ti, tt)
                if nxt is not None:
                    preB(nhb)
                    pre_done[0] = gi + 1
            for gi in range(len(groups)):
                p3_group(gi)
        for b in range(NB):
            p3_seq(b)
        P.barrier()
        ar.reset(layer_mark)

    for l in range(L):
        do_layer(l)
    P.emit()
    return nc


def rope_tables(S, grid_w=64, theta=10000.0):
    t = np.arange(S)
    rows, cols = (t // grid_w).astype(np.float32), (t % grid_w).astype(np.float32)

    def tab(dh):
        half = dh // 4
        freqs = (theta ** (-np.arange(half, dtype=np.float32) / half)).astype(np.float32)
        ar_, ac_ = rows[:, None] * freqs[None, :], cols[:, None] * freqs[None, :]
        cr, sr, cc, sc = np.cos(ar_), np.sin(ar_), np.cos(ac_), np.sin(ac_)
        cos = np.concatenate([cr, cr, cc, cc], 1)
        sin = np.concatenate([-sr, sr, -sc, sc], 1)
        return np.concatenate([cos, sin], 1).astype(np.float32)
    return tab(64), tab(32)


def const_inputs(S):
    r64, r32 = rope_tables(S)
    k = np.arange(128)[:, None]
    q = np.arange(128)[None, :]
    neg = np.float32(-30000.0)
    mprev = np.where(k >= q, np.float32(0), neg)
    mnext = np.where(k <= q, np.float32(0), neg)
    return {"rope64": r64, "rope32": r32, "ident": np.eye(128, dtype=np.float32),
            "masks": np.stack([mprev, mnext]).astype(np.float32)}


_W_KEYS = ("ada_w", "ada_b", "norm_g", "ffn_w_in", "ffn_w_out", "ev_w_in", "ev_sink", "ev_q_norm", "ev_w_uq",
           "ev_kv_norm", "ev_w_ukv", "ev_w_out", "od_w_in", "od_qk_norm", "od_lambda", "od_subln", "od_w_out")


def run(inputs, n_cores, nb):
    x = np.asarray(inputs["x"], np.float32)
    B, S, _ = x.shape
    ctx = np.asarray(inputs["ctx"], np.float32)
    C = ctx.shape[1]
    L = inputs["ada_w"].shape[0]
    c = np.asarray(inputs["c"], np.float32)
    c_ctx = np.asarray(inputs["c_ctx"], np.float32)
    nc = build_program(S, C, L, nb)
    consts = const_inputs(S)
    shared = {k: np.ascontiguousarray(np.asarray(inputs[k], np.float32)) for k in _W_KEYS}
    in_maps = []
    for i in range(n_cores):
        sl = slice(i * nb, (i + 1) * nb)
        m = {"x": np.ascontiguousarray(x[sl]), "ctx": np.ascontiguousarray(ctx[sl]),
             "cvec": np.ascontiguousarray(np.concatenate([c[sl], c_ctx[None, :]], 0))}
        m.update(shared)
        m.update(consts)
        in_maps.append(m)
    res = run_bass_kernel_spmd(nc, in_maps, core_ids=list(range(n_cores)))
    return np.concatenate([r["out"] for r in res.results], axis=0)


def kernel(**inputs):
    return run(inputs, 8, 2)
```

```python
import contextlib
import math
import numpy as np
import concourse.bass as bass
import concourse.mybir as mybir
from concourse.bass_utils import run_bass_kernel_spmd

F32 = mybir.dt.float32
BF16 = mybir.dt.bfloat16
AF = mybir.ActivationFunctionType
ALU = mybir.AluOpType
AX = mybir.AxisListType

ENGS = ("pe", "act", "dve", "pool", "sp")


class Buf:
    __slots__ = ("name", "w", "r")

    def __init__(self, name=""):
        self.name = name
        self.w = None
        self.r = []


class Op:
    __slots__ = ("eng", "fn", "deps", "sig", "idx", "dma", "dsem", "dval")

    def __init__(self, eng, fn, dma):
        self.eng = eng
        self.fn = fn
        self.deps = []
        self.sig = False
        self.idx = 0
        self.dma = dma
        self.dsem = None
        self.dval = 0


class Prog:
    NDSEM = 48
    NHW = 32

    def __init__(self, nc):
        self.nc = nc
        self.ops = []
        self.ndma = 0
        self.nsw = 0
        self.last = {}
        self.dmas_since = []

    def add(self, eng, fn, reads=(), writes=(), dma=False):
        op = Op(eng, fn, dma)
        deps = {}
        for b in reads:
            if b.w is not None:
                deps[id(b.w)] = b.w
        for b in writes:
            if b.w is not None:
                deps[id(b.w)] = b.w
            for r in b.r:
                deps[id(r)] = r
        for d in deps.values():
            if d is op:
                continue
            if d.eng == "pe" and eng == "pe" and not d.dma and not dma:
                continue
            op.deps.append(d)
            if not d.dma:
                d.sig = True
        for b in writes:
            b.w = op
            b.r = []
        for b in reads:
            if b.w is not op:
                b.r.append(op)
        if dma:
            if eng == "pool":
                op.dsem = self.NHW + self.nsw % (self.NDSEM - self.NHW)
                self.nsw += 1
            else:
                op.dsem = self.ndma % self.NHW
                self.ndma += 1
            self.dmas_since.append(op)
        else:
            self.last[eng] = op
        self.ops.append(op)
        return op

    def pe(self, fn, reads=(), writes=()):
        return self.add("pe", fn, reads, writes)

    def act(self, fn, reads=(), writes=()):
        return self.add("act", fn, reads, writes)

    def dve(self, fn, reads=(), writes=()):
        return self.add("dve", fn, reads, writes)

    def pool(self, fn, reads=(), writes=()):
        return self.add("pool", fn, reads, writes)

    def dma(self, fn, reads=(), writes=(), q="sp"):
        return self.add(q, fn, reads, writes, dma=True)

    def barrier(self):
        lasts = dict(self.last)
        dm = {}
        for d in self.dmas_since:
            dm[d.dsem] = d
        self.dmas_since = []
        for e in ENGS:
            op = Op(e, lambda eng: eng.nop(), False)
            for e2, lo in lasts.items():
                if e2 != e:
                    op.deps.append(lo)
                    lo.sig = True
            op.deps.extend(dm.values())
            self.ops.append(op)
            self.last[e] = op

    def emit(self):
        nc = self.nc
        cnt = {e: 0 for e in ENGS}
        dcount = [0] * self.NDSEM
        for op in self.ops:
            if op.dma:
                dcount[op.dsem] += 16
                op.dval = dcount[op.dsem]
            elif op.sig:
                cnt[op.eng] += 1
                op.idx = cnt[op.eng]
        per = {e: [op for op in self.ops if op.eng == e] for e in ENGS}
        with contextlib.ExitStack() as st:
            esem = {e: st.enter_context(nc.semaphore("s_" + e)) for e in ENGS}
            dsem = [st.enter_context(nc.semaphore("d%d" % i)) for i in range(self.NDSEM)]
            block = st.enter_context(nc.Block())

            def run(ename, eng):
                waited = {}
                for op in per[ename]:
                    for d in op.deps:
                        if d.dma:
                            key, val, sem = ("d", d.dsem), d.dval, dsem[d.dsem]
                        else:
                            key, val, sem = ("e", d.eng), d.idx, esem[d.eng]
                        if waited.get(key, 0) < val:
                            eng.wait_ge(sem, val)
                            waited[key] = val
                    if op.dma:
                        key = ("d", op.dsem)
                        if op.dval > 16 and waited.get(key, 0) < op.dval - 16:
                            eng.wait_ge(dsem[op.dsem], op.dval - 16)
                            waited[key] = op.dval - 16
                        op.fn(eng).then_inc(dsem[op.dsem], 16)
                    else:
                        ins = op.fn(eng)
                        if op.sig:
                            ins.then_inc(esem[ename], 1)
                if ename == "sp":
                    for i in range(self.NDSEM):
                        if dcount[i] > 0 and waited.get(("d", i), 0) < dcount[i]:
                            eng.wait_ge(dsem[i], dcount[i])

            @block.tensor
            def _(eng):
                run("pe", eng)

            @block.scalar
            def _(eng):
                run("act", eng)

            @block.vector
            def _(eng):
                run("dve", eng)

            @block.gpsimd
            def _(eng):
                run("pool", eng)

            @block.sync
            def _(eng):
                run("sp", eng)


class Arena:
    def __init__(self, nc, base, limit):
        self.nc, self.off, self.limit, self.n = nc, base, limit, 0

    def alloc(self, shape, dtype):
        per = 1
        for s in shape[1:]:
            per *= s
        nbytes = per * (4 if dtype == F32 else 2)
        off = (self.off + 31) // 32 * 32
        assert off + nbytes <= self.limit, ("SBUF arena overflow", off, nbytes, self.limit)
        self.off = off + nbytes
        self.n += 1
        return self.nc.alloc_sbuf_tensor_at("t%d" % self.n, list(shape), dtype, offset=off)

    def mark(self):
        return self.off

    def reset(self, m):
        self.off = m


class TB:
    __slots__ = ("t", "b")

    def __init__(self, t):
        self.t = t
        self.b = Buf()


class Ring:
    def __init__(self, arena, n, shape, dtype):
        self.items = [TB(arena.alloc(shape, dtype)) for _ in range(n)]
        self.i = 0

    def next(self):
        it = self.items[self.i % len(self.items)]
        self.i += 1
        return it


DEBUG_STOP = None
DEBUG_ROPE = None
DEBUG_NT = 1
D = 1024
FH = 2816
EPS = 1e-6
EV_IN = 1184
OD_IN = 2304


def lam_init_of(layer):
    return 0.8 - 0.6 * math.exp(-0.3 * layer)


def build_program(S, C, L, NB):
    NL, NC = S // 128, C // 128
    NT = NL + NC
    T = S + C
    NE, NO = (L + 1) // 2, L // 2
    nc = bass.Bass("TRN2", target_bir_lowering=False)

    def din(name, shape):
        return nc.dram_tensor(name, list(shape), F32, kind="ExternalInput").ap()

    x_in = din("x", [NB, S, D])
    ctx_in = din("ctx", [NB, C, D])
    cvec = din("cvec", [NB + 1, D])
    ada_w = din("ada_w", [L, D, 6 * D])
    ada_b = din("ada_b", [L, 6 * D])
    norm_g = din("norm_g", [L, 4, D])
    ffn_w_in = din("ffn_w_in", [L, D, 2 * FH])
    ffn_w_out = din("ffn_w_out", [L, FH, D])
    ev_w_in = din("ev_w_in", [NE, D, EV_IN])
    ev_sink = din("ev_sink", [NE, 8])
    ev_q_norm = din("ev_q_norm", [NE, 256])
    ev_w_uq = din("ev_w_uq", [NE, 256, 768])
    ev_kv_norm = din("ev_kv_norm", [NE, 128])
    ev_w_ukv = din("ev_w_ukv", [NE, 128, 1024])
    ev_w_out = din("ev_w_out", [NE, 1024, D])
    od_w_in = din("od_w_in", [max(NO, 1), D, OD_IN])
    od_qk_norm = din("od_qk_norm", [max(NO, 1), 2, 64])
    od_lambda = din("od_lambda", [max(NO, 1), 4, 64])
    od_subln = din("od_subln", [max(NO, 1), 128])
    od_w_out = din("od_w_out", [max(NO, 1), 1024, D])
    rope64 = din("rope64", [S, 128])
    rope32 = din("rope32", [S, 64])
    ident_in = din("ident", [128, 128])
    masks_in = din("masks", [2, 128, 128])
    out = nc.dram_tensor("out", [NB, S, D], F32, kind="ExternalOutput").ap()
    xres = nc.dram_tensor("xres", [NB, T, D], F32, kind="Internal").ap()
    x1s = nc.dram_tensor("x1s", [NB, T, D], F32, kind="Internal").ap()
    modv = nc.dram_tensor("modv", [L, 6, NB + 1, D], F32, kind="Internal").ap()

    P = Prog(nc)
    ar = Arena(nc, 16640, 229376 - 256)
    ps = nc.alloc_psum_tensor("ps", [128, 8, 512], F32)
    PB = [Buf("ps%d" % i) for i in range(8)]
    xresB = [[Buf() for _ in range(NT)] for _ in range(NB)]
    x1B = [[Buf() for _ in range(NT)] for _ in range(NB)]
    modvB = [Buf() for _ in range(L)]

    def psb(b):
        return ps[:, b, :]

    def psbf(b):
        return ps[:, b, :].bitcast(BF16)

    def ps2(b):
        return ps[:, b:b + 2, :].rearrange("p a b -> p (a b)")

    identb = TB(ar.alloc([128, 128], BF16))
    maskb = TB(ar.alloc([128, 2, 128], BF16))
    onesf = TB(ar.alloc([128, 64], F32))
    epsb = TB(ar.alloc([128, 1], F32))
    oneb = TB(ar.alloc([128, 1], F32))
    P.dma(lambda e: e.dma_start(out=identb.t[:], in_=ident_in[:, :]), writes=[identb.b], q="pool")
    P.dma(lambda e: e.dma_start(out=maskb.t[:], in_=masks_in.rearrange("m k q -> k m q")), writes=[maskb.b], q="pool")
    P.pool(lambda e: e.memset(onesf.t[:], 1.0), writes=[onesf.b])
    P.pool(lambda e: e.memset(epsb.t[:], EPS), writes=[epsb.b])
    P.pool(lambda e: e.memset(oneb.t[:], 1.0), writes=[oneb.b])
    base_mark = ar.mark()

    def rstd_from_ss(ss, n, col=slice(0, 1)):
        P.act(lambda e: e.activation(out=ss.t[:, col], in_=ss.t[:, col], func=AF.Ln, scale=1.0 / n, bias=epsb.t[:]),
              reads=[ss.b, epsb.b], writes=[ss.b])
        P.act(lambda e: e.activation(out=ss.t[:, col], in_=ss.t[:, col], func=AF.Exp, scale=-0.5),
              reads=[ss.b], writes=[ss.b])

    def phase0():
        cT = TB(ar.alloc([128, 8, NB + 1], F32))
        t1 = TB(ar.alloc([128, 8, NB + 1], F32))
        scT = TB(ar.alloc([128, 8, NB + 1], F32))
        R = NB + 1
        for r in range(R):
            P.dma(lambda e, r=r: e.dma_start(out=cT.t[:, :, r], in_=cvec[r, :].rearrange("(k p) -> p k", p=128), allow_slow_non_contiguous=True),
                  writes=[cT.b])
        P.act(lambda e: e.activation(out=t1.t[:], in_=cT.t[:], func=AF.Exp, scale=-1.0), reads=[cT.b], writes=[t1.b])
        P.act(lambda e: e.activation(out=t1.t[:], in_=t1.t[:], func=AF.Ln, bias=oneb.t[:]), reads=[t1.b, oneb.b], writes=[t1.b])
        P.act(lambda e: e.activation(out=t1.t[:], in_=t1.t[:], func=AF.Exp, scale=-1.0), reads=[t1.b], writes=[t1.b])
        P.dve(lambda e: e.tensor_tensor(out=scT.t[:], in0=cT.t[:], in1=t1.t[:], op=ALU.mult), reads=[cT.b, t1.b], writes=[scT.b])
        awr = Ring(ar, 2, [128, 8, 512], F32)
        modrow = TB(ar.alloc([R, 6 * D], F32))
        abt = TB(ar.alloc([R, 6 * D], F32))
        ngt = TB(ar.alloc([R, 4 * D], F32))
        mv = TB(ar.alloc([R, 6, D], F32))
        for l in range(L):
            for n in range(12):
                aw = awr.next()
                P.dma(lambda e, aw=aw, l=l, n=n: e.dma_start(out=aw.t[:], in_=ada_w[l, :, n * 512:(n + 1) * 512].rearrange("(k p) n -> p k n", p=128)),
                      writes=[aw.b])
                bk = n % 2
                for k in range(8):
                    P.pe(lambda e, aw=aw, k=k, bk=bk: e.matmul(ps[0:R, bk, :], lhsT=scT.t[:, k, :], rhs=aw.t[:, k, :], start=(k == 0), stop=(k == 7)),
                         reads=[scT.b, aw.b], writes=[PB[bk]])
                P.act(lambda e, n=n, bk=bk: e.activation(out=modrow.t[:, n * 512:(n + 1) * 512], in_=ps[0:R, bk, :], func=AF.Copy),
                      reads=[PB[bk]], writes=[modrow.b])
            P.dma(lambda e, l=l: e.dma_start(out=abt.t[:], in_=ada_b[l:l + 1, :].partition_broadcast(R)), writes=[abt.b])
            P.dma(lambda e, l=l: e.dma_start(out=ngt.t[:], in_=norm_g[l:l + 1, :, :].rearrange("o j d -> o (j d)").partition_broadcast(R)), writes=[ngt.b])
            P.dve(lambda e: e.tensor_tensor(out=modrow.t[:], in0=modrow.t[:], in1=abt.t[:], op=ALU.add), reads=[modrow.b, abt.b], writes=[modrow.b])

            def seg(i):
                return modrow.t[:, i * D:(i + 1) * D]

            def ng(i):
                return ngt.t[:, i * D:(i + 1) * D]
            P.dve(lambda e: e.scalar_tensor_tensor(out=mv.t[:, 0, :], in0=seg(1), scalar=1.0, in1=ng(0), op0=ALU.add, op1=ALU.mult), reads=[modrow.b, ngt.b], writes=[mv.b])
            P.dve(lambda e: e.tensor_copy(out=mv.t[:, 1, :], in_=seg(0)), reads=[modrow.b], writes=[mv.b])
            P.dve(lambda e: e.tensor_tensor(out=mv.t[:, 2, :], in0=seg(2), in1=ng(1), op=ALU.mult), reads=[modrow.b, ngt.b], writes=[mv.b])
            P.dve(lambda e: e.scalar_tensor_tensor(out=mv.t[:, 3, :], in0=seg(4), scalar=1.0, in1=ng(2), op0=ALU.add, op1=ALU.mult), reads=[modrow.b, ngt.b], writes=[mv.b])
            P.dve(lambda e: e.tensor_copy(out=mv.t[:, 4, :], in_=seg(3)), reads=[modrow.b], writes=[mv.b])
            P.dve(lambda e: e.tensor_tensor(out=mv.t[:, 5, :], in0=seg(5), in1=ng(3), op=ALU.mult), reads=[modrow.b, ngt.b], writes=[mv.b])
            P.dma(lambda e, l=l: e.dma_start(out=modv[l].rearrange("j r d -> r j d"), in_=mv.t[:]), reads=[mv.b], writes=[modvB[l]])

    phase0()
    P.barrier()
    ar.reset(base_mark)
    if DEBUG_STOP == "p0":
        P.emit()
        return nc

    def x_src(l, b, tt):
        if l == 0:
            if tt < NC:
                return ctx_in[b, tt * 128:(tt + 1) * 128, :], []
            return x_in[b, (tt - NC) * 128:(tt - NC + 1) * 128, :], []
        return xres[b, tt * 128:(tt + 1) * 128, :], [xresB[b][tt]]

    def load_bc(dst, src_row, reads=()):
        P.dma(lambda e: e.dma_start(out=dst.t[:], in_=src_row.partition_broadcast(128)), reads=list(reads), writes=[dst.b])

    def transposes(srcs, bank, src_bufs):
        for i, (ap, m) in enumerate(srcs):
            P.pe(lambda e, ap=ap, m=m, i=i: e.transpose(out=psbf(bank)[0:m, i * 128:(i + 1) * 128], in_=ap, identity=identb.t[:]),
                 reads=list(src_bufs) + [identb.b], writes=[PB[bank]])

    def norm_mod_T(xt, gt, sht, hT_dst_ap, hT_buf, rings):
        junk, ssr, tmpr, hbr = rings
        jk, ss, tmp, hb = junk.next(), ssr.next(), tmpr.next(), hbr.next()
        P.act(lambda e: e.activation(out=jk.t[:], in_=xt.t[:], func=AF.Square, accum_out=ss.t[:, 0:1]), reads=[xt.b], writes=[jk.b, ss.b])
        rstd_from_ss(ss, D)
        P.pool(lambda e: e.tensor_tensor(out=tmp.t[:], in0=xt.t[:], in1=gt.t[:], op=ALU.mult), reads=[xt.b, gt.b], writes=[tmp.b])
        P.dve(lambda e: e.scalar_tensor_tensor(out=hb.t[:], in0=tmp.t[:], scalar=ss.t[:, 0:1], in1=sht.t[:], op0=ALU.mult, op1=ALU.add),
              reads=[tmp.b, ss.b, sht.b], writes=[hb.b])
        transposes([(hb.t[:, k * 128:(k + 1) * 128], 128) for k in range(8)], 7, [hb.b])
        P.act(lambda e: e.activation(out=hT_dst_ap, in_=psbf(7).rearrange("p (k t) -> p k t", k=8), func=AF.Copy), reads=[PB[7]], writes=[hT_buf])

    def rope_apply(src_ap, nh, cs, dst_ap, tmps, src_bufs, dst_bufs, dh=64, split=None):
        t1, t2 = tmps
        q = dh // 4
        if DEBUG_ROPE == "copy":
            P.act(lambda e: e.activation(out=dst_ap, in_=(src_ap if split is None else src_ap.rearrange("p (j s) d -> p j s d", j=split)), func=AF.Copy), reads=list(src_bufs), writes=list(dst_bufs))
            return
        cos = cs.t[:, 0:dh].unsqueeze(1).to_broadcast([128, nh, dh])
        x3 = t1.t[:, 0:nh * dh].rearrange("p (h d) -> p h d", h=nh)
        P.act(lambda e: e.activation(out=x3, in_=src_ap, func=AF.Copy), reads=list(src_bufs), writes=[t1.b])
        s5 = t1.t[:, 0:nh * dh].rearrange("p (h b x d) -> p h b x d", h=nh, b=2, x=2)
        o5 = t2.t[:, 0:nh * dh].rearrange("p (h b x d) -> p h b x d", h=nh, b=2, x=2)
        sn = cs.t[:, dh:2 * dh].rearrange("p (b x d) -> p b x d", b=2, x=2)
        for xx in range(2):
            P.dve(lambda e, xx=xx: e.tensor_tensor(out=o5[:, :, :, xx, :], in0=s5[:, :, :, 1 - xx, :],
                                                 in1=sn[:, :, xx, :].unsqueeze(1).to_broadcast([128, nh, 2, q]), op=ALU.mult),
                  reads=[t1.b, cs.b], writes=[t2.b])
        P.dve(lambda e: e.tensor_tensor(out=x3, in0=x3, in1=cos, op=ALU.mult), reads=[t1.b, cs.b], writes=[t1.b])
        if split is None:
            a1 = t1.t[:, 0:nh * dh].rearrange("p (h d) -> p h d", h=nh)
            a2 = t2.t[:, 0:nh * dh].rearrange("p (h d) -> p h d", h=nh)
        else:
            a1 = t1.t[:, 0:nh * dh].rearrange("p (j s d) -> p j s d", j=split, d=dh)
            a2 = t2.t[:, 0:nh * dh].rearrange("p (j s d) -> p j s d", j=split, d=dh)
        if DEBUG_ROPE == "noswap":
            P.act(lambda e: e.activation(out=dst_ap, in_=a1, func=AF.Copy), reads=[t1.b, t2.b], writes=list(dst_bufs))
            return
        fin = P.dve if DEBUG_ROPE == "dvefin" else P.pool
        fin(lambda e: e.tensor_tensor(out=dst_ap, in0=a1, in1=a2, op=ALU.add), reads=[t1.b, t2.b], writes=list(dst_bufs))

    def do_layer(l):
        even = (l % 2 == 0)
        li = l // 2
        need_ctx = l < L - 1
        lam0 = lam_init_of(l)
        layer_mark = ar.mark()
        if even:
            qn_bc = TB(ar.alloc([128, 256], F32))
            kvn_bc = TB(ar.alloc([128, 128], F32))
            sinkx = TB(ar.alloc([128, 8], F32))
            load_bc(qn_bc, ev_q_norm[li:li + 1, :])
            load_bc(kvn_bc, ev_kv_norm[li:li + 1, :])
            load_bc(sinkx, ev_sink[li:li + 1, :])
            P.act(lambda e: e.activation(out=sinkx.t[:], in_=sinkx.t[:], func=AF.Exp), reads=[sinkx.b], writes=[sinkx.b])
        else:
            qkg = TB(ar.alloc([128, 10, 64], F32))
            lamt = TB(ar.alloc([128, 4, 64], F32))
            lamw = TB(ar.alloc([128, 2, 64], F32))
            lams = TB(ar.alloc([128, 4], F32))
            subs = TB(ar.alloc([64, 2], F32))
            for h in range(10):
                P.dma(lambda e, h=h: e.dma_start(out=qkg.t[:, h, :], in_=od_qk_norm[li, (0 if h < 8 else 1):(1 if h < 8 else 2), :].partition_broadcast(128)),
                      writes=[qkg.b])
            P.dma(lambda e: e.dma_start(out=lamt.t[:].rearrange("p a d -> p (a d)"), in_=od_lambda[li:li + 1, :, :].rearrange("o a d -> o (a d)").partition_broadcast(128)),
                  writes=[lamt.b])
            P.dma(lambda e: e.dma_start(out=subs.t[:], in_=od_subln[li, :].rearrange("(h p) -> p h", p=64), allow_slow_non_contiguous=True), writes=[subs.b])
            P.dve(lambda e: e.tensor_tensor(out=lamw.t[:, 0, :], in0=lamt.t[:, 0, :], in1=lamt.t[:, 1, :], op=ALU.mult), reads=[lamt.b], writes=[lamw.b])
            P.dve(lambda e: e.tensor_tensor(out=lamw.t[:, 1, :], in0=lamt.t[:, 2, :], in1=lamt.t[:, 3, :], op=ALU.mult), reads=[lamt.b, lamw.b], writes=[lamw.b])
            P.dve(lambda e: e.tensor_reduce(out=lams.t[:, 0:2], in_=lamw.t[:], axis=AX.X, op=ALU.add), reads=[lamw.b], writes=[lams.b])
            P.act(lambda e: e.activation(out=lams.t[:, 0:2], in_=lams.t[:, 0:2], func=AF.Exp), reads=[lams.b], writes=[lams.b])
            P.dve(lambda e: e.tensor_tensor(out=lams.t[:, 2:3], in0=lams.t[:, 1:2], in1=lams.t[:, 0:1], op=ALU.subtract), reads=[lams.b], writes=[lams.b])
            P.dve(lambda e: e.tensor_scalar(out=lams.t[:, 3:4], in0=lams.t[:, 2:3], scalar1=-lam0, scalar2=None, op0=ALU.add), reads=[lams.b], writes=[lams.b])
            P.dve(lambda e: e.tensor_scalar(out=subs.t[:], in0=subs.t[:], scalar1=1.0 - lam0, scalar2=None, op0=ALU.mult), reads=[subs.b], writes=[subs.b])
        const_mark = ar.mark()

        def do_seq(b):
            P.barrier()
            ar.reset(const_mark)
            if even:
                qaT = ar.alloc([128, 4, T], BF16)
                kaT = ar.alloc([128, T], BF16)
                va = ar.alloc([128, NT, 1, 3, 64], BF16)
                qbT = ar.alloc([128, 8, T], BF16)
                kbT = ar.alloc([128, 8, T], BF16)
                vb = ar.alloc([128, NT, 4, 3, 64], BF16)
                vones = [(va, 2), (vb, 8)]
            else:
                qcT = ar.alloc([128, 4, T], BF16)
                kcT = ar.alloc([128, T], BF16)
                vc = ar.alloc([128, NT, 1, 3, 64], BF16)
                qdT = ar.alloc([128, 4, T], BF16)
                kdT = ar.alloc([128, 4, T], BF16)
                vd = ar.alloc([128, NT, 4, 3, 64], BF16)
                vones = [(vc, 2), (vd, 8)]
            KB = [Buf() for _ in range(NT)]
            VB1 = Buf()
            for (vt, nh) in vones:
                P.pool(lambda e, vt=vt: e.memset(vt[:, :, :, 1, :], 1.0), writes=[VB1])
            qkv_mark = ar.mark()

            g1t = [TB(ar.alloc([128, D], F32)) for _ in range(2)]
            sh1t = [TB(ar.alloc([128, D], F32)) for _ in range(2)]
            for j, r in enumerate((b, NB)):
                load_bc(g1t[j], modv[l, 0, r:r + 1, :], [modvB[l]])
                load_bc(sh1t[j], modv[l, 1, r:r + 1, :], [modvB[l]])
            NIN = EV_IN if even else OD_IN
            win = TB(ar.alloc([128, 8, NIN], BF16))
            wsrc = ev_w_in[li] if even else od_w_in[li]
            for k2 in range(4):
                P.dma(lambda e, k2=k2: e.dma_start(out=win.t[:, 2 * k2:2 * k2 + 2, :], in_=wsrc[k2 * 256:(k2 + 1) * 256, :].rearrange("(k p) n -> p k n", p=128)),
                      writes=[win.b], q="pool")
            if even:
                wuq = TB(ar.alloc([128, 2, 768], BF16))
                wukv = TB(ar.alloc([128, 1024], BF16))
                P.dma(lambda e: e.dma_start(out=wuq.t[:], in_=ev_w_uq[li].rearrange("(k p) n -> p k n", p=128)), writes=[wuq.b], q="pool")
                P.dma(lambda e: e.dma_start(out=wukv.t[:], in_=ev_w_ukv[li]), writes=[wukv.b], q="pool")
            xr = Ring(ar, 2, [128, D], F32)
            rings = (Ring(ar, 1, [128, D], BF16), Ring(ar, 2, [128, 4], F32), Ring(ar, 1, [128, D], F32), Ring(ar, 2, [128, D], BF16))
            hTr = Ring(ar, 2, [128, 8, 128], BF16)
            csr64 = Ring(ar, 2, [128, 128], F32)
            csr32 = Ring(ar, 2, [128, 64], F32)
            rt1 = TB(ar.alloc([128, 512 if even else 1024], F32))
            rt2 = TB(ar.alloc([128, 512 if even else 1024], F32))
            stg = Ring(ar, 2, [128, 1024], BF16)
            ss2 = Ring(ar, 2, [128, 16], F32)
            sqt = TB(ar.alloc([128, 384 if even else 640], F32))
            nrm = Ring(ar, 2, [128, 256], BF16)
            smT = Ring(ar, 2, [128, 2, 128], BF16)
            kper = Ring(ar, 2, [128, 32], BF16)

            def p1_tile(tt):
                isctx = tt < NC
                pt = tt - NC
                tok = slice(tt * 128, (tt + 1) * 128)
                xt = xr.next()
                src, srcb = x_src(l, b, tt)
                P.dma(lambda e, xt=xt, src=src: e.dma_start(out=xt.t[:], in_=src), reads=srcb, writes=[xt.b])
                hT = hTr.next()
                norm_mod_T(xt, g1t[1 if isctx else 0], sh1t[1 if isctx else 0], hT.t[:], hT.b, rings)
                if not isctx:
                    cs64, cs32 = csr64.next(), csr32.next()
                    P.dma(lambda e, cs64=cs64, pt=pt: e.dma_start(out=cs64.t[:], in_=rope64[pt * 128:(pt + 1) * 128, :]), writes=[cs64.b])
                    if even:
                        P.dma(lambda e, cs32=cs32, pt=pt: e.dma_start(out=cs32.t[:], in_=rope32[pt * 128:(pt + 1) * 128, :]), writes=[cs32.b])
                nchunks = (NIN + 511) // 512
                for n in range(nchunks):
                    w = min(512, NIN - n * 512)
                    for k in range(8):
                        P.pe(lambda e, hT=hT, n=n, k=k, w=w: e.matmul(ps[:, n, 0:w], lhsT=hT.t[:, k, :], rhs=win.t[:, k, n * 512:n * 512 + w], start=(k == 0), stop=(k == 7)),
                             reads=[hT.b, win.b], writes=[PB[n]])
                flat = ps[:, 0:5, :].rearrange("p a b -> p (a b)")
                st = stg.next()
                if even:
                    qk_src = flat[:, 0:640].rearrange("p (h d) -> p h d", h=10)
                    dq = st.t[:, 0:512].rearrange("p (s j d) -> p j s d", s=4, j=2)
                    dk = st.t[:, 512:640].rearrange("p (h d) -> p h d", h=2)
                    if isctx:
                        P.act(lambda e, dq=dq, qk_src=qk_src: e.activation(out=dq, in_=qk_src[:, 0:8, :].rearrange("p (j s) d -> p j s d", j=2), func=AF.Copy), reads=[PB[0]], writes=[st.b])
                        P.act(lambda e, dk=dk, qk_src=qk_src: e.activation(out=dk, in_=qk_src[:, 8:10, :], func=AF.Copy), reads=[PB[1]], writes=[st.b])
                    else:
                        rope_apply(qk_src[:, 0:8, :], 8, cs64, dq, (rt1, rt2), [PB[0]], [st.b], split=2)
                        rope_apply(qk_src[:, 8:10, :], 2, cs64, dk, (rt1, rt2), [PB[1]], [st.b])
                    transposes([(st.t[:, s * 128:(s + 1) * 128], 128) for s in range(5)], 7, [st.b])
                    P.act(lambda e, tok=tok: e.activation(out=qaT[:, :, tok], in_=psbf(7)[:, 0:512].rearrange("p (s t) -> p s t", s=4), func=AF.Copy), reads=[PB[7]], writes=[KB[tt]])
                    P.act(lambda e, tok=tok: e.activation(out=kaT[:, tok], in_=psbf(7)[:, 512:640], func=AF.Copy), reads=[PB[7]], writes=[KB[tt]])
                    P.act(lambda e, tt=tt: e.activation(out=va[:, tt, :, 0:3:2, :], in_=flat[:, 640:768].rearrange("p (a e d) -> p a e d", a=1, e=2), func=AF.Copy), reads=[PB[1]], writes=[KB[tt]])
                    ss = ss2.next()
                    P.act(lambda e, ss=ss: e.activation(out=sqt.t[:, 0:256], in_=flat[:, 768:1024], func=AF.Square, accum_out=ss.t[:, 0:1]), reads=[PB[1]], writes=[sqt.b, ss.b])
                    P.act(lambda e, ss=ss: e.activation(out=sqt.t[:, 256:384], in_=flat[:, 1024:1152], func=AF.Square, accum_out=ss.t[:, 1:2]), reads=[PB[2]], writes=[sqt.b, ss.b])
                    rstd_from_ss(ss, 256, slice(0, 1))
                    rstd_from_ss(ss, 128, slice(1, 2))
                    nq = nrm.next()
                    P.dve(lambda e, nq=nq, ss=ss: e.scalar_tensor_tensor(out=nq.t[:, 0:256], in0=flat[:, 768:1024], scalar=ss.t[:, 0:1], in1=qn_bc.t[:], op0=ALU.mult, op1=ALU.mult),
                          reads=[PB[1], ss.b, qn_bc.b], writes=[nq.b])
                    transposes([(nq.t[:, k * 128:(k + 1) * 128], 128) for k in range(2)], 7, [nq.b])
                    cqT = smT.next()
                    P.act(lambda e, cqT=cqT: e.activation(out=cqT.t[:], in_=psbf(7)[:, 0:256].rearrange("p (k t) -> p k t", k=2), func=AF.Copy), reads=[PB[7]], writes=[cqT.b])
                    for n, (c0, w) in enumerate(((0, 512), (512, 256))):
                        for k in range(2):
                            P.pe(lambda e, cqT=cqT, n=n, k=k, c0=c0, w=w: e.matmul(ps[:, 3 + n, 0:w], lhsT=cqT.t[:, k, :], rhs=wuq.t[:, k, c0:c0 + w], start=(k == 0), stop=(k == 1)),
                                 reads=[cqT.b, wuq.b], writes=[PB[3 + n]])
                    qb_src = ps[:, 3:5, :].rearrange("p a b -> p (a b)")[:, 0:768].rearrange("p (h d) -> p h d", h=8)
                    st2 = stg.next()
                    qb_dst = st2.t[:, 0:768].rearrange("p (h d) -> p h d", h=8)
                    if isctx:
                        P.act(lambda e, qb_dst=qb_dst, qb_src=qb_src: e.activation(out=qb_dst, in_=qb_src, func=AF.Copy), reads=[PB[3], PB[4]], writes=[st2.b])
                    else:
                        P.act(lambda e, qb_dst=qb_dst, qb_src=qb_src: e.activation(out=qb_dst[:, :, 0:64], in_=qb_src[:, :, 0:64], func=AF.Copy), reads=[PB[3], PB[4]], writes=[st2.b])
                        rope_apply(qb_src[:, :, 64:96], 8, cs32, qb_dst[:, :, 64:96], (rt1, rt2), [PB[3], PB[4]], [st2.b], dh=32)
                    transposes([(st2.t[:, h * 96:(h + 1) * 96], 96) for h in range(8)], 7, [st2.b])
                    P.act(lambda e, tok=tok: e.activation(out=qbT[0:96, :, tok], in_=psbf(7)[0:96, :].rearrange("p (h t) -> p h t", h=8), func=AF.Copy), reads=[PB[7]], writes=[KB[tt]])
                    nk = nrm.next()
                    P.dve(lambda e, nk=nk, ss=ss: e.scalar_tensor_tensor(out=nk.t[:, 0:128], in0=flat[:, 1024:1152], scalar=ss.t[:, 1:2], in1=kvn_bc.t[:], op0=ALU.mult, op1=ALU.mult),
                          reads=[PB[2], ss.b, kvn_bc.b], writes=[nk.b])
                    transposes([(nk.t[:, 0:128], 128)], 7, [nk.b])
                    ckT = smT.next()
                    P.act(lambda e, ckT=ckT: e.activation(out=ckT.t[:, 0, :], in_=psbf(7)[:, 0:128], func=AF.Copy), reads=[PB[7]], writes=[ckT.b])
                    for n in range(2):
                        P.pe(lambda e, ckT=ckT, n=n: e.matmul(ps[:, 5 + n, :], lhsT=ckT.t[:, 0, :], rhs=wukv.t[:, n * 512:(n + 1) * 512], start=True, stop=True),
                             reads=[ckT.b, wukv.b], writes=[PB[5 + n]])
                    kv_src = ps[:, 5:7, :].rearrange("p a b -> p (a b)").rearrange("p (h d) -> p h d", h=8)
                    st3 = stg.next()
                    kb_dst = st3.t[:, 0:768].rearrange("p (h d) -> p h d", h=8)
                    P.act(lambda e, kb_dst=kb_dst, kv_src=kv_src: e.activation(out=kb_dst[:, :, 0:64], in_=kv_src[:, :, 0:64], func=AF.Copy), reads=[PB[5], PB[6]], writes=[st3.b])
                    P.act(lambda e, tt=tt, kv_src=kv_src: e.activation(out=vb[:, tt, :, 0:3:2, :], in_=kv_src[:, :, 64:128].rearrange("p (a e) d -> p a e d", e=2), func=AF.Copy), reads=[PB[5], PB[6]], writes=[KB[tt]])
                    kp = kper.next()
                    kpe_src = flat[:, 1152:1184].rearrange("p (h d) -> p h d", h=1)
                    if isctx:
                        P.act(lambda e, kp=kp, kpe_src=kpe_src: e.activation(out=kp.t[:].rearrange("p (h d) -> p h d", h=1), in_=kpe_src, func=AF.Copy), reads=[PB[2]], writes=[kp.b])
                    else:
                        rope_apply(kpe_src, 1, cs32, kp.t[:].rearrange("p (h d) -> p h d", h=1), (rt1, rt2), [PB[2]], [kp.b], dh=32)
                    P.pool(lambda e, kp=kp, kb_dst=kb_dst: e.tensor_copy(out=kb_dst[:, :, 64:96], in_=kp.t[:].unsqueeze(1).to_broadcast([128, 8, 32])), reads=[kp.b], writes=[st3.b])
                    transposes([(st3.t[:, h * 96:(h + 1) * 96], 96) for h in range(8)], 7, [st3.b])
                    P.act(lambda e, tok=tok: e.activation(out=kbT[0:96, :, tok], in_=psbf(7)[0:96, :].rearrange("p (h t) -> p h t", h=8), func=AF.Copy), reads=[PB[7]], writes=[KB[tt]])
                else:
                    qk_src = flat[:, 0:640].rearrange("p (h d) -> p h d", h=10)
                    ss = ss2.next()
                    P.act(lambda e: e.activation(out=sqt.t[:], in_=flat[:, 0:640], func=AF.Square), reads=[PB[0], PB[1]], writes=[sqt.b])
                    P.dve(lambda e, ss=ss: e.tensor_reduce(out=ss.t[:, 0:10], in_=sqt.t[:].rearrange("p (h d) -> p h d", h=10), axis=AX.X, op=ALU.add), reads=[sqt.b], writes=[ss.b])
                    rstd_from_ss(ss, 64, slice(0, 10))
                    xn = rt1.t[:, 0:640].rearrange("p (h d) -> p h d", h=10)
                    xs = rt2.t[:, 0:640].rearrange("p (h d) -> p h d", h=10)
                    P.act(lambda e, qk_src=qk_src, xs=xs: e.activation(out=xs, in_=qk_src, func=AF.Copy), reads=[PB[0], PB[1]], writes=[rt2.b])
                    P.dve(lambda e, ss=ss, xs=xs, xn=xn: e.tensor_tensor(out=xn, in0=xs, in1=ss.t[:, 0:10].unsqueeze(2).to_broadcast([128, 10, 64]), op=ALU.mult),
                          reads=[rt2.b, ss.b], writes=[rt1.b])
                    xg = sqt.t[:].rearrange("p (h d) -> p h d", h=10)
                    P.pool(lambda e, xn=xn, xg=xg: e.tensor_tensor(out=xg, in0=xn, in1=qkg.t[:], op=ALU.mult), reads=[rt1.b, qkg.b], writes=[sqt.b])
                    dq = st.t[:, 0:512].rearrange("p (s j d) -> p j s d", s=4, j=2)
                    dk = st.t[:, 512:640].rearrange("p (h d) -> p h d", h=2)
                    if isctx:
                        P.pool(lambda e, dq=dq, xg=xg: e.tensor_copy(out=dq, in_=xg[:, 0:8, :].rearrange("p (j s) d -> p j s d", j=2)), reads=[sqt.b], writes=[st.b])
                        P.pool(lambda e, dk=dk, xg=xg: e.tensor_copy(out=dk, in_=xg[:, 8:10, :]), reads=[sqt.b], writes=[st.b])
                    else:
                        rope_apply(xg[:, 0:8, :], 8, cs64, dq, (rt1, rt2), [sqt.b], [st.b], split=2)
                        rope_apply(xg[:, 8:10, :], 2, cs64, dk, (rt1, rt2), [sqt.b], [st.b])
                    transposes([(st.t[:, s * 128:(s + 1) * 128], 128) for s in range(5)], 7, [st.b])
                    P.act(lambda e, tok=tok: e.activation(out=qcT[:, :, tok], in_=psbf(7)[:, 0:512].rearrange("p (s t) -> p s t", s=4), func=AF.Copy), reads=[PB[7]], writes=[KB[tt]])
                    P.act(lambda e, tok=tok: e.activation(out=kcT[:, tok], in_=psbf(7)[:, 512:640], func=AF.Copy), reads=[PB[7]], writes=[KB[tt]])
                    P.act(lambda e, tt=tt: e.activation(out=vc[:, tt, :, 0:3:2, :], in_=flat[:, 640:768].rearrange("p (a e d) -> p a e d", a=1, e=2), func=AF.Copy), reads=[PB[1]], writes=[KB[tt]])
                    st2 = stg.next()
                    qd_src = flat[:, 768:1792].rearrange("p (h d) -> p h d", h=16)
                    qd_dst = st2.t[:].rearrange("p (h d) -> p h d", h=16)
                    if isctx:
                        P.act(lambda e, qd_dst=qd_dst, qd_src=qd_src: e.activation(out=qd_dst, in_=qd_src, func=AF.Copy), reads=[PB[1], PB[2], PB[3]], writes=[st2.b])
                    else:
                        rope_apply(qd_src, 16, cs64, qd_dst, (rt1, rt2), [PB[1], PB[2], PB[3]], [st2.b])
                    transposes([(st2.t[:, s * 128:(s + 1) * 128], 128) for s in range(8)], 7, [st2.b])
                    P.act(lambda e, tok=tok: e.activation(out=qdT[:, :, tok], in_=psbf(7)[:, 0:512].rearrange("p (s t) -> p s t", s=4), func=AF.Copy), reads=[PB[7]], writes=[KB[tt]])
                    P.act(lambda e, tok=tok: e.activation(out=kdT[:, :, tok], in_=psbf(7)[:, 512:1024].rearrange("p (s t) -> p s t", s=4), func=AF.Copy), reads=[PB[7]], writes=[KB[tt]])
                    P.act(lambda e, tt=tt: e.activation(out=vd[:, tt, :, 0:3:2, :], in_=flat[:, 1792:2304].rearrange("p (a e d) -> p a e d", a=4, e=2), func=AF.Copy), reads=[PB[3], PB[4]], writes=[KB[tt]])

            for tt in range(NT if DEBUG_STOP != "p1a" else DEBUG_NT):
                p1_tile(tt)
            if DEBUG_STOP in ("p1", "p1a"):
                return
            P.barrier()
            ar.reset(qkv_mark)
            wout = TB(ar.alloc([128, 8, D], BF16))
            wosrc = ev_w_out[li] if even else od_w_out[li]
            for c4 in range(4):
                P.dma(lambda e, c4=c4: e.dma_start(out=wout.t[:, 2 * c4:2 * c4 + 2, :], in_=wosrc[c4 * 256:(c4 + 1) * 256, :].rearrange("(c p) n -> p c n", p=128)),
                      writes=[wout.b], q="pool")
            gg1 = TB(ar.alloc([128, D], F32))
            ptr = Ring(ar, 3, [128, 2, 512], BF16)
            osbr = Ring(ar, 4, [128, 512], F32)
            oT = TB(ar.alloc([128, 8, 512], BF16))

            def oTa(hc, N):
                return oT.t[(hc % 2) * 64:(hc % 2) * 64 + 64, hc // 2, 0:N]
            xr2 = Ring(ar, 2, [128, D], F32)
            ytr = Ring(ar, 2, [128, D], F32)
            x1r = Ring(ar, 2, [128, D], F32)
            jk2 = Ring(ar, 1, [128, D], BF16)
            ss3 = Ring(ar, 2, [128, 4], F32)
            if not even:
                odr = Ring(ar, 8, [64, 512], F32)
                sqr = Ring(ar, 2, [64, 512], BF16)
                onesb = TB(ar.alloc([64, 64], BF16))
                P.pool(lambda e: e.memset(onesb.t[:], 1.0), writes=[onesb.b])
                rsd = Ring(ar, 3, [64, 512], F32)
            obank = [4]
            spair = [0]

            def next_obank():
                bk = obank[0]
                obank[0] = 4 + (bk - 4 + 1) % 4
                return bk

            def next_spair():
                s = spair[0]
                spair[0] = 2 - s
                return s

            def vop(vt, t, slot):
                p_, e_ = slot // 2, slot % 2
                return vt[:, t, p_, e_:e_ + 2, :].rearrange("p a d -> p (a d)")

            def normalise(bk, N, dst_ap, dst_bufs, sink_h=None, keep=None, e_=0):
                vs = slice(e_ * 64, e_ * 64 + 64)
                ds = slice((1 - e_) * 64, (1 - e_) * 64 + 64)
                r = osbr.next()
                if sink_h is not None:
                    P.dve(lambda e: e.tensor_scalar(out=r.t[ds, 0:N], in0=ps[ds, bk, 0:N], scalar1=sinkx.t[ds, sink_h:sink_h + 1], scalar2=None, op0=ALU.add),
                          reads=[PB[bk], sinkx.b], writes=[r.b])
                    P.dve(lambda e: e.reciprocal(out=r.t[ds, 0:N], in_=r.t[ds, 0:N]), reads=[r.b], writes=[r.b])
                else:
                    P.dve(lambda e: e.reciprocal(out=r.t[ds, 0:N], in_=ps[ds, bk, 0:N]), reads=[PB[bk]], writes=[r.b])
                if keep is None:
                    P.dve(lambda e: e.tensor_tensor(out=dst_ap, in0=ps[vs, bk, 0:N], in1=r.t[ds, 0:N], op=ALU.mult), reads=[r.b, PB[bk]], writes=list(dst_bufs))
                else:
                    P.dve(lambda e: e.tensor_tensor(out=keep.t[:, 0:N], in0=ps[vs, bk, 0:N], in1=r.t[ds, 0:N], op=ALU.mult), reads=[r.b, PB[bk]], writes=[keep.b])

            def attend_jobs(jobs, units, ktiles, N, scale, post, qB):
                if len(units) == 1:
                    u = units[0]
                    jl = [[(u, t)] for t in ktiles]
                    steps = [jl[i] + (jl[i + 1] if i + 1 < len(jl) else []) for i in range(0, len(jl), 2)]
                else:
                    steps = [[(units[0], t), (units[1], t)] for t in ktiles]
                banks = {}
                for u in units:
                    for vi in range(len(u["V"])):
                        banks[(id(u), vi)] = next_obank()
                first = {k: True for k in banks}
                last_step = len(steps) - 1
                single = len(units) == 1
                for si, step in enumerate(steps):
                    sp = next_spair()
                    pt = ptr.next()
                    nj = len(step)

                    def S(step=step, sp=sp):
                        for j, (u, t) in enumerate(step):
                            P.pe(lambda e, u=u, t=t, j=j: e.matmul(ps[:, sp + j, 0:N], lhsT=u["kT"](t), rhs=u["qT"], start=True, stop=True),
                                 reads=[KB[t]] + qB, writes=[PB[sp + j]])

                    def E(sp=sp, pt=pt, nj=nj):
                        P.act(lambda e: e.activation(out=pt.t[:, 0:nj, 0:N], in_=ps[:, sp:sp + nj, 0:N], func=AF.Exp, scale=scale),
                              reads=[PB[sp + j] for j in range(nj)], writes=[pt.b])

                    def V(step=step, pt=pt, si=si, nj=nj):
                        for j, (u, t) in enumerate(step):
                            for vi, vfn in enumerate(u["V"]):
                                key = (id(u), vi)
                                bk = banks[key]
                                lastc = (si == last_step and j == nj - 1) if single else (si == last_step)
                                P.pe(lambda e, j=j, t=t, vfn=vfn, bk=bk, st_=first[key], lastc=lastc: e.matmul(ps[:, bk, 0:N], lhsT=vfn(t), rhs=pt.t[:, j, 0:N], start=st_, stop=lastc),
                                     reads=[KB[t], VB1, pt.b], writes=[PB[bk]])
                                first[key] = False
                    jobs.append(dict(S=S, E=E, V=V, post=None, obanks=set(banks.values())))
                jobs[-1]["post"] = lambda: post(banks, sp)

            def run_jobs(jobs):
                n = len(jobs)
                pend = None
                later = []

                def do_post(p, i):
                    c = p()
                    if c is not None:
                        later.append((i + 3, c))
                if n:
                    jobs[0]["S"]()
                for i in range(n):
                    if i + 1 < n:
                        jobs[i + 1]["S"]()
                    jobs[i]["E"]()
                    jobs[i]["V"]()
                    for (due, c) in [x for x in later if x[0] <= i]:
                        c()
                    later[:] = [x for x in later if x[0] > i]
                    if pend is not None:
                        do_post(pend, i)
                        pend = None
                    p = jobs[i]["post"]
                    if p is not None:
                        if i + 1 < n and not (jobs[i]["obanks"] & jobs[i + 1]["obanks"]):
                            pend = p
                        else:
                            do_post(p, i)
                if pend is not None:
                    do_post(pend, n)
                for (due, c) in later:
                    c()

            groups = []
            if need_ctx:
                groups.append(("ctx", list(range(NC))))
            for g in range(NL // 4):
                groups.append(("lat", [NC + 4 * g + j for j in range(4)]))

            gg_loaded = [None]
            def p2_group(gkind, qtiles):
                if gg_loaded[0] != gkind:
                    r = NB if gkind == "ctx" else b
                    load_bc(gg1, modv[l, 2, r:r + 1, :], [modvB[l]])
                    gg_loaded[0] = gkind
                N = 128 * len(qtiles)
                q0 = qtiles[0] * 128
                qs = slice(q0, q0 + N)
                qB = [KB[t] for t in qtiles]
                ktiles = list(range(NC)) if gkind == "ctx" else list(range(NT))
                jobs = []
                if even:
                    def a_head(h):
                        s_, j_ = h % 4, h // 4
                        pr = slice(j_ * 64, (j_ + 1) * 64)
                        bk = next_obank()
                        jobs = []

                        def a_qtile(qi, qt):
                            lt = qt - NC
                            kl = [(t, None) for t in range(NC)]
                            if lt > 0:
                                kl.append((qt - 1, 0))
                            kl.append((qt, None))
                            if lt < NL - 1:
                                kl.append((qt + 1, 1))
                            qcols = slice(qt * 128, (qt + 1) * 128)
                            nk = len(kl)
                            st = {}

                            def S():
                                sp = st["sp"] = next_spair()
                                pt = st["pt"] = ptr.next()
                                st["ptv"] = pt.t[:].rearrange("p a (c n) -> p (a c) n", n=128)
                                st["psv"] = ps[:, sp:sp + 2, :].rearrange("p a (c n) -> p (a c) n", n=128)
                                for ci, (t, mk) in enumerate(kl):
                                    bnk, col = sp + ci // 4, (ci % 4) * 128
                                    P.pe(lambda e, t=t, bnk=bnk, col=col, mk=mk: e.matmul(
                                        ps[:, bnk, col:col + 128], lhsT=kaT[pr, t * 128:(t + 1) * 128], rhs=qaT[pr, s_, qcols], start=True, stop=(mk is None)),
                                        reads=[KB[t], KB[qt]], writes=[PB[bnk]])
                                    if mk is not None:
                                        P.pe(lambda e, bnk=bnk, col=col, mk=mk: e.matmul(ps[:, bnk, col:col + 128], lhsT=identb.t[:], rhs=maskb.t[:, mk, :], start=False, stop=True),
                                             reads=[identb.b, maskb.b], writes=[PB[bnk]])

                            def E():
                                sp, pt, ptv, psv = st["sp"], st["pt"], st["ptv"], st["psv"]
                                P.act(lambda e: e.activation(out=ptv[:, 0:nk, :], in_=psv[:, 0:nk, :], func=AF.Exp, scale=0.125),
                                      reads=[PB[sp], PB[sp + 1]], writes=[pt.b])

                            def V():
                                pt, ptv = st["pt"], st["ptv"]
                                for ci, (t, mk) in enumerate(kl):
                                    P.pe(lambda e, ci=ci, t=t: e.matmul(ps[:, bk, qi * 128:(qi + 1) * 128], lhsT=vop(va, t, j_), rhs=ptv[:, ci, :],
                                                                       start=(ci == 0), stop=(ci == nk - 1)),
                                         reads=[KB[t], VB1, pt.b], writes=[PB[bk]])
                            jobs.append(dict(S=S, E=E, V=V, post=None, obanks={bk}))
                        for qi, qt in enumerate(qtiles):
                            a_qtile(qi, qt)
                        jobs[-1]["post"] = lambda: normalise(bk, N, oTa(h, N), [oT.b], sink_h=h, e_=j_)
                        return jobs

                    def a_ctx_pair(s_):
                        us = []
                        for j_ in range(2):
                            pr = slice(j_ * 64, (j_ + 1) * 64)
                            us.append(dict(kT=lambda t, pr=pr: kaT[pr, t * 128:(t + 1) * 128], qT=qaT[pr, s_, qs], V=[lambda t, j_=j_: vop(va, t, j_)]))

                        def post(banks, sp_):
                            for j_ in range(2):
                                normalise(banks[(id(us[j_]), 0)], N, oTa(s_ + 4 * j_, N), [oT.b], sink_h=s_ + 4 * j_, e_=j_)
                        attend_jobs(jobs, us, ktiles, N, 0.125, post, qB)
                    for s_ in range(4):
                        if gkind == "ctx":
                            a_ctx_pair(s_)
                        else:
                            ja, jb = a_head(s_), a_head(s_ + 4)
                            for x_, y_ in zip(ja, jb):
                                jobs.append(x_)
                                jobs.append(y_)

                    def b_head(h):
                        u = dict(kT=lambda t: kbT[0:96, h, t * 128:(t + 1) * 128], qT=qbT[0:96, h, qs], V=[lambda t: vop(vb, t, h)])
                        attend_jobs(jobs, [u], ktiles, N, 96 ** -0.5, lambda banks, sp_: normalise(banks[(id(u), 0)], N, oTa(8 + h, N), [oT.b], e_=h % 2), qB)
                    for h in range(8):
                        b_head(h)
                else:
                    def c_pair(s_):
                        us = []
                        for j_ in range(2):
                            pr = slice(j_ * 64, (j_ + 1) * 64)
                            us.append(dict(kT=lambda t, pr=pr: kcT[pr, t * 128:(t + 1) * 128], qT=qcT[pr, s_, qs], V=[lambda t, j_=j_: vop(vc, t, j_)]))

                        def post(banks, sp_):
                            for j_ in range(2):
                                normalise(banks[(id(us[j_]), 0)], N, oTa(s_ + 4 * j_, N), [oT.b], e_=j_)
                        attend_jobs(jobs, us, ktiles, N, 0.125, post, qB)
                    for s_ in range(4):
                        c_pair(s_)

                    def d_head(h):
                        us = []
                        for i in range(2):
                            pr = slice(i * 64, (i + 1) * 64)
                            us.append(dict(kT=lambda t, pr=pr: kdT[pr, h, t * 128:(t + 1) * 128], qT=qdT[pr, h, qs],
                                           V=[lambda t, hf=hf: vop(vd, t, 2 * h + hf) for hf in range(2)]))

                        def d_post(banks, sp):
                            ods = []
                            rr = []
                            for i in range(2):
                                r = osbr.next()
                                bk0 = banks[(id(us[i]), 0)]
                                if i == 0:
                                    P.dve(lambda e, r=r, bk0=bk0: e.reciprocal(out=r.t[64:128, 0:N], in_=ps[64:128, bk0, 0:N]), reads=[PB[bk0]], writes=[r.b])
                                else:
                                    P.act(lambda e, r=r, bk0=bk0: e.activation(out=r.t[64:128, 0:N], in_=ps[64:128, bk0, 0:N], func=AF.Ln), reads=[PB[bk0]], writes=[r.b])
                                    P.act(lambda e, r=r: e.activation(out=r.t[64:128, 0:N], in_=r.t[64:128, 0:N], func=AF.Exp, scale=-1.0), reads=[r.b], writes=[r.b])
                                rr.append(r)
                            oo = [[odr.next(), odr.next()] for _ in range(2)]
                            for hf in range(2):
                                vs = slice(hf * 64, hf * 64 + 64)
                                for i in range(2):
                                    bk_ = banks[(id(us[i]), hf)]
                                    P.dve(lambda e, o=oo[hf][i], r=rr[i], bk_=bk_, vs=vs: e.tensor_tensor(out=o.t[:, 0:N], in0=ps[vs, bk_, 0:N], in1=r.t[64:128, 0:N], op=ALU.mult),
                                          reads=[rr[i].b, PB[bk_]], writes=[oo[hf][i].b])
                            for hf in range(2):
                                o1, o2 = oo[hf]
                                P.dve(lambda e, o1=o1, o2=o2: e.scalar_tensor_tensor(out=o1.t[:, 0:N], in0=o2.t[:, 0:N], scalar=lams.t[0:64, 3:4], in1=o1.t[:, 0:N], op0=ALU.mult, op1=ALU.add),
                                      reads=[o1.b, o2.b, lams.b], writes=[o1.b])
                                ods.append(o1)
                            for hf in range(2):
                                sq = sqr.next()
                                P.pool(lambda e, sq=sq, o=ods[hf]: e.tensor_tensor(out=sq.t[:, 0:N], in0=o.t[:, 0:N], in1=o.t[:, 0:N], op=ALU.mult), reads=[ods[hf].b], writes=[sq.b])
                                P.pe(lambda e, sq=sq, hf=hf: e.matmul(ps[0:64, sp, 0:N], lhsT=onesb.t[:], rhs=sq.t[:, 0:N], start=(hf == 0), stop=(hf == 1)),
                                     reads=[onesb.b, sq.b], writes=[PB[sp]])
                            rs = rsd.next()
                            P.dve(lambda e: e.tensor_copy(out=rs.t[:, 0:N], in_=ps[0:64, sp, 0:N]), reads=[PB[sp]], writes=[rs.b])

                            def stage2():
                                P.act(lambda e: e.activation(out=rs.t[:, 0:N], in_=rs.t[:, 0:N], func=AF.Ln, scale=1.0 / 128, bias=epsb.t[0:64, :]), reads=[rs.b, epsb.b], writes=[rs.b])
                                P.act(lambda e: e.activation(out=rs.t[:, 0:N], in_=rs.t[:, 0:N], func=AF.Exp, scale=-0.5), reads=[rs.b], writes=[rs.b])
                                for hf in range(2):
                                    P.dve(lambda e, hf=hf, o=ods[hf]: e.scalar_tensor_tensor(out=oTa(8 + 2 * h + hf, N), in0=o.t[:, 0:N], scalar=subs.t[:, hf:hf + 1], in1=rs.t[:, 0:N],
                                                                                          op0=ALU.mult, op1=ALU.mult),
                                          reads=[ods[hf].b, subs.b, rs.b], writes=[oT.b])
                            return stage2
                        attend_jobs(jobs, us, ktiles, N, 0.125, d_post, qB)
                    for h in range(4):
                        d_head(h)
                run_jobs(jobs)

                def outproj(qi, qt):
                    sp = next_spair()
                    for hfc in range(2):
                        for hc in range(8):
                            P.pe(lambda e, hc=hc, hfc=hfc, sp=sp, qi=qi: e.matmul(ps[:, sp + hfc, :], lhsT=oT.t[:, hc, qi * 128:(qi + 1) * 128], rhs=wout.t[:, hc, hfc * 512:(hfc + 1) * 512],
                                                                            start=(hc == 0), stop=(hc == 7)),
                                 reads=[oT.b, wout.b], writes=[PB[sp + hfc]])
                    jk, ss = jk2.next(), ss3.next()
                    P.act(lambda e, jk=jk, ss=ss, sp=sp: e.activation(out=jk.t[:], in_=ps2(sp), func=AF.Square, accum_out=ss.t[:, 0:1]), reads=[PB[sp], PB[sp + 1]], writes=[jk.b, ss.b])
                    rstd_from_ss(ss, D)
                    xt = xr2.next()
                    src, srcb = x_src(l, b, qt)
                    P.dma(lambda e, xt=xt, src=src: e.dma_start(out=xt.t[:], in_=src), reads=srcb, writes=[xt.b])
                    yt = ytr.next()
                    ggs = gg1
                    P.dve(lambda e, yt=yt, ss=ss, sp=sp, ggs=ggs: e.scalar_tensor_tensor(out=yt.t[:], in0=ps2(sp), scalar=ss.t[:, 0:1], in1=ggs.t[:], op0=ALU.mult, op1=ALU.mult),
                          reads=[PB[sp], PB[sp + 1], ss.b, ggs.b], writes=[yt.b])
                    x1 = x1r.next()
                    P.pool(lambda e, x1=x1, xt=xt, yt=yt: e.tensor_tensor(out=x1.t[:], in0=xt.t[:], in1=yt.t[:], op=ALU.add), reads=[xt.b, yt.b], writes=[x1.b])
                    P.dma(lambda e, x1=x1, qt=qt: e.dma_start(out=x1s[b, qt * 128:(qt + 1) * 128, :], in_=x1.t[:]), reads=[x1.b], writes=[x1B[b][qt]], q="pool")

                for qi, qt in enumerate(qtiles):
                    outproj(qi, qt)
            for gkind, qtiles in groups:
                p2_group(gkind, qtiles)
        for b in range(NB if DEBUG_STOP != "p1a" else 1):
            do_seq(b)
        if DEBUG_STOP in ("p1", "p1a", "p2"):
            return
        P.barrier()
        ar.reset(const_mark)
        fwin = TB(ar.alloc([128, 8, 2 * FH], BF16))
        fwout = TB(ar.alloc([128, 22, D], BF16))
        fwinB = [Buf() for _ in range(11)]
        fwoutB = [Buf() for _ in range(11)]
        for c2 in range(11):
            for gu in range(2):
                col = gu * FH + c2 * 256
                P.dma(lambda e, col=col: e.dma_start(out=fwin.t[:, :, col:col + 256], in_=ffn_w_in[l, :, col:col + 256].rearrange("(k p) n -> p k n", p=128)),
                      writes=[fwinB[c2]], q="pool")
        for c2 in range(11):
            P.dma(lambda e, c2=c2: e.dma_start(out=fwout.t[:, 2 * c2:2 * c2 + 2, :], in_=ffn_w_out[l, c2 * 256:(c2 + 1) * 256, :].rearrange("(c p) n -> p c n", p=128)),
                  writes=[fwoutB[c2]], q="pool")
        mod3 = [TB(ar.alloc([128, D], F32)) for _ in range(3)]
        xr3 = Ring(ar, 2, [128, D], F32)
        rings3 = (Ring(ar, 1, [128, D], BF16), Ring(ar, 2, [128, 4], F32), Ring(ar, 1, [128, D], F32))
        hbr4 = Ring(ar, 4, [128, D], BF16)
        h2T = TB(ar.alloc([128, 8, 512], BF16))
        actT = TB(ar.alloc([128, 22, 512], BF16))
        sgr = Ring(ar, 2, [128, 512], F32)
        ytr3 = rings3[2]
        spair3 = [0]
        opair3 = [4]
        def p3_seq(b):
            groups = []
            if need_ctx:
                groups.append(list(range(NC)))
            for g in range(NL // 4):
                groups.append([NC + 4 * g + j for j in range(4)])
            m3_loaded = [None]
            pre_done = [None]

            def preA(tiles):
                hbs = []
                for tt in tiles:
                    xt = xr3.next()
                    P.dma(lambda e, xt=xt, tt=tt: e.dma_start(out=xt.t[:], in_=x1s[b, tt * 128:(tt + 1) * 128, :]), reads=[x1B[b][tt]], writes=[xt.b])
                    jk, ss, tmp, hb = rings3[0].next(), rings3[1].next(), rings3[2].next(), hbr4.next()
                    P.act(lambda e, jk=jk, ss=ss, xt=xt: e.activation(out=jk.t[:], in_=xt.t[:], func=AF.Square, accum_out=ss.t[:, 0:1]), reads=[xt.b], writes=[jk.b, ss.b])
                    rstd_from_ss(ss, D)
                    P.pool(lambda e, tmp=tmp, xt=xt: e.tensor_tensor(out=tmp.t[:], in0=xt.t[:], in1=mod3[0].t[:], op=ALU.mult), reads=[xt.b, mod3[0].b], writes=[tmp.b])
                    P.dve(lambda e, hb=hb, tmp=tmp, ss=ss: e.scalar_tensor_tensor(out=hb.t[:], in0=tmp.t[:], scalar=ss.t[:, 0:1], in1=mod3[1].t[:], op0=ALU.mult, op1=ALU.add),
                          reads=[tmp.b, ss.b, mod3[1].b], writes=[hb.b])
                    hbs.append(hb)
                return hbs

            def preB(hbs):
                for ti, hb in enumerate(hbs):
                    transposes([(hb.t[:, k * 128:(k + 1) * 128], 128) for k in range(8)], 7, [hb.b])
                    P.act(lambda e, ti=ti: e.activation(out=h2T.t[:, :, ti * 128:(ti + 1) * 128], in_=psbf(7).rearrange("p (k t) -> p k t", k=8), func=AF.Copy), reads=[PB[7]], writes=[h2T.b])

            def kind_of(tiles):
                return "ctx" if tiles[0] < NC else "lat"

            def p3_group(gi):
                tiles = groups[gi]
                N = 128 * len(tiles)
                kind = kind_of(tiles)
                if m3_loaded[0] != kind:
                    r = NB if kind == "ctx" else b
                    for j in range(3):
                        load_bc(mod3[j], modv[l, 3 + j, r:r + 1, :], [modvB[l]])
                    m3_loaded[0] = kind
                if pre_done[0] != gi:
                    preB(preA(tiles))
                nxt = groups[gi + 1] if gi + 1 < len(groups) and kind_of(groups[gi + 1]) == kind else None
                def p3_chunk(c):
                    sp = spair3[0]
                    spair3[0] = 2 - sp
                    for gu in range(2):
                        col = gu * FH + c * 128
                        for k in range(8):
                            P.pe(lambda e, gu=gu, col=col, k=k, sp=sp: e.matmul(ps[:, sp + gu, 0:N], lhsT=fwin.t[:, k, col:col + 128], rhs=h2T.t[:, k, 0:N], start=(k == 0), stop=(k == 7)),
                                 reads=[fwinB[c // 2], h2T.b], writes=[PB[sp + gu]])
                    sg = sgr.next()
                    P.act(lambda e, sg=sg, sp=sp: e.activation(out=sg.t[:, 0:N], in_=ps[:, sp, 0:N], func=AF.Silu), reads=[PB[sp]], writes=[sg.b])
                    P.dve(lambda e, sg=sg, sp=sp, c=c: e.tensor_tensor(out=actT.t[:, c, 0:N], in0=sg.t[:, 0:N], in1=ps[:, sp + 1, 0:N], op=ALU.mult), reads=[sg.b, PB[sp + 1]], writes=[actT.b])
                for c in range(22):
                    p3_chunk(c)
                def p3_epi(ti, tt):
                    op_ = opair3[0]
                    opair3[0] = 10 - op_
                    for hfc in range(2):
                        for c in range(22):
                            P.pe(lambda e, c=c, hfc=hfc, op_=op_, ti=ti: e.matmul(ps[:, op_ + hfc, :], lhsT=actT.t[:, c, ti * 128:(ti + 1) * 128], rhs=fwout.t[:, c, hfc * 512:(hfc + 1) * 512],
                                                                            start=(c == 0), stop=(c == 21)),
                                 reads=[actT.b, fwoutB[c // 2]], writes=[PB[op_ + hfc]])
                    jk, ss = rings3[0].next(), rings3[1].next()
                    P.act(lambda e, jk=jk, ss=ss, op_=op_: e.activation(out=jk.t[:], in_=ps2(op_), func=AF.Square, accum_out=ss.t[:, 0:1]), reads=[PB[op_], PB[op_ + 1]], writes=[jk.b, ss.b])
                    rstd_from_ss(ss, D)
                    xt = xr3.next()
                    P.dma(lambda e, xt=xt, tt=tt: e.dma_start(out=xt.t[:], in_=x1s[b, tt * 128:(tt + 1) * 128, :]), reads=[x1B[b][tt]], writes=[xt.b])
                    yt = ytr3.next()
                    ggs = mod3[2]
                    P.dve(lambda e, yt=yt, ss=ss, op_=op_, ggs=ggs: e.scalar_tensor_tensor(out=yt.t[:], in0=ps2(op_), scalar=ss.t[:, 0:1], in1=ggs.t[:], op0=ALU.mult, op1=ALU.mult),
                          reads=[PB[op_], PB[op_ + 1], ss.b, ggs.b], writes=[yt.b])
                    x2 = xt
                    P.pool(lambda e, xt=xt, yt=yt: e.tensor_tensor(out=xt.t[:], in0=xt.t[:], in1=yt.t[:], op=ALU.add), reads=[xt.b, yt.b], writes=[xt.b])
                    if l == L - 1:
                        P.dma(lambda e, x2=x2, tt=tt: e.dma_start(out=out[b, (tt - NC) * 128:(tt - NC + 1) * 128, :], in_=x2.t[:]), reads=[x2.b], q="pool")
                    else:
                        P.dma(lambda e, x2=x2, tt=tt: e.dma_start(out=xres[b, tt * 128:(tt + 1) * 128, :], in_=x2.t[:]), reads=[x2.b], writes=[xresB[b][tt]], q="pool")
                nhb = preA(nxt) if nxt is not None else None
                for ti, tt in enumerate(tiles):
                    p3_epi(ti, tt)
                if nxt is not None:
                    preB(nhb)
                    pre_done[0] = gi + 1
            for gi in range(len(groups)):
                p3_group(gi)
        for b in range(NB):
            p3_seq(b)
        P.barrier()
        ar.reset(layer_mark)

    for l in range(L):
        do_layer(l)
    P.emit()
    return nc


def rope_tables(S, grid_w=64, theta=10000.0):
    t = np.arange(S)
    rows, cols = (t // grid_w).astype(np.float32), (t % grid_w).astype(np.float32)

    def tab(dh):
        half = dh // 4
        freqs = (theta ** (-np.arange(half, dtype=np.float32) / half)).astype(np.float32)
        ar_, ac_ = rows[:, None] * freqs[None, :], cols[:, None] * freqs[None, :]
        cr, sr, cc, sc = np.cos(ar_), np.sin(ar_), np.cos(ac_), np.sin(ac_)
        cos = np.concatenate([cr, cr, cc, cc], 1)
        sin = np.concatenate([-sr, sr, -sc, sc], 1)
        return np.concatenate([cos, sin], 1).astype(np.float32)
    return tab(64), tab(32)


def const_inputs(S):
    r64, r32 = rope_tables(S)
    k = np.arange(128)[:, None]
    q = np.arange(128)[None, :]
    neg = np.float32(-30000.0)
    mprev = np.where(k >= q, np.float32(0), neg)
    mnext = np.where(k <= q, np.float32(0), neg)
    return {"rope64": r64, "rope32": r32, "ident": np.eye(128, dtype=np.float32),
            "masks": np.stack([mprev, mnext]).astype(np.float32)}


_W_KEYS = ("ada_w", "ada_b", "norm_g", "ffn_w_in", "ffn_w_out", "ev_w_in", "ev_sink", "ev_q_norm", "ev_w_uq",
           "ev_kv_norm", "ev_w_ukv", "ev_w_out", "od_w_in", "od_qk_norm", "od_lambda", "od_subln", "od_w_out")


def run(inputs, n_cores, nb):
    x = np.asarray(inputs["x"], np.float32)
    B, S, _ = x.shape
    ctx = np.asarray(inputs["ctx"], np.float32)
    C = ctx.shape[1]
    L = inputs["ada_w"].shape[0]
    c = np.asarray(inputs["c"], np.float32)
    c_ctx = np.asarray(inputs["c_ctx"], np.float32)
    nc = build_program(S, C, L, nb)
    consts = const_inputs(S)
    shared = {k: np.ascontiguousarray(np.asarray(inputs[k], np.float32)) for k in _W_KEYS}
    in_maps = []
    for i in range(n_cores):
        sl = slice(i * nb, (i + 1) * nb)
        m = {"x": np.ascontiguousarray(x[sl]), "ctx": np.ascontiguousarray(ctx[sl]),
             "cvec": np.ascontiguousarray(np.concatenate([c[sl], c_ctx[None, :]], 0))}
        m.update(shared)
        m.update(consts)
        in_maps.append(m)
    res = run_bass_kernel_spmd(nc, in_maps, core_ids=list(range(n_cores)))
    return np.concatenate([r["out"] for r in res.results], axis=0)


def kernel(**inputs):
    return run(inputs, 8, 2)
```

```python
import contextlib
import math
import numpy as np
import concourse.bass as bass
import concourse.mybir as mybir
from concourse.bass_utils import run_bass_kernel_spmd

F32 = mybir.dt.float32
BF16 = mybir.dt.bfloat16
AF = mybir.ActivationFunctionType
ALU = mybir.AluOpType
AX = mybir.AxisListType

ENGS = ("pe", "act", "dve", "pool", "sp")


class Buf:
    __slots__ = ("name", "w", "r")

    def __init__(self, name=""):
        self.name = name
        self.w = None
        self.r = []


class Op:
    __slots__ = ("eng", "fn", "deps", "sig", "idx", "dma", "dsem", "dval")

    def __init__(self, eng, fn, dma):
        self.eng = eng
        self.fn = fn
        self.deps = []
        self.sig = False
        self.idx = 0
        self.dma = dma
        self.dsem = None
        self.dval = 0


class Prog:
    NDSEM = 48
    NHW = 32

    def __init__(self, nc):
        self.nc = nc
        self.ops = []
        self.ndma = 0
        self.nsw = 0
        self.last = {}
        self.dmas_since = []

    def add(self, eng, fn, reads=(), writes=(), dma=False):
        op = Op(eng, fn, dma)
        deps = {}
        for b in reads:
            if b.w is not None:
                deps[id(b.w)] = b.w
        for b in writes:
            if b.w is not None:
                deps[id(b.w)] = b.w
            for r in b.r:
                deps[id(r)] = r
        for d in deps.values():
            if d is op:
                continue
            if d.eng == "pe" and eng == "pe" and not d.dma and not dma:
                continue
            op.deps.append(d)
            if not d.dma:
                d.sig = True
        for b in writes:
            b.w = op
            b.r = []
        for b in reads:
            if b.w is not op:
                b.r.append(op)
        if dma:
            if eng == "pool":
                op.dsem = self.NHW + self.nsw % (self.NDSEM - self.NHW)
                self.nsw += 1
            else:
                op.dsem = self.ndma % self.NHW
                self.ndma += 1
            self.dmas_since.append(op)
        else:
            self.last[eng] = op
        self.ops.append(op)
        return op

    def pe(self, fn, reads=(), writes=()):
        return self.add("pe", fn, reads, writes)

    def act(self, fn, reads=(), writes=()):
        return self.add("act", fn, reads, writes)

    def dve(self, fn, reads=(), writes=()):
        return self.add("dve", fn, reads, writes)

    def pool(self, fn, reads=(), writes=()):
        return self.add("pool", fn, reads, writes)

    def dma(self, fn, reads=(), writes=(), q="sp"):
        return self.add(q, fn, reads, writes, dma=True)

    def barrier(self):
        lasts = dict(self.last)
        dm = {}
        for d in self.dmas_since:
            dm[d.dsem] = d
        self.dmas_since = []
        for e in ENGS:
            op = Op(e, lambda eng: eng.nop(), False)
            for e2, lo in lasts.items():
                if e2 != e:
                    op.deps.append(lo)
                    lo.sig = True
            op.deps.extend(dm.values())
            self.ops.append(op)
            self.last[e] = op

    def emit(self):
        nc = self.nc
        cnt = {e: 0 for e in ENGS}
        dcount = [0] * self.NDSEM
        for op in self.ops:
            if op.dma:
                dcount[op.dsem] += 16
                op.dval = dcount[op.dsem]
            elif op.sig:
                cnt[op.eng] += 1
                op.idx = cnt[op.eng]
        per = {e: [op for op in self.ops if op.eng == e] for e in ENGS}
        with contextlib.ExitStack() as st:
            esem = {e: st.enter_context(nc.semaphore("s_" + e)) for e in ENGS}
            dsem = [st.enter_context(nc.semaphore("d%d" % i)) for i in range(self.NDSEM)]
            block = st.enter_context(nc.Block())

            def run(ename, eng):
                waited = {}
                for op in per[ename]:
                    for d in op.deps:
                        if d.dma:
                            key, val, sem = ("d", d.dsem), d.dval, dsem[d.dsem]
                        else:
                            key, val, sem = ("e", d.eng), d.idx, esem[d.eng]
                        if waited.get(key, 0) < val:
                            eng.wait_ge(sem, val)
                            waited[key] = val
                    if op.dma:
                        key = ("d", op.dsem)
                        if op.dval > 16 and waited.get(key, 0) < op.dval - 16:
                            eng.wait_ge(dsem[op.dsem], op.dval - 16)
                            waited[key] = op.dval - 16
                        op.fn(eng).then_inc(dsem[op.dsem], 16)
                    else:
                        ins = op.fn(eng)
                        if op.sig:
                            ins.then_inc(esem[ename], 1)
                if ename == "sp":
                    for i in range(self.NDSEM):
                        if dcount[i] > 0 and waited.get(("d", i), 0) < dcount[i]:
                            eng.wait_ge(dsem[i], dcount[i])

            @block.tensor
            def _(eng):
                run("pe", eng)

            @block.scalar
            def _(eng):
                run("act", eng)

            @block.vector
            def _(eng):
                run("dve", eng)

            @block.gpsimd
            def _(eng):
                run("pool", eng)

            @block.sync
            def _(eng):
                run("sp", eng)


class Arena:
    def __init__(self, nc, base, limit):
        self.nc, self.off, self.limit, self.n = nc, base, limit, 0

    def alloc(self, shape, dtype):
        per = 1
        for s in shape[1:]:
            per *= s
        nbytes = per * (4 if dtype == F32 else 2)
        off = (self.off + 31) // 32 * 32
        assert off + nbytes <= self.limit, ("SBUF arena overflow", off, nbytes, self.limit)
        self.off = off + nbytes
        self.n += 1
        return self.nc.alloc_sbuf_tensor_at("t%d" % self.n, list(shape), dtype, offset=off)

    def mark(self):
        return self.off

    def reset(self, m):
        self.off = m


class TB:
    __slots__ = ("t", "b")

    def __init__(self, t):
        self.t = t
        self.b = Buf()


class Ring:
    def __init__(self, arena, n, shape, dtype):
        self.items = [TB(arena.alloc(shape, dtype)) for _ in range(n)]
        self.i = 0

    def next(self):
        it = self.items[self.i % len(self.items)]
        self.i += 1
        return it


DEBUG_STOP = None
DEBUG_ROPE = None
DEBUG_NT = 1
D = 1024
FH = 2816
EPS = 1e-6
EV_IN = 1184
OD_IN = 2304


def lam_init_of(layer):
    return 0.8 - 0.6 * math.exp(-0.3 * layer)


def build_program(S, C, L, NB):
    NL, NC = S // 128, C // 128
    NT = NL + NC
    T = S + C
    NE, NO = (L + 1) // 2, L // 2
    nc = bass.Bass("TRN2", target_bir_lowering=False)

    def din(name, shape):
        return nc.dram_tensor(name, list(shape), F32, kind="ExternalInput").ap()

    x_in = din("x", [NB, S, D])
    ctx_in = din("ctx", [NB, C, D])
    cvec = din("cvec", [NB + 1, D])
    ada_w = din("ada_w", [L, D, 6 * D])
    ada_b = din("ada_b", [L, 6 * D])
    norm_g = din("norm_g", [L, 4, D])
    ffn_w_in = din("ffn_w_in", [L, D, 2 * FH])
    ffn_w_out = din("ffn_w_out", [L, FH, D])
    ev_w_in = din("ev_w_in", [NE, D, EV_IN])
    ev_sink = din("ev_sink", [NE, 8])
    ev_q_norm = din("ev_q_norm", [NE, 256])
    ev_w_uq = din("ev_w_uq", [NE, 256, 768])
    ev_kv_norm = din("ev_kv_norm", [NE, 128])
    ev_w_ukv = din("ev_w_ukv", [NE, 128, 1024])
    ev_w_out = din("ev_w_out", [NE, 1024, D])
    od_w_in = din("od_w_in", [max(NO, 1), D, OD_IN])
    od_qk_norm = din("od_qk_norm", [max(NO, 1), 2, 64])
    od_lambda = din("od_lambda", [max(NO, 1), 4, 64])
    od_subln = din("od_subln", [max(NO, 1), 128])
    od_w_out = din("od_w_out", [max(NO, 1), 1024, D])
    rope64 = din("rope64", [S, 128])
    rope32 = din("rope32", [S, 64])
    ident_in = din("ident", [128, 128])
    masks_in = din("masks", [2, 128, 128])
    out = nc.dram_tensor("out", [NB, S, D], F32, kind="ExternalOutput").ap()
    xres = nc.dram_tensor("xres", [NB, T, D], F32, kind="Internal").ap()
    x1s = nc.dram_tensor("x1s", [NB, T, D], F32, kind="Internal").ap()
    modv = nc.dram_tensor("modv", [L, 6, NB + 1, D], F32, kind="Internal").ap()

    P = Prog(nc)
    ar = Arena(nc, 16640, 229376 - 256)
    ps = nc.alloc_psum_tensor("ps", [128, 8, 512], F32)
    PB = [Buf("ps%d" % i) for i in range(8)]
    xresB = [[Buf() for _ in range(NT)] for _ in range(NB)]
    x1B = [[Buf() for _ in range(NT)] for _ in range(NB)]
    modvB = [Buf() for _ in range(L)]

    def psb(b):
        return ps[:, b, :]

    def psbf(b):
        return ps[:, b, :].bitcast(BF16)

    def ps2(b):
        return ps[:, b:b + 2, :].rearrange("p a b -> p (a b)")

    identb = TB(ar.alloc([128, 128], BF16))
    maskb = TB(ar.alloc([128, 2, 128], BF16))
    onesf = TB(ar.alloc([128, 64], F32))
    epsb = TB(ar.alloc([128, 1], F32))
    oneb = TB(ar.alloc([128, 1], F32))
    P.dma(lambda e: e.dma_start(out=identb.t[:], in_=ident_in[:, :]), writes=[identb.b], q="pool")
    P.dma(lambda e: e.dma_start(out=maskb.t[:], in_=masks_in.rearrange("m k q -> k m q")), writes=[maskb.b], q="pool")
    P.pool(lambda e: e.memset(onesf.t[:], 1.0), writes=[onesf.b])
    P.pool(lambda e: e.memset(epsb.t[:], EPS), writes=[epsb.b])
    P.pool(lambda e: e.memset(oneb.t[:], 1.0), writes=[oneb.b])
    base_mark = ar.mark()

    def rstd_from_ss(ss, n, col=slice(0, 1)):
        P.act(lambda e: e.activation(out=ss.t[:, col], in_=ss.t[:, col], func=AF.Ln, scale=1.0 / n, bias=epsb.t[:]),
              reads=[ss.b, epsb.b], writes=[ss.b])
        P.act(lambda e: e.activation(out=ss.t[:, col], in_=ss.t[:, col], func=AF.Exp, scale=-0.5),
              reads=[ss.b], writes=[ss.b])

    def phase0():
        cT = TB(ar.alloc([128, 8, NB + 1], F32))
        t1 = TB(ar.alloc([128, 8, NB + 1], F32))
        scT = TB(ar.alloc([128, 8, NB + 1], F32))
        R = NB + 1
        for r in range(R):
            P.dma(lambda e, r=r: e.dma_start(out=cT.t[:, :, r], in_=cvec[r, :].rearrange("(k p) -> p k", p=128), allow_slow_non_contiguous=True),
                  writes=[cT.b])
        P.act(lambda e: e.activation(out=t1.t[:], in_=cT.t[:], func=AF.Exp, scale=-1.0), reads=[cT.b], writes=[t1.b])
        P.act(lambda e: e.activation(out=t1.t[:], in_=t1.t[:], func=AF.Ln, bias=oneb.t[:]), reads=[t1.b, oneb.b], writes=[t1.b])
        P.act(lambda e: e.activation(out=t1.t[:], in_=t1.t[:], func=AF.Exp, scale=-1.0), reads=[t1.b], writes=[t1.b])
        P.dve(lambda e: e.tensor_tensor(out=scT.t[:], in0=cT.t[:], in1=t1.t[:], op=ALU.mult), reads=[cT.b, t1.b], writes=[scT.b])
        awr = Ring(ar, 2, [128, 8, 512], F32)
        modrow = TB(ar.alloc([R, 6 * D], F32))
        abt = TB(ar.alloc([R, 6 * D], F32))
        ngt = TB(ar.alloc([R, 4 * D], F32))
        mv = TB(ar.alloc([R, 6, D], F32))
        for l in range(L):
            for n in range(12):
                aw = awr.next()
                P.dma(lambda e, aw=aw, l=l, n=n: e.dma_start(out=aw.t[:], in_=ada_w[l, :, n * 512:(n + 1) * 512].rearrange("(k p) n -> p k n", p=128)),
                      writes=[aw.b])
                bk = n % 2
                for k in range(8):
                    P.pe(lambda e, aw=aw, k=k, bk=bk: e.matmul(ps[0:R, bk, :], lhsT=scT.t[:, k, :], rhs=aw.t[:, k, :], start=(k == 0), stop=(k == 7)),
                         reads=[scT.b, aw.b], writes=[PB[bk]])
                P.act(lambda e, n=n, bk=bk: e.activation(out=modrow.t[:, n * 512:(n + 1) * 512], in_=ps[0:R, bk, :], func=AF.Copy),
                      reads=[PB[bk]], writes=[modrow.b])
            P.dma(lambda e, l=l: e.dma_start(out=abt.t[:], in_=ada_b[l:l + 1, :].partition_broadcast(R)), writes=[abt.b])
            P.dma(lambda e, l=l: e.dma_start(out=ngt.t[:], in_=norm_g[l:l + 1, :, :].rearrange("o j d -> o (j d)").partition_broadcast(R)), writes=[ngt.b])
            P.dve(lambda e: e.tensor_tensor(out=modrow.t[:], in0=modrow.t[:], in1=abt.t[:], op=ALU.add), reads=[modrow.b, abt.b], writes=[modrow.b])

            def seg(i):
                return modrow.t[:, i * D:(i + 1) * D]

            def ng(i):
                return ngt.t[:, i * D:(i + 1) * D]
            P.dve(lambda e: e.scalar_tensor_tensor(out=mv.t[:, 0, :], in0=seg(1), scalar=1.0, in1=ng(0), op0=ALU.add, op1=ALU.mult), reads=[modrow.b, ngt.b], writes=[mv.b])
            P.dve(lambda e: e.tensor_copy(out=mv.t[:, 1, :], in_=seg(0)), reads=[modrow.b], writes=[mv.b])
            P.dve(lambda e: e.tensor_tensor(out=mv.t[:, 2, :], in0=seg(2), in1=ng(1), op=ALU.mult), reads=[modrow.b, ngt.b], writes=[mv.b])
            P.dve(lambda e: e.scalar_tensor_tensor(out=mv.t[:, 3, :], in0=seg(4), scalar=1.0, in1=ng(2), op0=ALU.add, op1=ALU.mult), reads=[modrow.b, ngt.b], writes=[mv.b])
            P.dve(lambda e: e.tensor_copy(out=mv.t[:, 4, :], in_=seg(3)), reads=[modrow.b], writes=[mv.b])
            P.dve(lambda e: e.tensor_tensor(out=mv.t[:, 5, :], in0=seg(5), in1=ng(3), op=ALU.mult), reads=[modrow.b, ngt.b], writes=[mv.b])
            P.dma(lambda e, l=l: e.dma_start(out=modv[l].rearrange("j r d -> r j d"), in_=mv.t[:]), reads=[mv.b], writes=[modvB[l]])

    phase0()
    P.barrier()
    ar.reset(base_mark)
    if DEBUG_STOP == "p0":
        P.emit()
        return nc

    def x_src(l, b, tt):
        if l == 0:
            if tt < NC:
                return ctx_in[b, tt * 128:(tt + 1) * 128, :], []
            return x_in[b, (tt - NC) * 128:(tt - NC + 1) * 128, :], []
        return xres[b, tt * 128:(tt + 1) * 128, :], [xresB[b][tt]]

    def load_bc(dst, src_row, reads=()):
        P.dma(lambda e: e.dma_start(out=dst.t[:], in_=src_row.partition_broadcast(128)), reads=list(reads), writes=[dst.b])

    def transposes(srcs, bank, src_bufs):
        for i, (ap, m) in enumerate(srcs):
            P.pe(lambda e, ap=ap, m=m, i=i: e.transpose(out=psbf(bank)[0:m, i * 128:(i + 1) * 128], in_=ap, identity=identb.t[:]),
                 reads=list(src_bufs) + [identb.b], writes=[PB[bank]])

    def norm_mod_T(xt, gt, sht, hT_dst_ap, hT_buf, rings):
        junk, ssr, tmpr, hbr = rings
        jk, ss, tmp, hb = junk.next(), ssr.next(), tmpr.next(), hbr.next()
        P.act(lambda e: e.activation(out=jk.t[:], in_=xt.t[:], func=AF.Square, accum_out=ss.t[:, 0:1]), reads=[xt.b], writes=[jk.b, ss.b])
        rstd_from_ss(ss, D)
        P.pool(lambda e: e.tensor_tensor(out=tmp.t[:], in0=xt.t[:], in1=gt.t[:], op=ALU.mult), reads=[xt.b, gt.b], writes=[tmp.b])
        P.dve(lambda e: e.scalar_tensor_tensor(out=hb.t[:], in0=tmp.t[:], scalar=ss.t[:, 0:1], in1=sht.t[:], op0=ALU.mult, op1=ALU.add),
              reads=[tmp.b, ss.b, sht.b], writes=[hb.b])
        transposes([(hb.t[:, k * 128:(k + 1) * 128], 128) for k in range(8)], 7, [hb.b])
        P.act(lambda e: e.activation(out=hT_dst_ap, in_=psbf(7).rearrange("p (k t) -> p k t", k=8), func=AF.Copy), reads=[PB[7]], writes=[hT_buf])

    def rope_apply(src_ap, nh, cs, dst_ap, tmps, src_bufs, dst_bufs, dh=64, split=None):
        t1, t2 = tmps
        q = dh // 4
        if DEBUG_ROPE == "copy":
            P.act(lambda e: e.activation(out=dst_ap, in_=(src_ap if split is None else src_ap.rearrange("p (j s) d -> p j s d", j=split)), func=AF.Copy), reads=list(src_bufs), writes=list(dst_bufs))
            return
        cos = cs.t[:, 0:dh].unsqueeze(1).to_broadcast([128, nh, dh])
        x3 = t1.t[:, 0:nh * dh].rearrange("p (h d) -> p h d", h=nh)
        P.act(lambda e: e.activation(out=x3, in_=src_ap, func=AF.Copy), reads=list(src_bufs), writes=[t1.b])
        s5 = t1.t[:, 0:nh * dh].rearrange("p (h b x d) -> p h b x d", h=nh, b=2, x=2)
        o5 = t2.t[:, 0:nh * dh].rearrange("p (h b x d) -> p h b x d", h=nh, b=2, x=2)
        sn = cs.t[:, dh:2 * dh].rearrange("p (b x d) -> p b x d", b=2, x=2)
        for xx in range(2):
            P.dve(lambda e, xx=xx: e.tensor_tensor(out=o5[:, :, :, xx, :], in0=s5[:, :, :, 1 - xx, :],
                                                 in1=sn[:, :, xx, :].unsqueeze(1).to_broadcast([128, nh, 2, q]), op=ALU.mult),
                  reads=[t1.b, cs.b], writes=[t2.b])
        P.dve(lambda e: e.tensor_tensor(out=x3, in0=x3, in1=cos, op=ALU.mult), reads=[t1.b, cs.b], writes=[t1.b])
        if split is None:
            a1 = t1.t[:, 0:nh * dh].rearrange("p (h d) -> p h d", h=nh)
            a2 = t2.t[:, 0:nh * dh].rearrange("p (h d) -> p h d", h=nh)
        else:
            a1 = t1.t[:, 0:nh * dh].rearrange("p (j s d) -> p j s d", j=split, d=dh)
            a2 = t2.t[:, 0:nh * dh].rearrange("p (j s d) -> p j s d", j=split, d=dh)
        if DEBUG_ROPE == "noswap":
            P.act(lambda e: e.activation(out=dst_ap, in_=a1, func=AF.Copy), reads=[t1.b, t2.b], writes=list(dst_bufs))
            return
        fin = P.dve if DEBUG_ROPE == "dvefin" else P.pool
        fin(lambda e: e.tensor_tensor(out=dst_ap, in0=a1, in1=a2, op=ALU.add), reads=[t1.b, t2.b], writes=list(dst_bufs))

    def do_layer(l):
        even = (l % 2 == 0)
        li = l // 2
        need_ctx = l < L - 1
        lam0 = lam_init_of(l)
        layer_mark = ar.mark()
        if even:
            qn_bc = TB(ar.alloc([128, 256], F32))
            kvn_bc = TB(ar.alloc([128, 128], F32))
            sinkx = TB(ar.alloc([128, 8], F32))
            load_bc(qn_bc, ev_q_norm[li:li + 1, :])
            load_bc(kvn_bc, ev_kv_norm[li:li + 1, :])
            load_bc(sinkx, ev_sink[li:li + 1, :])
            P.act(lambda e: e.activation(out=sinkx.t[:], in_=sinkx.t[:], func=AF.Exp), reads=[sinkx.b], writes=[sinkx.b])
        else:
            qkg = TB(ar.alloc([128, 10, 64], F32))
            lamt = TB(ar.alloc([128, 4, 64], F32))
            lamw = TB(ar.alloc([128, 2, 64], F32))
            lams = TB(ar.alloc([128, 4], F32))
            subs = TB(ar.alloc([64, 2], F32))
            for h in range(10):
                P.dma(lambda e, h=h: e.dma_start(out=qkg.t[:, h, :], in_=od_qk_norm[li, (0 if h < 8 else 1):(1 if h < 8 else 2), :].partition_broadcast(128)),
                      writes=[qkg.b])
            P.dma(lambda e: e.dma_start(out=lamt.t[:].rearrange("p a d -> p (a d)"), in_=od_lambda[li:li + 1, :, :].rearrange("o a d -> o (a d)").partition_broadcast(128)),
                  writes=[lamt.b])
            P.dma(lambda e: e.dma_start(out=subs.t[:], in_=od_subln[li, :].rearrange("(h p) -> p h", p=64), allow_slow_non_contiguous=True), writes=[subs.b])
            P.dve(lambda e: e.tensor_tensor(out=lamw.t[:, 0, :], in0=lamt.t[:, 0, :], in1=lamt.t[:, 1, :], op=ALU.mult), reads=[lamt.b], writes=[lamw.b])
            P.dve(lambda e: e.tensor_tensor(out=lamw.t[:, 1, :], in0=lamt.t[:, 2, :], in1=lamt.t[:, 3, :], op=ALU.mult), reads=[lamt.b, lamw.b], writes=[lamw.b])
            P.dve(lambda e: e.tensor_reduce(out=lams.t[:, 0:2], in_=lamw.t[:], axis=AX.X, op=ALU.add), reads=[lamw.b], writes=[lams.b])
            P.act(lambda e: e.activation(out=lams.t[:, 0:2], in_=lams.t[:, 0:2], func=AF.Exp), reads=[lams.b], writes=[lams.b])
            P.dve(lambda e: e.tensor_tensor(out=lams.t[:, 2:3], in0=lams.t[:, 1:2], in1=lams.t[:, 0:1], op=ALU.subtract), reads=[lams.b], writes=[lams.b])
            P.dve(lambda e: e.tensor_scalar(out=lams.t[:, 3:4], in0=lams.t[:, 2:3], scalar1=-lam0, scalar2=None, op0=ALU.add), reads=[lams.b], writes=[lams.b])
            P.dve(lambda e: e.tensor_scalar(out=subs.t[:], in0=subs.t[:], scalar1=1.0 - lam0, scalar2=None, op0=ALU.mult), reads=[subs.b], writes=[subs.b])
        const_mark = ar.mark()

        def do_seq(b):
            P.barrier()
            ar.reset(const_mark)
            if even:
                qaT = ar.alloc([128, 4, T], BF16)
                kaT = ar.alloc([128, T], BF16)
                va = ar.alloc([128, NT, 1, 3, 64], BF16)
                qbT = ar.alloc([128, 8, T], BF16)
                kbT = ar.alloc([128, 8, T], BF16)
                vb = ar.alloc([128, NT, 4, 3, 64], BF16)
                vones = [(va, 2), (vb, 8)]
            else:
                qcT = ar.alloc([128, 4, T], BF16)
                kcT = ar.alloc([128, T], BF16)
                vc = ar.alloc([128, NT, 1, 3, 64], BF16)
                qdT = ar.alloc([128, 4, T], BF16)
                kdT = ar.alloc([128, 4, T], BF16)
                vd = ar.alloc([128, NT, 4, 3, 64], BF16)
                vones = [(vc, 2), (vd, 8)]
            KB = [Buf() for _ in range(NT)]
            VB1 = Buf()
            for (vt, nh) in vones:
                P.pool(lambda e, vt=vt: e.memset(vt[:, :, :, 1, :], 1.0), writes=[VB1])
            qkv_mark = ar.mark()

            g1t = [TB(ar.alloc([128, D], F32)) for _ in range(2)]
            sh1t = [TB(ar.alloc([128, D], F32)) for _ in range(2)]
            for j, r in enumerate((b, NB)):
                load_bc(g1t[j], modv[l, 0, r:r + 1, :], [modvB[l]])
                load_bc(sh1t[j], modv[l, 1, r:r + 1, :], [modvB[l]])
            NIN = EV_IN if even else OD_IN
            win = TB(ar.alloc([128, 8, NIN], BF16))
            wsrc = ev_w_in[li] if even else od_w_in[li]
            winB = [Buf() for _ in range(4)]
            for k2 in range(4):
                P.dma(lambda e, k2=k2: e.dma_start(out=win.t[:, 2 * k2:2 * k2 + 2, :], in_=wsrc[k2 * 256:(k2 + 1) * 256, :].rearrange("(k p) n -> p k n", p=128)),
                      writes=[winB[k2]], q="pool")
            if even:
                wuq = TB(ar.alloc([128, 2, 768], BF16))
                wukv = TB(ar.alloc([128, 1024], BF16))
                P.dma(lambda e: e.dma_start(out=wuq.t[:], in_=ev_w_uq[li].rearrange("(k p) n -> p k n", p=128)), writes=[wuq.b], q="pool")
                P.dma(lambda e: e.dma_start(out=wukv.t[:], in_=ev_w_ukv[li]), writes=[wukv.b], q="pool")
            xr = Ring(ar, 2, [128, D], F32)
            rings = (Ring(ar, 1, [128, D], BF16), Ring(ar, 2, [128, 4], F32), Ring(ar, 1, [128, D], F32), Ring(ar, 2, [128, D], BF16))
            hTr = Ring(ar, 2, [128, 8, 128], BF16)
            csr64 = Ring(ar, 2, [128, 128], F32)
            csr32 = Ring(ar, 2, [128, 64], F32)
            rt1 = TB(ar.alloc([128, 512 if even else 1024], F32))
            rt2 = TB(ar.alloc([128, 512 if even else 1024], F32))
            stg = Ring(ar, 2, [128, 1024], BF16)
            ss2 = Ring(ar, 2, [128, 16], F32)
            sqt = TB(ar.alloc([128, 384 if even else 640], F32))
            nrm = Ring(ar, 2, [128, 256], BF16)
            smT = Ring(ar, 2, [128, 2, 128], BF16)
            kper = Ring(ar, 2, [128, 32], BF16)

            def p1_tile(tt):
                isctx = tt < NC
                pt = tt - NC
                tok = slice(tt * 128, (tt + 1) * 128)
                xt = xr.next()
                src, srcb = x_src(l, b, tt)
                P.dma(lambda e, xt=xt, src=src: e.dma_start(out=xt.t[:], in_=src), reads=srcb, writes=[xt.b])
                hT = hTr.next()
                norm_mod_T(xt, g1t[1 if isctx else 0], sh1t[1 if isctx else 0], hT.t[:], hT.b, rings)
                if not isctx:
                    cs64, cs32 = csr64.next(), csr32.next()
                    P.dma(lambda e, cs64=cs64, pt=pt: e.dma_start(out=cs64.t[:], in_=rope64[pt * 128:(pt + 1) * 128, :]), writes=[cs64.b])
                    if even:
                        P.dma(lambda e, cs32=cs32, pt=pt: e.dma_start(out=cs32.t[:], in_=rope32[pt * 128:(pt + 1) * 128, :]), writes=[cs32.b])
                nchunks = (NIN + 511) // 512
                for n in range(nchunks):
                    w = min(512, NIN - n * 512)
                    for k in range(8):
                        P.pe(lambda e, hT=hT, n=n, k=k, w=w: e.matmul(ps[:, n, 0:w], lhsT=hT.t[:, k, :], rhs=win.t[:, k, n * 512:n * 512 + w], start=(k == 0), stop=(k == 7)),
                             reads=[hT.b, winB[k // 2]], writes=[PB[n]])
                flat = ps[:, 0:5, :].rearrange("p a b -> p (a b)")
                st = stg.next()
                if even:
                    qk_src = flat[:, 0:640].rearrange("p (h d) -> p h d", h=10)
                    dq = st.t[:, 0:512].rearrange("p (s j d) -> p j s d", s=4, j=2)
                    dk = st.t[:, 512:640].rearrange("p (h d) -> p h d", h=2)
                    if isctx:
                        P.act(lambda e, dq=dq, qk_src=qk_src: e.activation(out=dq, in_=qk_src[:, 0:8, :].rearrange("p (j s) d -> p j s d", j=2), func=AF.Copy), reads=[PB[0]], writes=[st.b])
                        P.act(lambda e, dk=dk, qk_src=qk_src: e.activation(out=dk, in_=qk_src[:, 8:10, :], func=AF.Copy), reads=[PB[1]], writes=[st.b])
                    else:
                        rope_apply(qk_src[:, 0:8, :], 8, cs64, dq, (rt1, rt2), [PB[0]], [st.b], split=2)
                        rope_apply(qk_src[:, 8:10, :], 2, cs64, dk, (rt1, rt2), [PB[1]], [st.b])
                    transposes([(st.t[:, s * 128:(s + 1) * 128], 128) for s in range(5)], 7, [st.b])
                    P.act(lambda e, tok=tok: e.activation(out=qaT[:, :, tok], in_=psbf(7)[:, 0:512].rearrange("p (s t) -> p s t", s=4), func=AF.Copy), reads=[PB[7]], writes=[KB[tt]])
                    P.act(lambda e, tok=tok: e.activation(out=kaT[:, tok], in_=psbf(7)[:, 512:640], func=AF.Copy), reads=[PB[7]], writes=[KB[tt]])
                    P.act(lambda e, tt=tt: e.activation(out=va[:, tt, :, 0:3:2, :], in_=flat[:, 640:768].rearrange("p (a e d) -> p a e d", a=1, e=2), func=AF.Copy), reads=[PB[1]], writes=[KB[tt]])
                    ss = ss2.next()
                    P.act(lambda e, ss=ss: e.activation(out=sqt.t[:, 0:256], in_=flat[:, 768:1024], func=AF.Square, accum_out=ss.t[:, 0:1]), reads=[PB[1]], writes=[sqt.b, ss.b])
                    P.act(lambda e, ss=ss: e.activation(out=sqt.t[:, 256:384], in_=flat[:, 1024:1152], func=AF.Square, accum_out=ss.t[:, 1:2]), reads=[PB[2]], writes=[sqt.b, ss.b])
                    rstd_from_ss(ss, 256, slice(0, 1))
                    rstd_from_ss(ss, 128, slice(1, 2))
                    nq = nrm.next()
                    P.dve(lambda e, nq=nq, ss=ss: e.scalar_tensor_tensor(out=nq.t[:, 0:256], in0=flat[:, 768:1024], scalar=ss.t[:, 0:1], in1=qn_bc.t[:], op0=ALU.mult, op1=ALU.mult),
                          reads=[PB[1], ss.b, qn_bc.b], writes=[nq.b])
                    transposes([(nq.t[:, k * 128:(k + 1) * 128], 128) for k in range(2)], 7, [nq.b])
                    cqT = smT.next()
                    P.act(lambda e, cqT=cqT: e.activation(out=cqT.t[:], in_=psbf(7)[:, 0:256].rearrange("p (k t) -> p k t", k=2), func=AF.Copy), reads=[PB[7]], writes=[cqT.b])
                    for n, (c0, w) in enumerate(((0, 512), (512, 256))):
                        for k in range(2):
                            P.pe(lambda e, cqT=cqT, n=n, k=k, c0=c0, w=w: e.matmul(ps[:, 3 + n, 0:w], lhsT=cqT.t[:, k, :], rhs=wuq.t[:, k, c0:c0 + w], start=(k == 0), stop=(k == 1)),
                                 reads=[cqT.b, wuq.b], writes=[PB[3 + n]])
                    qb_src = ps[:, 3:5, :].rearrange("p a b -> p (a b)")[:, 0:768].rearrange("p (h d) -> p h d", h=8)
                    st2 = stg.next()
                    qb_dst = st2.t[:, 0:768].rearrange("p (h d) -> p h d", h=8)
                    if isctx:
                        P.act(lambda e, qb_dst=qb_dst, qb_src=qb_src: e.activation(out=qb_dst, in_=qb_src, func=AF.Copy), reads=[PB[3], PB[4]], writes=[st2.b])
                    else:
                        P.act(lambda e, qb_dst=qb_dst, qb_src=qb_src: e.activation(out=qb_dst[:, :, 0:64], in_=qb_src[:, :, 0:64], func=AF.Copy), reads=[PB[3], PB[4]], writes=[st2.b])
                        rope_apply(qb_src[:, :, 64:96], 8, cs32, qb_dst[:, :, 64:96], (rt1, rt2), [PB[3], PB[4]], [st2.b], dh=32)
                    transposes([(st2.t[:, h * 96:(h + 1) * 96], 96) for h in range(8)], 7, [st2.b])
                    P.act(lambda e, tok=tok: e.activation(out=qbT[0:96, :, tok], in_=psbf(7)[0:96, :].rearrange("p (h t) -> p h t", h=8), func=AF.Copy), reads=[PB[7]], writes=[KB[tt]])
                    nk = nrm.next()
                    P.dve(lambda e, nk=nk, ss=ss: e.scalar_tensor_tensor(out=nk.t[:, 0:128], in0=flat[:, 1024:1152], scalar=ss.t[:, 1:2], in1=kvn_bc.t[:], op0=ALU.mult, op1=ALU.mult),
                          reads=[PB[2], ss.b, kvn_bc.b], writes=[nk.b])
                    transposes([(nk.t[:, 0:128], 128)], 7, [nk.b])
                    ckT = smT.next()
                    P.act(lambda e, ckT=ckT: e.activation(out=ckT.t[:, 0, :], in_=psbf(7)[:, 0:128], func=AF.Copy), reads=[PB[7]], writes=[ckT.b])
                    for n in range(2):
                        P.pe(lambda e, ckT=ckT, n=n: e.matmul(ps[:, 5 + n, :], lhsT=ckT.t[:, 0, :], rhs=wukv.t[:, n * 512:(n + 1) * 512], start=True, stop=True),
                             reads=[ckT.b, wukv.b], writes=[PB[5 + n]])
                    kv_src = ps[:, 5:7, :].rearrange("p a b -> p (a b)").rearrange("p (h d) -> p h d", h=8)
                    st3 = stg.next()
                    kb_dst = st3.t[:, 0:768].rearrange("p (h d) -> p h d", h=8)
                    P.act(lambda e, kb_dst=kb_dst, kv_src=kv_src: e.activation(out=kb_dst[:, :, 0:64], in_=kv_src[:, :, 0:64], func=AF.Copy), reads=[PB[5], PB[6]], writes=[st3.b])
                    P.act(lambda e, tt=tt, kv_src=kv_src: e.activation(out=vb[:, tt, :, 0:3:2, :], in_=kv_src[:, :, 64:128].rearrange("p (a e) d -> p a e d", e=2), func=AF.Copy), reads=[PB[5], PB[6]], writes=[KB[tt]])
                    kp = kper.next()
                    kpe_src = flat[:, 1152:1184].rearrange("p (h d) -> p h d", h=1)
                    if isctx:
                        P.act(lambda e, kp=kp, kpe_src=kpe_src: e.activation(out=kp.t[:].rearrange("p (h d) -> p h d", h=1), in_=kpe_src, func=AF.Copy), reads=[PB[2]], writes=[kp.b])
                    else:
                        rope_apply(kpe_src, 1, cs32, kp.t[:].rearrange("p (h d) -> p h d", h=1), (rt1, rt2), [PB[2]], [kp.b], dh=32)
                    P.pool(lambda e, kp=kp, kb_dst=kb_dst: e.tensor_copy(out=kb_dst[:, :, 64:96], in_=kp.t[:].unsqueeze(1).to_broadcast([128, 8, 32])), reads=[kp.b], writes=[st3.b])
                    transposes([(st3.t[:, h * 96:(h + 1) * 96], 96) for h in range(8)], 7, [st3.b])
                    P.act(lambda e, tok=tok: e.activation(out=kbT[0:96, :, tok], in_=psbf(7)[0:96, :].rearrange("p (h t) -> p h t", h=8), func=AF.Copy), reads=[PB[7]], writes=[KB[tt]])
                else:
                    qk_src = flat[:, 0:640].rearrange("p (h d) -> p h d", h=10)
                    ss = ss2.next()
                    P.act(lambda e: e.activation(out=sqt.t[:], in_=flat[:, 0:640], func=AF.Square), reads=[PB[0], PB[1]], writes=[sqt.b])
                    P.dve(lambda e, ss=ss: e.tensor_reduce(out=ss.t[:, 0:10], in_=sqt.t[:].rearrange("p (h d) -> p h d", h=10), axis=AX.X, op=ALU.add), reads=[sqt.b], writes=[ss.b])
                    rstd_from_ss(ss, 64, slice(0, 10))
                    xn = rt1.t[:, 0:640].rearrange("p (h d) -> p h d", h=10)
                    xs = rt2.t[:, 0:640].rearrange("p (h d) -> p h d", h=10)
                    P.act(lambda e, qk_src=qk_src, xs=xs: e.activation(out=xs, in_=qk_src, func=AF.Copy), reads=[PB[0], PB[1]], writes=[rt2.b])
                    P.dve(lambda e, ss=ss, xs=xs, xn=xn: e.tensor_tensor(out=xn, in0=xs, in1=ss.t[:, 0:10].unsqueeze(2).to_broadcast([128, 10, 64]), op=ALU.mult),
                          reads=[rt2.b, ss.b], writes=[rt1.b])
                    xg = sqt.t[:].rearrange("p (h d) -> p h d", h=10)
                    P.pool(lambda e, xn=xn, xg=xg: e.tensor_tensor(out=xg, in0=xn, in1=qkg.t[:], op=ALU.mult), reads=[rt1.b, qkg.b], writes=[sqt.b])
                    dq = st.t[:, 0:512].rearrange("p (s j d) -> p j s d", s=4, j=2)
                    dk = st.t[:, 512:640].rearrange("p (h d) -> p h d", h=2)
                    if isctx:
                        P.pool(lambda e, dq=dq, xg=xg: e.tensor_copy(out=dq, in_=xg[:, 0:8, :].rearrange("p (j s) d -> p j s d", j=2)), reads=[sqt.b], writes=[st.b])
                        P.pool(lambda e, dk=dk, xg=xg: e.tensor_copy(out=dk, in_=xg[:, 8:10, :]), reads=[sqt.b], writes=[st.b])
                    else:
                        rope_apply(xg[:, 0:8, :], 8, cs64, dq, (rt1, rt2), [sqt.b], [st.b], split=2)
                        rope_apply(xg[:, 8:10, :], 2, cs64, dk, (rt1, rt2), [sqt.b], [st.b])
                    transposes([(st.t[:, s * 128:(s + 1) * 128], 128) for s in range(5)], 7, [st.b])
                    P.act(lambda e, tok=tok: e.activation(out=qcT[:, :, tok], in_=psbf(7)[:, 0:512].rearrange("p (s t) -> p s t", s=4), func=AF.Copy), reads=[PB[7]], writes=[KB[tt]])
                    P.act(lambda e, tok=tok: e.activation(out=kcT[:, tok], in_=psbf(7)[:, 512:640], func=AF.Copy), reads=[PB[7]], writes=[KB[tt]])
                    P.act(lambda e, tt=tt: e.activation(out=vc[:, tt, :, 0:3:2, :], in_=flat[:, 640:768].rearrange("p (a e d) -> p a e d", a=1, e=2), func=AF.Copy), reads=[PB[1]], writes=[KB[tt]])
                    st2 = stg.next()
                    qd_src = flat[:, 768:1792].rearrange("p (h d) -> p h d", h=16)
                    qd_dst = st2.t[:].rearrange("p (h d) -> p h d", h=16)
                    if isctx:
                        P.act(lambda e, qd_dst=qd_dst, qd_src=qd_src: e.activation(out=qd_dst, in_=qd_src, func=AF.Copy), reads=[PB[1], PB[2], PB[3]], writes=[st2.b])
                    else:
                        rope_apply(qd_src, 16, cs64, qd_dst, (rt1, rt2), [PB[1], PB[2], PB[3]], [st2.b])
                    transposes([(st2.t[:, s * 128:(s + 1) * 128], 128) for s in range(8)], 7, [st2.b])
                    P.act(lambda e, tok=tok: e.activation(out=qdT[:, :, tok], in_=psbf(7)[:, 0:512].rearrange("p (s t) -> p s t", s=4), func=AF.Copy), reads=[PB[7]], writes=[KB[tt]])
                    P.act(lambda e, tok=tok: e.activation(out=kdT[:, :, tok], in_=psbf(7)[:, 512:1024].rearrange("p (s t) -> p s t", s=4), func=AF.Copy), reads=[PB[7]], writes=[KB[tt]])
                    P.act(lambda e, tt=tt: e.activation(out=vd[:, tt, :, 0:3:2, :], in_=flat[:, 1792:2304].rearrange("p (a e d) -> p a e d", a=4, e=2), func=AF.Copy), reads=[PB[3], PB[4]], writes=[KB[tt]])

            for tt in range(NT if DEBUG_STOP != "p1a" else DEBUG_NT):
                p1_tile(tt)
            if DEBUG_STOP in ("p1", "p1a"):
                return
            P.barrier()
            ar.reset(qkv_mark)
            wout = TB(ar.alloc([128, 8, D], BF16))
            wosrc = ev_w_out[li] if even else od_w_out[li]
            woutB = [Buf() for _ in range(4)]
            for c4 in range(4):
                P.dma(lambda e, c4=c4: e.dma_start(out=wout.t[:, 2 * c4:2 * c4 + 2, :], in_=wosrc[c4 * 256:(c4 + 1) * 256, :].rearrange("(c p) n -> p c n", p=128)),
                      writes=[woutB[c4]], q="pool")
            gg1 = TB(ar.alloc([128, D], F32))
            ptr = Ring(ar, 3, [128, 2, 512], BF16)
            osbr = Ring(ar, 4, [128, 512], F32)
            oT = TB(ar.alloc([128, 8, 512], BF16))

            def oTa(hc, N):
                return oT.t[(hc % 2) * 64:(hc % 2) * 64 + 64, hc // 2, 0:N]
            xr2 = Ring(ar, 2, [128, D], F32)
            ytr = Ring(ar, 2, [128, D], F32)
            x1r = Ring(ar, 2, [128, D], F32)
            jk2 = Ring(ar, 1, [128, D], BF16)
            ss3 = Ring(ar, 2, [128, 4], F32)
            if not even:
                odr = Ring(ar, 8, [64, 512], F32)
                sqr = Ring(ar, 2, [64, 512], BF16)
                onesb = TB(ar.alloc([64, 64], BF16))
                P.pool(lambda e: e.memset(onesb.t[:], 1.0), writes=[onesb.b])
                rsd = Ring(ar, 3, [64, 512], F32)
            obank = [4]
            spair = [0]

            def next_obank():
                bk = obank[0]
                obank[0] = 4 + (bk - 4 + 1) % 4
                return bk

            def next_spair():
                s = spair[0]
                spair[0] = 2 - s
                return s

            def vop(vt, t, slot):
                p_, e_ = slot // 2, slot % 2
                return vt[:, t, p_, e_:e_ + 2, :].rearrange("p a d -> p (a d)")

            def normalise(bk, N, dst_ap, dst_bufs, sink_h=None, keep=None, e_=0):
                vs = slice(e_ * 64, e_ * 64 + 64)
                ds = slice((1 - e_) * 64, (1 - e_) * 64 + 64)
                r = osbr.next()
                if sink_h is not None:
                    P.dve(lambda e: e.tensor_scalar(out=r.t[ds, 0:N], in0=ps[ds, bk, 0:N], scalar1=sinkx.t[ds, sink_h:sink_h + 1], scalar2=None, op0=ALU.add),
                          reads=[PB[bk], sinkx.b], writes=[r.b])
                    P.dve(lambda e: e.reciprocal(out=r.t[ds, 0:N], in_=r.t[ds, 0:N]), reads=[r.b], writes=[r.b])
                else:
                    P.dve(lambda e: e.reciprocal(out=r.t[ds, 0:N], in_=ps[ds, bk, 0:N]), reads=[PB[bk]], writes=[r.b])
                if keep is None:
                    P.dve(lambda e: e.tensor_tensor(out=dst_ap, in0=ps[vs, bk, 0:N], in1=r.t[ds, 0:N], op=ALU.mult), reads=[r.b, PB[bk]], writes=list(dst_bufs))
                else:
                    P.dve(lambda e: e.tensor_tensor(out=keep.t[:, 0:N], in0=ps[vs, bk, 0:N], in1=r.t[ds, 0:N], op=ALU.mult), reads=[r.b, PB[bk]], writes=[keep.b])

            def attend_jobs(jobs, units, ktiles, N, scale, post, qB):
                if len(units) == 1:
                    u = units[0]
                    jl = [[(u, t)] for t in ktiles]
                    steps = [jl[i] + (jl[i + 1] if i + 1 < len(jl) else []) for i in range(0, len(jl), 2)]
                else:
                    steps = [[(units[0], t), (units[1], t)] for t in ktiles]
                banks = {}
                for u in units:
                    for vi in range(len(u["V"])):
                        banks[(id(u), vi)] = next_obank()
                first = {k: True for k in banks}
                last_step = len(steps) - 1
                single = len(units) == 1
                for si, step in enumerate(steps):
                    sp = next_spair()
                    pt = ptr.next()
                    nj = len(step)

                    def S(step=step, sp=sp):
                        for j, (u, t) in enumerate(step):
                            P.pe(lambda e, u=u, t=t, j=j: e.matmul(ps[:, sp + j, 0:N], lhsT=u["kT"](t), rhs=u["qT"], start=True, stop=True),
                                 reads=[KB[t]] + qB, writes=[PB[sp + j]])

                    def E(sp=sp, pt=pt, nj=nj):
                        P.act(lambda e: e.activation(out=pt.t[:, 0:nj, 0:N], in_=ps[:, sp:sp + nj, 0:N], func=AF.Exp, scale=scale),
                              reads=[PB[sp + j] for j in range(nj)], writes=[pt.b])

                    def V(step=step, pt=pt, si=si, nj=nj):
                        for j, (u, t) in enumerate(step):
                            for vi, vfn in enumerate(u["V"]):
                                key = (id(u), vi)
                                bk = banks[key]
                                lastc = (si == last_step and j == nj - 1) if single else (si == last_step)
                                P.pe(lambda e, j=j, t=t, vfn=vfn, bk=bk, st_=first[key], lastc=lastc: e.matmul(ps[:, bk, 0:N], lhsT=vfn(t), rhs=pt.t[:, j, 0:N], start=st_, stop=lastc),
                                     reads=[KB[t], VB1, pt.b], writes=[PB[bk]])
                                first[key] = False
                    jobs.append(dict(S=S, E=E, V=V, post=None, obanks=set(banks.values())))
                jobs[-1]["post"] = lambda: post(banks, sp)

            def run_jobs(jobs):
                n = len(jobs)
                pend = None
                later = []

                def do_post(p, i):
                    c = p()
                    if c is not None:
                        later.append((i + 3, c))
                if n:
                    jobs[0]["S"]()
                for i in range(n):
                    if i + 1 < n:
                        jobs[i + 1]["S"]()
                    jobs[i]["E"]()
                    jobs[i]["V"]()
                    for (due, c) in [x for x in later if x[0] <= i]:
                        c()
                    later[:] = [x for x in later if x[0] > i]
                    if pend is not None:
                        do_post(pend, i)
                        pend = None
                    p = jobs[i]["post"]
                    if p is not None:
                        if i + 1 < n and not (jobs[i]["obanks"] & jobs[i + 1]["obanks"]):
                            pend = p
                        else:
                            do_post(p, i)
                if pend is not None:
                    do_post(pend, n)
                for (due, c) in later:
                    c()

            groups = []
            if need_ctx:
                groups.append(("ctx", list(range(NC))))
            for g in range(NL // 4):
                groups.append(("lat", [NC + 4 * g + j for j in range(4)]))

            gg_loaded = [None]
            def p2_group(gkind, qtiles):
                if gg_loaded[0] != gkind:
                    r = NB if gkind == "ctx" else b
                    load_bc(gg1, modv[l, 2, r:r + 1, :], [modvB[l]])
                    gg_loaded[0] = gkind
                N = 128 * len(qtiles)
                q0 = qtiles[0] * 128
                qs = slice(q0, q0 + N)
                qB = [KB[t] for t in qtiles]
                ktiles = list(range(NC)) if gkind == "ctx" else list(range(NT))
                jobs = []
                if even:
                    def a_head(h):
                        s_, j_ = h % 4, h // 4
                        pr = slice(j_ * 64, (j_ + 1) * 64)
                        bk = next_obank()
                        jobs = []

                        def a_qtile(qi, qt):
                            lt = qt - NC
                            kl = [(t, None) for t in range(NC)]
                            if lt > 0:
                                kl.append((qt - 1, 0))
                            kl.append((qt, None))
                            if lt < NL - 1:
                                kl.append((qt + 1, 1))
                            qcols = slice(qt * 128, (qt + 1) * 128)
                            nk = len(kl)
                            st = {}

                            def S():
                                sp = st["sp"] = next_spair()
                                pt = st["pt"] = ptr.next()
                                st["ptv"] = pt.t[:].rearrange("p a (c n) -> p (a c) n", n=128)
                                st["psv"] = ps[:, sp:sp + 2, :].rearrange("p a (c n) -> p (a c) n", n=128)
                                for ci, (t, mk) in enumerate(kl):
                                    bnk, col = sp + ci // 4, (ci % 4) * 128
                                    P.pe(lambda e, t=t, bnk=bnk, col=col, mk=mk: e.matmul(
                                        ps[:, bnk, col:col + 128], lhsT=kaT[pr, t * 128:(t + 1) * 128], rhs=qaT[pr, s_, qcols], start=True, stop=(mk is None)),
                                        reads=[KB[t], KB[qt]], writes=[PB[bnk]])
                                    if mk is not None:
                                        P.pe(lambda e, bnk=bnk, col=col, mk=mk: e.matmul(ps[:, bnk, col:col + 128], lhsT=identb.t[:], rhs=maskb.t[:, mk, :], start=False, stop=True),
                                             reads=[identb.b, maskb.b], writes=[PB[bnk]])

                            def E():
                                sp, pt, ptv, psv = st["sp"], st["pt"], st["ptv"], st["psv"]
                                P.act(lambda e: e.activation(out=ptv[:, 0:nk, :], in_=psv[:, 0:nk, :], func=AF.Exp, scale=0.125),
                                      reads=[PB[sp], PB[sp + 1]], writes=[pt.b])

                            def V():
                                pt, ptv = st["pt"], st["ptv"]
                                for ci, (t, mk) in enumerate(kl):
                                    P.pe(lambda e, ci=ci, t=t: e.matmul(ps[:, bk, qi * 128:(qi + 1) * 128], lhsT=vop(va, t, j_), rhs=ptv[:, ci, :],
                                                                       start=(ci == 0), stop=(ci == nk - 1)),
                                         reads=[KB[t], VB1, pt.b], writes=[PB[bk]])
                            jobs.append(dict(S=S, E=E, V=V, post=None, obanks={bk}))
                        for qi, qt in enumerate(qtiles):
                            a_qtile(qi, qt)
                        jobs[-1]["post"] = lambda: normalise(bk, N, oTa(h, N), [oT.b], sink_h=h, e_=j_)
                        return jobs

                    def a_ctx_pair(s_):
                        us = []
                        for j_ in range(2):
                            pr = slice(j_ * 64, (j_ + 1) * 64)
                            us.append(dict(kT=lambda t, pr=pr: kaT[pr, t * 128:(t + 1) * 128], qT=qaT[pr, s_, qs], V=[lambda t, j_=j_: vop(va, t, j_)]))

                        def post(banks, sp_):
                            for j_ in range(2):
                                normalise(banks[(id(us[j_]), 0)], N, oTa(s_ + 4 * j_, N), [oT.b], sink_h=s_ + 4 * j_, e_=j_)
                        attend_jobs(jobs, us, ktiles, N, 0.125, post, qB)
                    for s_ in range(4):
                        if gkind == "ctx":
                            a_ctx_pair(s_)
                        else:
                            ja, jb = a_head(s_), a_head(s_ + 4)
                            for x_, y_ in zip(ja, jb):
                                jobs.append(x_)
                                jobs.append(y_)

                    def b_head(h):
                        u = dict(kT=lambda t: kbT[0:96, h, t * 128:(t + 1) * 128], qT=qbT[0:96, h, qs], V=[lambda t: vop(vb, t, h)])
                        attend_jobs(jobs, [u], ktiles, N, 96 ** -0.5, lambda banks, sp_: normalise(banks[(id(u), 0)], N, oTa(8 + h, N), [oT.b], e_=h % 2), qB)
                    for h in range(8):
                        b_head(h)
                else:
                    def c_pair(s_):
                        us = []
                        for j_ in range(2):
                            pr = slice(j_ * 64, (j_ + 1) * 64)
                            us.append(dict(kT=lambda t, pr=pr: kcT[pr, t * 128:(t + 1) * 128], qT=qcT[pr, s_, qs], V=[lambda t, j_=j_: vop(vc, t, j_)]))

                        def post(banks, sp_):
                            for j_ in range(2):
                                normalise(banks[(id(us[j_]), 0)], N, oTa(s_ + 4 * j_, N), [oT.b], e_=j_)
                        attend_jobs(jobs, us, ktiles, N, 0.125, post, qB)
                    for s_ in range(4):
                        c_pair(s_)

                    def d_head(h):
                        us = []
                        for i in range(2):
                            pr = slice(i * 64, (i + 1) * 64)
                            us.append(dict(kT=lambda t, pr=pr: kdT[pr, h, t * 128:(t + 1) * 128], qT=qdT[pr, h, qs],
                                           V=[lambda t, hf=hf: vop(vd, t, 2 * h + hf) for hf in range(2)]))

                        def d_post(banks, sp):
                            ods = []
                            rr = []
                            for i in range(2):
                                r = osbr.next()
                                bk0 = banks[(id(us[i]), 0)]
                                if i == 0:
                                    P.dve(lambda e, r=r, bk0=bk0: e.reciprocal(out=r.t[64:128, 0:N], in_=ps[64:128, bk0, 0:N]), reads=[PB[bk0]], writes=[r.b])
                                else:
                                    P.act(lambda e, r=r, bk0=bk0: e.activation(out=r.t[64:128, 0:N], in_=ps[64:128, bk0, 0:N], func=AF.Ln), reads=[PB[bk0]], writes=[r.b])
                                    P.act(lambda e, r=r: e.activation(out=r.t[64:128, 0:N], in_=r.t[64:128, 0:N], func=AF.Exp, scale=-1.0), reads=[r.b], writes=[r.b])
                                rr.append(r)
                            oo = [[odr.next(), odr.next()] for _ in range(2)]
                            for hf in range(2):
                                vs = slice(hf * 64, hf * 64 + 64)
                                for i in range(2):
                                    bk_ = banks[(id(us[i]), hf)]
                                    P.dve(lambda e, o=oo[hf][i], r=rr[i], bk_=bk_, vs=vs: e.tensor_tensor(out=o.t[:, 0:N], in0=ps[vs, bk_, 0:N], in1=r.t[64:128, 0:N], op=ALU.mult),
                                          reads=[rr[i].b, PB[bk_]], writes=[oo[hf][i].b])
                            for hf in range(2):
                                o1, o2 = oo[hf]
                                P.dve(lambda e, o1=o1, o2=o2: e.scalar_tensor_tensor(out=o1.t[:, 0:N], in0=o2.t[:, 0:N], scalar=lams.t[0:64, 3:4], in1=o1.t[:, 0:N], op0=ALU.mult, op1=ALU.add),
                                      reads=[o1.b, o2.b, lams.b], writes=[o1.b])
                                ods.append(o1)
                            for hf in range(2):
                                sq = sqr.next()
                                P.pool(lambda e, sq=sq, o=ods[hf]: e.tensor_tensor(out=sq.t[:, 0:N], in0=o.t[:, 0:N], in1=o.t[:, 0:N], op=ALU.mult), reads=[ods[hf].b], writes=[sq.b])
                                P.pe(lambda e, sq=sq, hf=hf: e.matmul(ps[0:64, sp, 0:N], lhsT=onesb.t[:], rhs=sq.t[:, 0:N], start=(hf == 0), stop=(hf == 1)),
                                     reads=[onesb.b, sq.b], writes=[PB[sp]])
                            rs = rsd.next()
                            P.dve(lambda e: e.tensor_copy(out=rs.t[:, 0:N], in_=ps[0:64, sp, 0:N]), reads=[PB[sp]], writes=[rs.b])

                            def stage2():
                                P.act(lambda e: e.activation(out=rs.t[:, 0:N], in_=rs.t[:, 0:N], func=AF.Ln, scale=1.0 / 128, bias=epsb.t[0:64, :]), reads=[rs.b, epsb.b], writes=[rs.b])
                                P.act(lambda e: e.activation(out=rs.t[:, 0:N], in_=rs.t[:, 0:N], func=AF.Exp, scale=-0.5), reads=[rs.b], writes=[rs.b])
                                for hf in range(2):
                                    P.dve(lambda e, hf=hf, o=ods[hf]: e.scalar_tensor_tensor(out=oTa(8 + 2 * h + hf, N), in0=o.t[:, 0:N], scalar=subs.t[:, hf:hf + 1], in1=rs.t[:, 0:N],
                                                                                          op0=ALU.mult, op1=ALU.mult),
                                          reads=[ods[hf].b, subs.b, rs.b], writes=[oT.b])
                            return stage2
                        attend_jobs(jobs, us, ktiles, N, 0.125, d_post, qB)
                    for h in range(4):
                        d_head(h)
                run_jobs(jobs)

                def outproj(qi, qt):
                    sp = next_spair()
                    for hfc in range(2):
                        for hc in range(8):
                            P.pe(lambda e, hc=hc, hfc=hfc, sp=sp, qi=qi: e.matmul(ps[:, sp + hfc, :], lhsT=oT.t[:, hc, qi * 128:(qi + 1) * 128], rhs=wout.t[:, hc, hfc * 512:(hfc + 1) * 512],
                                                                            start=(hc == 0), stop=(hc == 7)),
                                 reads=[oT.b, woutB[hc // 2]], writes=[PB[sp + hfc]])
                    jk, ss = jk2.next(), ss3.next()
                    P.act(lambda e, jk=jk, ss=ss, sp=sp: e.activation(out=jk.t[:], in_=ps2(sp), func=AF.Square, accum_out=ss.t[:, 0:1]), reads=[PB[sp], PB[sp + 1]], writes=[jk.b, ss.b])
                    rstd_from_ss(ss, D)
                    xt = xr2.next()
                    src, srcb = x_src(l, b, qt)
                    P.dma(lambda e, xt=xt, src=src: e.dma_start(out=xt.t[:], in_=src), reads=srcb, writes=[xt.b])
                    yt = ytr.next()
                    ggs = gg1
                    P.dve(lambda e, yt=yt, ss=ss, sp=sp, ggs=ggs: e.scalar_tensor_tensor(out=yt.t[:], in0=ps2(sp), scalar=ss.t[:, 0:1], in1=ggs.t[:], op0=ALU.mult, op1=ALU.mult),
                          reads=[PB[sp], PB[sp + 1], ss.b, ggs.b], writes=[yt.b])
                    x1 = x1r.next()
                    P.pool(lambda e, x1=x1, xt=xt, yt=yt: e.tensor_tensor(out=x1.t[:], in0=xt.t[:], in1=yt.t[:], op=ALU.add), reads=[xt.b, yt.b], writes=[x1.b])
                    P.dma(lambda e, x1=x1, qt=qt: e.dma_start(out=x1s[b, qt * 128:(qt + 1) * 128, :], in_=x1.t[:]), reads=[x1.b], writes=[x1B[b][qt]], q="pool")

                for qi, qt in enumerate(qtiles):
                    outproj(qi, qt)
            for gkind, qtiles in groups:
                p2_group(gkind, qtiles)
        for b in range(NB if DEBUG_STOP != "p1a" else 1):
            do_seq(b)
        if DEBUG_STOP in ("p1", "p1a", "p2"):
            return
        P.barrier()
        ar.reset(const_mark)
        fwin = TB(ar.alloc([128, 8, 2 * FH], BF16))
        fwout = TB(ar.alloc([128, 22, D], BF16))
        fwinB = [Buf() for _ in range(11)]
        fwoutB = [Buf() for _ in range(11)]
        for c2 in range(11):
            for gu in range(2):
                col = gu * FH + c2 * 256
                P.dma(lambda e, col=col: e.dma_start(out=fwin.t[:, :, col:col + 256], in_=ffn_w_in[l, :, col:col + 256].rearrange("(k p) n -> p k n", p=128)),
                      writes=[fwinB[c2]], q="pool")
        for c2 in range(11):
            P.dma(lambda e, c2=c2: e.dma_start(out=fwout.t[:, 2 * c2:2 * c2 + 2, :], in_=ffn_w_out[l, c2 * 256:(c2 + 1) * 256, :].rearrange("(c p) n -> p c n", p=128)),
                  writes=[fwoutB[c2]], q="pool")
        mod3 = [TB(ar.alloc([128, D], F32)) for _ in range(3)]
        xr3 = Ring(ar, 2, [128, D], F32)
        rings3 = (Ring(ar, 1, [128, D], BF16), Ring(ar, 2, [128, 4], F32), Ring(ar, 1, [128, D], F32))
        hbr4 = Ring(ar, 4, [128, D], BF16)
        h2T = TB(ar.alloc([128, 8, 512], BF16))
        actT = TB(ar.alloc([128, 22, 512], BF16))
        sgr = Ring(ar, 2, [128, 512], F32)
        ytr3 = rings3[2]
        spair3 = [0]
        opair3 = [4]
        def p3_seq(b):
            groups = []
            if need_ctx:
                groups.append(list(range(NC)))
            for g in range(NL // 4):
                groups.append([NC + 4 * g + j for j in range(4)])
            m3_loaded = [None]
            pre_done = [None]

            def preA(tiles):
                hbs = []
                for tt in tiles:
                    xt = xr3.next()
                    P.dma(lambda e, xt=xt, tt=tt: e.dma_start(out=xt.t[:], in_=x1s[b, tt * 128:(tt + 1) * 128, :]), reads=[x1B[b][tt]], writes=[xt.b])
                    jk, ss, tmp, hb = rings3[0].next(), rings3[1].next(), rings3[2].next(), hbr4.next()
                    P.act(lambda e, jk=jk, ss=ss, xt=xt: e.activation(out=jk.t[:], in_=xt.t[:], func=AF.Square, accum_out=ss.t[:, 0:1]), reads=[xt.b], writes=[jk.b, ss.b])
                    rstd_from_ss(ss, D)
                    P.pool(lambda e, tmp=tmp, xt=xt: e.tensor_tensor(out=tmp.t[:], in0=xt.t[:], in1=mod3[0].t[:], op=ALU.mult), reads=[xt.b, mod3[0].b], writes=[tmp.b])
                    P.dve(lambda e, hb=hb, tmp=tmp, ss=ss: e.scalar_tensor_tensor(out=hb.t[:], in0=tmp.t[:], scalar=ss.t[:, 0:1], in1=mod3[1].t[:], op0=ALU.mult, op1=ALU.add),
                          reads=[tmp.b, ss.b, mod3[1].b], writes=[hb.b])
                    hbs.append(hb)
                return hbs

            def preB(hbs):
                for ti, hb in enumerate(hbs):
                    transposes([(hb.t[:, k * 128:(k + 1) * 128], 128) for k in range(8)], 7, [hb.b])
                    P.act(lambda e, ti=ti: e.activation(out=h2T.t[:, :, ti * 128:(ti + 1) * 128], in_=psbf(7).rearrange("p (k t) -> p k t", k=8), func=AF.Copy), reads=[PB[7]], writes=[h2T.b])

            def kind_of(tiles):
                return "ctx" if tiles[0] < NC else "lat"

            def p3_group(gi):
                tiles = groups[gi]
                N = 128 * len(tiles)
                kind = kind_of(tiles)
                if m3_loaded[0] != kind:
                    r = NB if kind == "ctx" else b
                    for j in range(3):
                        load_bc(mod3[j], modv[l, 3 + j, r:r + 1, :], [modvB[l]])
                    m3_loaded[0] = kind
                if pre_done[0] != gi:
                    preB(preA(tiles))
                nxt = groups[gi + 1] if gi + 1 < len(groups) and kind_of(groups[gi + 1]) == kind else None
                def p3_chunk(c):
                    sp = spair3[0]
                    spair3[0] = 2 - sp
                    for gu in range(2):
                        col = gu * FH + c * 128
                        for k in range(8):
                            P.pe(lambda e, gu=gu, col=col, k=k, sp=sp: e.matmul(ps[:, sp + gu, 0:N], lhsT=fwin.t[:, k, col:col + 128], rhs=h2T.t[:, k, 0:N], start=(k == 0), stop=(k == 7)),
                                 reads=[fwinB[c // 2], h2T.b], writes=[PB[sp + gu]])
                    sg = sgr.next()
                    P.act(lambda e, sg=sg, sp=sp: e.activation(out=sg.t[:, 0:N], in_=ps[:, sp, 0:N], func=AF.Silu), reads=[PB[sp]], writes=[sg.b])
                    P.dve(lambda e, sg=sg, sp=sp, c=c: e.tensor_tensor(out=actT.t[:, c, 0:N], in0=sg.t[:, 0:N], in1=ps[:, sp + 1, 0:N], op=ALU.mult), reads=[sg.b, PB[sp + 1]], writes=[actT.b])
                for c in range(22):
                    p3_chunk(c)
                def p3_epi(ti, tt):
                    op_ = opair3[0]
                    opair3[0] = 10 - op_
                    for hfc in range(2):
                        for c in range(22):
                            P.pe(lambda e, c=c, hfc=hfc, op_=op_, ti=ti: e.matmul(ps[:, op_ + hfc, :], lhsT=actT.t[:, c, ti * 128:(ti + 1) * 128], rhs=fwout.t[:, c, hfc * 512:(hfc + 1) * 512],
                                                                            start=(c == 0), stop=(c == 21)),
                                 reads=[actT.b, fwoutB[c // 2]], writes=[PB[op_ + hfc]])
                    jk, ss = rings3[0].next(), rings3[1].next()
                    P.act(lambda e, jk=jk, ss=ss, op_=op_: e.activation(out=jk.t[:], in_=ps2(op_), func=AF.Square, accum_out=ss.t[:, 0:1]), reads=[PB[op_], PB[op_ + 1]], writes=[jk.b, ss.b])
                    rstd_from_ss(ss, D)
                    xt = xr3.next()
                    P.dma(lambda e, xt=xt, tt=tt: e.dma_start(out=xt.t[:], in_=x1s[b, tt * 128:(tt + 1) * 128, :]), reads=[x1B[b][tt]], writes=[xt.b])
                    yt = ytr3.next()
                    ggs = mod3[2]
                    P.dve(lambda e, yt=yt, ss=ss, op_=op_, ggs=ggs: e.scalar_tensor_tensor(out=yt.t[:], in0=ps2(op_), scalar=ss.t[:, 0:1], in1=ggs.t[:], op0=ALU.mult, op1=ALU.mult),
                          reads=[PB[op_], PB[op_ + 1], ss.b, ggs.b], writes=[yt.b])
                    x2 = xt
                    P.pool(lambda e, xt=xt, yt=yt: e.tensor_tensor(out=xt.t[:], in0=xt.t[:], in1=yt.t[:], op=ALU.add), reads=[xt.b, yt.b], writes=[xt.b])
                    if l == L - 1:
                        P.dma(lambda e, x2=x2, tt=tt: e.dma_start(out=out[b, (tt - NC) * 128:(tt - NC + 1) * 128, :], in_=x2.t[:]), reads=[x2.b], q="pool")
                    else:
                        P.dma(lambda e, x2=x2, tt=tt: e.dma_start(out=xres[b, tt * 128:(tt + 1) * 128, :], in_=x2.t[:]), reads=[x2.b], writes=[xresB[b][tt]], q="pool")
                nhb = preA(nxt) if nxt is not None else None
                for ti, tt in enumerate(tiles):
                    p3_epi(ti, tt)
                if nxt is not None:
                    preB(nhb)
                    pre_done[0] = gi + 1
            for gi in range(len(groups)):
                p3_group(gi)
        for b in range(NB):
            p3_seq(b)
        P.barrier()
        ar.reset(layer_mark)

    for l in range(L):
        do_layer(l)
    P.emit()
    return nc


def rope_tables(S, grid_w=64, theta=10000.0):
    t = np.arange(S)
    rows, cols = (t // grid_w).astype(np.float32), (t % grid_w).astype(np.float32)

    def tab(dh):
        half = dh // 4
        freqs = (theta ** (-np.arange(half, dtype=np.float32) / half)).astype(np.float32)
        ar_, ac_ = rows[:, None] * freqs[None, :], cols[:, None] * freqs[None, :]
        cr, sr, cc, sc = np.cos(ar_), np.sin(ar_), np.cos(ac_), np.sin(ac_)
        cos = np.concatenate([cr, cr, cc, cc], 1)
        sin = np.concatenate([-sr, sr, -sc, sc], 1)
        return np.concatenate([cos, sin], 1).astype(np.float32)
    return tab(64), tab(32)


def const_inputs(S):
    r64, r32 = rope_tables(S)
    k = np.arange(128)[:, None]
    q = np.arange(128)[None, :]
    neg = np.float32(-30000.0)
    mprev = np.where(k >= q, np.float32(0), neg)
    mnext = np.where(k <= q, np.float32(0), neg)
    return {"rope64": r64, "rope32": r32, "ident": np.eye(128, dtype=np.float32),
            "masks": np.stack([mprev, mnext]).astype(np.float32)}


_W_KEYS = ("ada_w", "ada_b", "norm_g", "ffn_w_in", "ffn_w_out", "ev_w_in", "ev_sink", "ev_q_norm", "ev_w_uq",
           "ev_kv_norm", "ev_w_ukv", "ev_w_out", "od_w_in", "od_qk_norm", "od_lambda", "od_subln", "od_w_out")


def run(inputs, n_cores, nb):
    x = np.asarray(inputs["x"], np.float32)
    B, S, _ = x.shape
    ctx = np.asarray(inputs["ctx"], np.float32)
    C = ctx.shape[1]
    L = inputs["ada_w"].shape[0]
    c = np.asarray(inputs["c"], np.float32)
    c_ctx = np.asarray(inputs["c_ctx"], np.float32)
    nc = build_program(S, C, L, nb)
    consts = const_inputs(S)
    shared = {k: np.ascontiguousarray(np.asarray(inputs[k], np.float32)) for k in _W_KEYS}
    in_maps = []
    for i in range(n_cores):
        sl = slice(i * nb, (i + 1) * nb)
        m = {"x": np.ascontiguousarray(x[sl]), "ctx": np.ascontiguousarray(ctx[sl]),
             "cvec": np.ascontiguousarray(np.concatenate([c[sl], c_ctx[None, :]], 0))}
        m.update(shared)
        m.update(consts)
        in_maps.append(m)
    res = run_bass_kernel_spmd(nc, in_maps, core_ids=list(range(n_cores)))
    return np.concatenate([r["out"] for r in res.results], axis=0)


def kernel(**inputs):
    return run(inputs, 8, 2)
```
